# Optimizing a Trainium2 kernel written in Bass

```python
import math
import jax, jax.numpy as jnp
from jax import lax
import numpy as np

D_MODEL = 1024
BATCH = 16
SEQ = 256
DEPTH = 4
DEC_BATCH = 8
DEC_SEQ = 4096
PAST_LEN = 256

GRID_W = 64
HEAD_DIM = 64
N_EVEN = (DEPTH + 1) // 2
N_ODD = DEPTH // 2
H_A = D_MODEL // (2 * HEAD_DIM)
WIN_R = 8
WIN_C = 16
H_B = D_MODEL // (2 * HEAD_DIM)
DK_B = HEAD_DIM
DV_B = HEAD_DIM
GLA_RANK = 16
GLA_TAU = 16.0
H_C = D_MODEL // (2 * HEAD_DIM)
KV_C = H_C // 4
WIN_1D = 128
ROPE_BASE = 10000.0
H_D = D_MODEL // (2 * HEAD_DIM)
DK_D = HEAD_DIM
DV_D = HEAD_DIM
SHORT_CONV = 3
CHUNK = 64
Q_BLOCK = 128
D_FF = ((8 * D_MODEL // 3 + 127) // 128) * 128
FFN_CONV = 3
EPS = 1e-6
EV_SIZES = (H_A * HEAD_DIM, H_A * HEAD_DIM, H_A * HEAD_DIM, H_B * DK_B, H_B * DK_B, H_B * DV_B, 2 * GLA_RANK, H_B * DV_B)
OD_SIZES = (H_C * HEAD_DIM, KV_C * HEAD_DIM, KV_C * HEAD_DIM, H_D * DK_D, H_D * DK_D, H_D * DV_D, 2 * H_D, 2 * H_D, H_D * DV_D)
EV_IN = sum(EV_SIZES)
OD_IN = sum(OD_SIZES)
EV_OUT = H_A * HEAD_DIM + H_B * DV_B
OD_OUT = H_C * HEAD_DIM + H_D * DV_D
F32 = jnp.float32

kernel_name = 'hybrid_flow_trunk_step'


def rmsnorm(x, g):
    xf = x.astype(F32)
    y = xf * lax.rsqrt(jnp.mean(xf * xf, axis=-1, keepdims=True) + EPS)
    return (y * g.astype(F32)).astype(x.dtype)


def l2norm(x):
    return x * lax.rsqrt(jnp.sum(x * x, axis=-1, keepdims=True) + EPS)


def split_cols(p, sizes):
    cuts = [int(s) for s in np.cumsum(sizes)[:-1]]
    return jnp.split(p, cuts, axis=-1)


def adaln(cond, w, b):
    return jnp.split(jax.nn.silu(cond) @ w + b, 6, axis=-1)


def modulate(h, shift, scale):
    return h * (1.0 + scale) + shift


def dwconv(x, w):
    K = w.shape[0]
    T = x.shape[1]
    pad = K // 2
    xp = jnp.pad(x, ((0, 0), (pad, pad), (0, 0)))
    out = xp[:, 0:T] * w[0]
    for i in range(1, K):
        out = out + xp[:, i:i + T] * w[i]
    return out


def softmax_sink(s, sink):
    if sink is None:
        return jax.nn.softmax(s, axis=-1)
    m = jnp.maximum(jnp.max(s, axis=-1, keepdims=True), sink)
    e = jnp.exp(s - m)
    return e / (jnp.sum(e, axis=-1, keepdims=True) + jnp.exp(sink - m))


def axial_rope(x):
    T = x.shape[1]
    t = jnp.arange(T)
    half = HEAD_DIM // 2
    quarter = half // 2
    inv = 1.0 / (ROPE_BASE ** (jnp.arange(quarter, dtype=F32) / quarter))

    def rot(xa, pos):
        ang = pos.astype(F32)[:, None] * inv[None, :]
        cos = jnp.cos(ang)[None, :, None, :]
        sin = jnp.sin(ang)[None, :, None, :]
        x1, x2 = xa[..., :quarter], xa[..., quarter:]
        return jnp.concatenate([x1 * cos - x2 * sin, x1 * sin + x2 * cos], axis=-1)

    xf = x.astype(F32)
    out = jnp.concatenate([rot(xf[..., :half], t // GRID_W), rot(xf[..., half:], t % GRID_W)], axis=-1)
    return out.astype(x.dtype)


def dense_attn(q, k, v, sink):
    B, Tq, Hq, Dh = q.shape
    Hk = k.shape[1]
    G = Hq // Hk
    nb = Tq // Q_BLOCK
    qb = q.reshape(B, nb, Q_BLOCK, Hk, G, Dh).swapaxes(0, 1)
    sk = None if sink is None else sink.astype(F32).reshape(Hk, G, 1, 1)
    scale = Dh ** -0.5

    def one(qblk):
        s = jnp.einsum('bqkgd,bksd->bkgqs', qblk, k).astype(F32) * scale
        p = softmax_sink(s, sk).astype(v.dtype)
        return jnp.einsum('bkgqs,bksd->bqkgd', p, v)

    o = lax.map(one, qb)
    return o.swapaxes(0, 1).reshape(B, Tq, Hq, Dh)


def neighborhood_attn(q, k, v, rpb, ck, cv):
    B, T, H, Dh = q.shape
    R = T // GRID_W
    kr = min(WIN_R, R)
    kc = WIN_C
    qg = q.reshape(B, R, GRID_W, H, Dh)
    kg = k.reshape(B, R, GRID_W, H, Dh)
    vg = v.reshape(B, R, GRID_W, H, Dh)
    col = jnp.arange(GRID_W)
    col_idx = jnp.clip(col - kc // 2, 0, GRID_W - kc)[:, None] + jnp.arange(kc)[None, :]
    dc = col_idx - col[:, None] + (WIN_C - 1)
    scale = Dh ** -0.5
    rpb32 = rpb.astype(F32)

    def one(i):
        r0 = jnp.clip(i - kr // 2, 0, R - kr)
        kb = lax.dynamic_slice_in_dim(kg, r0, kr, axis=1)[:, :, col_idx]
        vb = lax.dynamic_slice_in_dim(vg, r0, kr, axis=1)[:, :, col_idx]
        qi = lax.dynamic_index_in_dim(qg, i, axis=1, keepdims=False)
        dr = r0 + jnp.arange(kr) - i + (WIN_R - 1)
        bias = rpb32[:, dr[None, :, None], dc[:, None, :]]
        s_loc = jnp.einsum('bwhd,brwchd->bhwrc', qi, kb).astype(F32) * scale + bias
        s_loc = s_loc.reshape(B, H, GRID_W, kr * kc)
        s_ctx = jnp.einsum('bwhd,bhld->bhwl', qi, ck).astype(F32) * scale
        p = jax.nn.softmax(jnp.concatenate([s_loc, s_ctx], axis=-1), axis=-1).astype(v.dtype)
        p_loc = p[..., :kr * kc].reshape(B, H, GRID_W, kr, kc)
        return (jnp.einsum('bhwrc,brwchd->bwhd', p_loc, vb)
                + jnp.einsum('bhwl,bhld->bwhd', p[..., kr * kc:], cv))

    o = lax.map(one, jnp.arange(R))
    return o.swapaxes(0, 1).reshape(B, T, H, Dh)


def window_attn(q, k, v, sink, ck, cv):
    B, T, Hq, Dh = q.shape
    Hk = k.shape[2]
    G = Hq // Hk
    blk = Q_BLOCK
    nb = T // blk
    pad = ((0, 0), (blk, blk), (0, 0), (0, 0))
    kp = jnp.pad(k, pad)
    vp = jnp.pad(v, pad)
    sk = sink.astype(F32).reshape(1, Hk, G, 1, 1)
    scale = Dh ** -0.5

    def one(n):
        q0 = n * blk
        qn = lax.dynamic_slice_in_dim(q, q0, blk, axis=1).reshape(B, blk, Hk, G, Dh)
        kn = lax.dynamic_slice_in_dim(kp, q0, 3 * blk, axis=1)
        vn = lax.dynamic_slice_in_dim(vp, q0, 3 * blk, axis=1)
        qpos = q0 + jnp.arange(blk)
        kpos = q0 - blk + jnp.arange(3 * blk)
        valid = ((jnp.abs(kpos[None, :] - qpos[:, None]) <= WIN_1D)
                 & (kpos >= 0)[None, :] & (kpos < T)[None, :])
        s_loc = jnp.einsum('bqkgd,bskd->bkgqs', qn, kn).astype(F32) * scale
        s_loc = jnp.where(valid, s_loc, -jnp.inf)
        s_ctx = jnp.einsum('bqkgd,bksd->bkgqs', qn, ck).astype(F32) * scale
        p = softmax_sink(jnp.concatenate([s_loc, s_ctx], axis=-1), sk).astype(v.dtype)
        o = (jnp.einsum('bkgqs,bskd->bqkgd', p[..., :3 * blk], vn)
             + jnp.einsum('bkgqs,bksd->bqkgd', p[..., 3 * blk:], cv))
        return o.reshape(B, blk, Hq, Dh)

    o = lax.map(one, jnp.arange(nb))
    return o.swapaxes(0, 1).reshape(B, T, Hq, Dh)


def gla_scan(q, k, v, g, s0):
    B, T, H, dk = q.shape
    dv = v.shape[-1]
    n = T // CHUNK
    q, k, v, g = [a.reshape(B, n, CHUNK, H, a.shape[-1]) for a in (q, k, v, g)]
    G = jnp.cumsum(g, axis=2)
    G_last = G[:, :, -1:]
    q_t = q * jnp.exp(G)
    k_t = k * jnp.exp(-G)
    k_end = k * jnp.exp(G_last - G)
    causal = jnp.tril(jnp.ones((CHUNK, CHUNK), bool))
    att = jnp.where(causal, jnp.einsum('bnchd,bnshd->bnhcs', q_t, k_t), 0.0)
    o_intra = jnp.einsum('bnhcs,bnshv->bnchv', att, v)
    u = jnp.einsum('bnshd,bnshv->bnhdv', k_end, v)
    decay = jnp.exp(G_last[:, :, 0])

    def step(S, inp):
        d, du = inp
        return d[..., None] * S + du, S

    S_fin, S_start = lax.scan(step, s0, (decay.swapaxes(0, 1), u.swapaxes(0, 1)))
    o_inter = jnp.einsum('bnchd,bnhdv->bnchv', q_t, S_start.swapaxes(0, 1))
    return (o_intra + o_inter).reshape(B, T, H, dv), S_fin


def delta_scan(q, k, v, beta, g, s0):
    B, T, H, dk = q.shape
    dv = v.shape[-1]
    n = T // CHUNK
    blk = lambda a: a.reshape((B, n, CHUNK) + a.shape[2:]).swapaxes(2, 3)
    q, k, v, beta, g = [blk(a) for a in (q, k, v, beta, g)]
    G = jnp.cumsum(g, axis=-1)
    lower = jnp.tril(jnp.ones((CHUNK, CHUNK), bool))
    strict = jnp.tril(jnp.ones((CHUNK, CHUNK), bool), -1)
    diff = G[..., :, None] - G[..., None, :]
    gam = jnp.where(lower, jnp.exp(jnp.where(lower, diff, 0.0)), 0.0)
    k_beta = k * beta[..., None]
    a_mat = jnp.where(strict, jnp.einsum('bnhcd,bnhsd->bnhcs', k_beta, k) * gam, 0.0)
    m_mat = a_mat + jnp.eye(CHUNK, dtype=F32)
    rhs = jnp.concatenate([v * beta[..., None], k_beta * jnp.exp(G)[..., None]], axis=-1)
    sol = lax.linalg.triangular_solve(m_mat, rhs, left_side=True, lower=True, unit_diagonal=True)
    w_val, k_cum = sol[..., :dv], sol[..., dv:]
    a_qk = jnp.einsum('bnhcd,bnhsd->bnhcs', q, k) * gam
    q_g = q * jnp.exp(G)[..., None]
    k_end = k * jnp.exp(G[..., -1:] - G)[..., None]
    d_last = jnp.exp(G[..., -1])

    def step(S, inp):
        aqk, wv, kc, qg, ke, d = inp
        v_new = wv - jnp.einsum('bhcd,bhdv->bhcv', kc, S)
        o = jnp.einsum('bhcd,bhdv->bhcv', qg, S) + jnp.einsum('bhcs,bhsv->bhcv', aqk, v_new)
        S = S * d[..., None, None] + jnp.einsum('bhcd,bhcv->bhdv', ke, v_new)
        return S, o

    xs = tuple(a.swapaxes(0, 1) for a in (a_qk, w_val, k_cum, q_g, k_end, d_last))
    S_fin, o = lax.scan(step, s0, xs)
    return o.transpose(1, 0, 3, 2, 4).reshape(B, T, H, dv), S_fin


def gla_mixer(bq, bk, bv, glr, br, w_g2, b_g, norm_g, s0):
    B, T, _ = bq.shape
    q = bq.reshape(B, T, H_B, DK_B).astype(F32) * DK_B ** -0.5
    k = bk.reshape(B, T, H_B, DK_B).astype(F32)
    v = bv.reshape(B, T, H_B, DV_B).astype(F32)
    z = jnp.einsum('btzr,zrc->btzc', glr.reshape(B, T, 2, GLA_RANK).astype(F32), w_g2.astype(F32)) + b_g.astype(F32)
    g = (jax.nn.log_sigmoid(z) / GLA_TAU).reshape(B, T, 2, H_B, DK_B)
    s0 = s0.astype(F32)
    fl = lambda a: jnp.flip(a, 1)
    o_f, s_f = gla_scan(q, k, v, g[:, :, 0], s0[:, 0])
    o_b, s_b = gla_scan(fl(q), fl(k), fl(v), fl(g[:, :, 1]), s0[:, 1])
    o = o_f + fl(o_b)
    o = rmsnorm(o, norm_g.reshape(H_B, DV_B)) * jax.nn.silu(br.reshape(B, T, H_B, DV_B).astype(F32))
    return o.reshape(B, T, H_B * DV_B).astype(bq.dtype), jnp.stack([s_f, s_b], axis=1).astype(bq.dtype)


def delta_mixer(dq, dk, dv, da, db, dz, w_conv, a_log, dt_bias, norm_g, s0):
    B, T, _ = dq.shape
    qkv = jax.nn.silu(dwconv(jnp.concatenate([dq, dk, dv], axis=-1), w_conv).astype(F32))
    q, k, v = split_cols(qkv, (H_D * DK_D, H_D * DK_D, H_D * DV_D))
    q = l2norm(q.reshape(B, T, H_D, DK_D)) * DK_D ** -0.5
    k = l2norm(k.reshape(B, T, H_D, DK_D))
    v = v.reshape(B, T, H_D, DV_D)
    beta = jax.nn.sigmoid(db.reshape(B, T, 2, H_D).astype(F32))
    g = -jnp.exp(a_log.astype(F32)) * jax.nn.softplus(da.reshape(B, T, 2, H_D).astype(F32) + dt_bias.astype(F32))
    s0 = s0.astype(F32)
    fl = lambda a: jnp.flip(a, 1)
    o_f, s_f = delta_scan(q, k, v, beta[:, :, 0], g[:, :, 0], s0[:, 0])
    o_b, s_b = delta_scan(fl(q), fl(k), fl(v), fl(beta[:, :, 1]), fl(g[:, :, 1]), s0[:, 1])
    o = o_f + fl(o_b)
    o = rmsnorm(o, norm_g) * jax.nn.silu(dz.reshape(B, T, H_D, DV_D).astype(F32))
    return o.reshape(B, T, H_D * DV_D).astype(dq.dtype), jnp.stack([s_f, s_b], axis=1).astype(dq.dtype)


def even_context(h, w_in, w_out, w_g2, b_g, norm_g):
    B, T, _ = h.shape
    aq, ak, av, bq, bk, bv, glr, br = split_cols(h @ w_in, EV_SIZES)
    k_h = ak.reshape(B, T, H_A, HEAD_DIM).transpose(0, 2, 1, 3)
    v_h = av.reshape(B, T, H_A, HEAD_DIM).transpose(0, 2, 1, 3)
    o_a = dense_attn(aq.reshape(B, T, H_A, HEAD_DIM), k_h, v_h, None)
    o_b, s_b = gla_mixer(bq, bk, bv, glr, br, w_g2, b_g, norm_g, jnp.zeros((B, 2, H_B, DK_B, DV_B), F32))
    out = jnp.concatenate([o_a.reshape(B, T, H_A * HEAD_DIM), o_b], axis=-1) @ w_out
    return out, k_h, v_h, s_b


def even_latent(h, w_in, w_out, rpb, w_g2, b_g, norm_g, ak_cache, av_cache, sb_cache):
    B, T, _ = h.shape
    aq, ak, av, bq, bk, bv, glr, br = split_cols(h @ w_in, EV_SIZES)
    o_a = neighborhood_attn(aq.reshape(B, T, H_A, HEAD_DIM), ak.reshape(B, T, H_A, HEAD_DIM),
                            av.reshape(B, T, H_A, HEAD_DIM), rpb, ak_cache, av_cache)
    o_b, _ = gla_mixer(bq, bk, bv, glr, br, w_g2, b_g, norm_g, sb_cache)
    return jnp.concatenate([o_a.reshape(B, T, H_A * HEAD_DIM), o_b], axis=-1) @ w_out


def odd_context(h, w_in, w_out, sink, w_conv, a_log, dt_bias, norm_g):
    B, T, _ = h.shape
    cq, ck, cv, dq, dk, dv, da, db, dz = split_cols(h @ w_in, OD_SIZES)
    k_h = ck.reshape(B, T, KV_C, HEAD_DIM).transpose(0, 2, 1, 3)
    v_h = cv.reshape(B, T, KV_C, HEAD_DIM).transpose(0, 2, 1, 3)
    o_c = dense_attn(cq.reshape(B, T, H_C, HEAD_DIM), k_h, v_h, sink)
    o_d, s_d = delta_mixer(dq, dk, dv, da, db, dz, w_conv, a_log, dt_bias, norm_g,
                           jnp.zeros((B, 2, H_D, DK_D, DV_D), F32))
    out = jnp.concatenate([o_c.reshape(B, T, H_C * HEAD_DIM), o_d], axis=-1) @ w_out
    return out, k_h, v_h, s_d


def odd_latent(h, w_in, w_out, sink, w_conv, a_log, dt_bias, norm_g, ck_cache, cv_cache, sd_cache):
    B, T, _ = h.shape
    cq, ck, cv, dq, dk, dv, da, db, dz = split_cols(h @ w_in, OD_SIZES)
    q = axial_rope(cq.reshape(B, T, H_C, HEAD_DIM))
    k = axial_rope(ck.reshape(B, T, KV_C, HEAD_DIM))
    o_c = window_attn(q, k, cv.reshape(B, T, KV_C, HEAD_DIM), sink, ck_cache, cv_cache)
    o_d, _ = delta_mixer(dq, dk, dv, da, db, dz, w_conv, a_log, dt_bias, norm_g, sd_cache)
    return jnp.concatenate([o_c.reshape(B, T, H_C * HEAD_DIM), o_d], axis=-1) @ w_out


def conv_ffn(h, w_up, w_conv, w_down):
    u = dwconv(h @ w_up, w_conv)
    a, gt = jnp.split(u, 2, axis=-1)
    return (a * jax.nn.silu(gt)) @ w_down


def setup_inputs(seed: int = 0) -> dict:
    key = jax.random.key(seed)
    ks = iter(jax.random.split(key, 40))
    nrm = lambda shape, s: jax.random.normal(next(ks), shape, F32) * s
    gain = lambda shape: 1.0 + nrm(shape, 0.02)
    x_prompt = nrm((BATCH, SEQ, D_MODEL), 1.0)
    x_sample = nrm((DEC_BATCH, DEC_SEQ, D_MODEL), 1.0)
    cache_a_k = nrm((DEC_BATCH, N_EVEN, H_A, PAST_LEN, HEAD_DIM), 1.0)
    cache_a_v = nrm((DEC_BATCH, N_EVEN, H_A, PAST_LEN, HEAD_DIM), 1.0)
    state_b = nrm((DEC_BATCH, N_EVEN, 2, H_B, DK_B, DV_B), 0.5)
    cache_c_k = nrm((DEC_BATCH, N_ODD, KV_C, PAST_LEN, HEAD_DIM), 1.0)
    cache_c_v = nrm((DEC_BATCH, N_ODD, KV_C, PAST_LEN, HEAD_DIM), 1.0)
    state_d = nrm((DEC_BATCH, N_ODD, 2, H_D, DK_D, DV_D), 0.5)
    c = nrm((DEC_BATCH, D_MODEL), 1.0)
    c_ctx = nrm((D_MODEL,), 1.0)
    ada_w = nrm((DEPTH, D_MODEL, 6 * D_MODEL), 0.5 * D_MODEL ** -0.5)
    ada_b = nrm((DEPTH, 6 * D_MODEL), 0.02)
    norm1_g = gain((DEPTH, D_MODEL))
    norm2_g = gain((DEPTH, D_MODEL))
    ffn_up = nrm((DEPTH, D_MODEL, 2 * D_FF), D_MODEL ** -0.5)
    ffn_conv = nrm((DEPTH, FFN_CONV, 2 * D_FF), FFN_CONV ** -0.5)
    ffn_down = nrm((DEPTH, D_FF, D_MODEL), D_FF ** -0.5)
    ev_w_in = nrm((N_EVEN, D_MODEL, EV_IN), D_MODEL ** -0.5)
    ev_w_out = nrm((N_EVEN, EV_OUT, D_MODEL), EV_OUT ** -0.5)
    a_rpb = nrm((N_EVEN, H_A, 2 * WIN_R - 1, 2 * WIN_C - 1), 0.1)
    b_w_g2 = nrm((N_EVEN, 2, GLA_RANK, H_B * DK_B), GLA_RANK ** -0.5)
    b_b_g = nrm((N_EVEN, 2, H_B * DK_B), 0.1)
    b_norm_g = gain((N_EVEN, H_B * DV_B))
    od_w_in = nrm((N_ODD, D_MODEL, OD_IN), D_MODEL ** -0.5)
    od_w_out = nrm((N_ODD, OD_OUT, D_MODEL), OD_OUT ** -0.5)
    c_sink = nrm((N_ODD, H_C), 0.5)
    d_conv = nrm((N_ODD, SHORT_CONV, H_D * (2 * DK_D + DV_D)), SHORT_CONV ** -0.5)
    d_a_log = jnp.log(jax.random.uniform(next(ks), (N_ODD, 2, H_D), F32, 1.0, 16.0))
    u = jax.random.uniform(next(ks), (N_ODD, 2, H_D), F32)
    dt = jnp.exp(u * (math.log(0.1) - math.log(0.001)) + math.log(0.001))
    d_dt_bias = dt + jnp.log(-jnp.expm1(-dt))
    d_norm_g = gain((N_ODD, DV_D))
    final_g = gain((D_MODEL,))
    return {'x_prompt': x_prompt, 'x_sample': x_sample,
            'cache_a_k': cache_a_k, 'cache_a_v': cache_a_v, 'state_b': state_b,
            'cache_c_k': cache_c_k, 'cache_c_v': cache_c_v, 'state_d': state_d,
            'c': c, 'c_ctx': c_ctx, 'ada_w': ada_w, 'ada_b': ada_b,
            'norm1_g': norm1_g, 'norm2_g': norm2_g,
            'ffn_up': ffn_up, 'ffn_conv': ffn_conv, 'ffn_down': ffn_down,
            'ev_w_in': ev_w_in, 'ev_w_out': ev_w_out, 'a_rpb': a_rpb,
            'b_w_g2': b_w_g2, 'b_b_g': b_b_g, 'b_norm_g': b_norm_g,
            'od_w_in': od_w_in, 'od_w_out': od_w_out, 'c_sink': c_sink,
            'd_conv': d_conv, 'd_a_log': d_a_log, 'd_dt_bias': d_dt_bias, 'd_norm_g': d_norm_g,
            'final_g': final_g}


def reference(x_prompt, x_sample, cache_a_k, cache_a_v, state_b, cache_c_k, cache_c_v, state_d, c,
              c_ctx, ada_w, ada_b, norm1_g, norm2_g, ffn_up, ffn_conv, ffn_down,
              ev_w_in, ev_w_out, a_rpb, b_w_g2, b_b_g, b_norm_g,
              od_w_in, od_w_out, c_sink, d_conv, d_a_log, d_dt_bias, d_norm_g, final_g):
    xp = x_prompt
    xs = x_sample
    ak_l, av_l, sb_l, ck_l, cv_l, sd_l = [], [], [], [], [], []
    for l in range(DEPTH):
        j = l // 2
        sh1p, sc1p, g1p, sh2p, sc2p, g2p = adaln(c_ctx, ada_w[l], ada_b[l])
        sh1s, sc1s, g1s, sh2s, sc2s, g2s = [m[:, None, :] for m in adaln(c, ada_w[l], ada_b[l])]
        hp = modulate(rmsnorm(xp, norm1_g[l]), sh1p, sc1p)
        hs = modulate(rmsnorm(xs, norm1_g[l]), sh1s, sc1s)
        if l % 2 == 0:
            mp, ak, av, sb = even_context(hp, ev_w_in[j], ev_w_out[j], b_w_g2[j], b_b_g[j], b_norm_g[j])
            ms = even_latent(hs, ev_w_in[j], ev_w_out[j], a_rpb[j], b_w_g2[j], b_b_g[j], b_norm_g[j],
                             cache_a_k[:, j], cache_a_v[:, j], state_b[:, j])
            ak_l.append(ak)
            av_l.append(av)
            sb_l.append(sb)
        else:
            mp, ck, cv, sd = odd_context(hp, od_w_in[j], od_w_out[j], c_sink[j], d_conv[j],
                                         d_a_log[j], d_dt_bias[j], d_norm_g[j])
            ms = odd_latent(hs, od_w_in[j], od_w_out[j], c_sink[j], d_conv[j], d_a_log[j], d_dt_bias[j],
                            d_norm_g[j], cache_c_k[:, j], cache_c_v[:, j], state_d[:, j])
            ck_l.append(ck)
            cv_l.append(cv)
            sd_l.append(sd)
        xp = xp + g1p * mp
        xs = xs + g1s * ms
        hp = modulate(rmsnorm(xp, norm2_g[l]), sh2p, sc2p)
        hs = modulate(rmsnorm(xs, norm2_g[l]), sh2s, sc2s)
        xp = xp + g2p * conv_ffn(hp, ffn_up[l], ffn_conv[l], ffn_down[l])
        xs = xs + g2s * conv_ffn(hs, ffn_up[l], ffn_conv[l], ffn_down[l])
    y_prompt = rmsnorm(xp, final_g)
    y_sample = rmsnorm(xs, final_g)
    new_a_k = jnp.stack(ak_l, axis=1)
    new_a_v = jnp.stack(av_l, axis=1)
    new_state_b = jnp.stack(sb_l, axis=1)
    new_c_k = jnp.stack(ck_l, axis=1)
    new_c_v = jnp.stack(cv_l, axis=1)
    new_state_d = jnp.stack(sd_l, axis=1)
    return (y_prompt, y_sample, new_a_k, new_a_v, new_state_b, new_c_k, new_c_v, new_state_d)
```

```python
import contextlib
import os
import numpy as np
import ml_dtypes
import concourse.bass as bass
import concourse.mybir as mybir
from concourse.bass_utils import run_bass_kernel_spmd

F32 = mybir.dt.float32
BF16 = mybir.dt.bfloat16
AF = mybir.ActivationFunctionType
ALU = mybir.AluOpType
AX = mybir.AxisListType

NCORES = 8
D = 1024
TS = 4096
TP = 256
T = TS + 2 * TP
NT = T // 128
DEPTH = 4
DFF = 2816
EV_IN = 3616
OD_IN = 2848
EPS = 1e-6
SEQS = [(0, TS), (TS, TP), (TS + TP, TP)]

ENGS = ("pe", "act", "dve", "pool", "sp")
NDMA = 24


class Op:
    __slots__ = ("eng", "fn", "r", "w", "dma", "id", "waits", "sig", "dslot", "dval", "prev_dma")


class Prog:
    def __init__(self, nc):
        self.nc = nc
        self.es = contextlib.ExitStack()
        self.ops = []
        self.nid = 0
        self.engobj = {"pe": nc.tensor, "act": nc.scalar, "dve": nc.vector, "pool": nc.gpsimd, "sp": nc.sync}
        self.sem = {e: self.es.enter_context(nc.semaphore("sem_" + e)) for e in ENGS}
        self.dsem = {e: [self.es.enter_context(nc.semaphore(f"dsem_{e}_{i}")) for i in range(NDMA)]
                     for e in ("sp", "act", "pool")}
        self.sigcnt = {e: 0 for e in ENGS}
        self.dcnt = {e: 0 for e in ("sp", "act", "pool")}
        self.dhist = {e: [] for e in ("sp", "act", "pool")}
        self.last_w = {}
        self.readers = {}
        self.seen = {e: {} for e in ENGS}
        self.out_dmas = []
        self.phase_allocs = None
        self.psum_keys = set(["ps_t", "ps_fm", "ps_row", "mh_ps0", "mh_ps1", "lfm_pst", "l2psT0", "l2psT1", "ae_psm", "ao_psm",
                              "at_pso0", "at_pso1", "at_pso2", "at_pso3", "gs_psatt0", "gs_psatt1", "gs_pso0", "gs_pso1", "gs_pss0", "gs_pss1"])

    def begin(self, name=None):
        import inspect
        self.phase_name = name or inspect.stack()[1].function
        self.phase_allocs = contextlib.ExitStack()

    def uname(self, name):
        self.uid = getattr(self, "uid", 0) + 1
        return f"{name}_u{self.uid}"

    def sb(self, name, shape, dt):
        return self.phase_allocs.enter_context(self.nc.sbuf_tensor(self.uname(name), list(shape), dt))

    def ps(self, name, shape, dt=F32):
        return self.phase_allocs.enter_context(self.nc.psum_tensor(self.uname(name), list(shape), dt))

    def sub_begin(self):
        self._outer = self.phase_allocs
        self.phase_allocs = contextlib.ExitStack()

    def sub_end(self):
        self.flush(fence=True)
        self.phase_allocs.close()
        self.phase_allocs = self._outer

    def end(self):
        self.flush(fence=True)
        self.phase_allocs.close()
        self.phase_allocs = None

    def op(self, eng, method, r=(), w=(), dma=False, isout=False, **kw):
        o = Op()
        o.eng = eng; o.fn = (method, kw); o.r = tuple(r); o.w = tuple(w); o.dma = dma
        o.id = self.nid; self.nid += 1
        o.waits = None; o.sig = 0; o.dslot = None; o.dval = 0; o.prev_dma = None
        self.ops.append(o)
        if isout:
            self.out_dmas.append(o)
        return o

    def dma(self, out, in_, r=(), w=(), q="sp", isout=False, **kw):
        return self._dma(q, out, in_, r, w, isout, kw)

    def _dma(self, q, out, in_, r, w, isout, kw):
        o = Op()
        kk = dict(kw); kk["out"] = out; kk["in_"] = in_
        o.eng = q; o.fn = ("dma_start", kk); o.r = tuple(r); o.w = tuple(w); o.dma = True
        o.id = self.nid; self.nid += 1
        o.waits = None; o.sig = 0; o.dslot = None; o.dval = 0; o.prev_dma = None
        self.ops.append(o)
        if isout:
            self.out_dmas.append(o)
        return o

    def flush(self, fence=False, final=False):
        ops = self.ops
        self.ops = []
        for o in ops:
            deps = {}
            def add(p):
                if p is None or p is o:
                    return
                deps[p.id] = p
            for k in o.r:
                add(self.last_w.get(k))
                if k in self.psum_keys:
                    rd = self.readers.get(k)
                    if rd:
                        for kk, p in rd.items():
                            if kk != "dma" and kk != o.eng:
                                add(p)
            for k in o.w:
                add(self.last_w.get(k))
                rd = self.readers.get(k)
                if rd:
                    for kk, p in rd.items():
                        if kk == "dma":
                            for pp in p:
                                add(pp)
                        else:
                            add(p)
            if o.dma:
                q = o.eng
                n = self.dcnt[q]
                self.dcnt[q] += 1
                o.dslot = n % NDMA
                o.dval = 16 * (n // NDMA + 1)
                if n >= NDMA:
                    add(self.dhist[q][n - NDMA])
                self.dhist[q].append(o)
            o.waits = list(deps.values())
            for p in o.waits:
                if not p.dma:
                    if not (p.eng == "pe" and o.eng == "pe" and not o.dma):
                        p.sig = -1
            for k in o.w:
                self.last_w[k] = o
                self.readers[k] = {}
            for k in o.r:
                rd = self.readers.setdefault(k, {})
                if o.dma:
                    rd.setdefault("dma", []).append(o)
                else:
                    rd[o.eng] = o
        lastop = {}
        for o in ops:
            lastop[o.eng] = o
        if fence or final:
            for e, o in lastop.items():
                if not o.dma:
                    o.sig = -1
        for o in ops:
            if o.dma:
                continue
            if o.sig == -1:
                self.sigcnt[o.eng] += 1
                o.sig = self.sigcnt[o.eng]
        per_eng = {e: [] for e in ENGS}
        for o in ops:
            per_eng[o.eng].append(o)
        nc = self.nc
        fence_dmas = [(q, i) for q in self.dcnt for i in range(min(NDMA, self.dcnt[q]))]

        def emit_engine(ename):
            def body(eng):
                seen = self.seen[ename]
                for o in per_eng[ename]:
                    need = {}
                    for p in o.waits:
                        if p.dma:
                            key = ("d", p.eng, p.dslot)
                            if seen.get(key, 0) < p.dval:
                                seen[key] = p.dval
                                eng.wait_ge(self.dsem[p.eng][p.dslot], p.dval)
                        else:
                            if p.eng == "pe" and ename == "pe" and not o.dma:
                                continue
                            if p.sig > need.get(p.eng, 0):
                                need[p.eng] = p.sig
                    for pe_, v in need.items():
                        if seen.get(pe_, 0) < v:
                            seen[pe_] = v
                            eng.wait_ge(self.sem[pe_], v)
                    ins = getattr(eng, o.fn[0])(**o.fn[1])
                    if o.dma:
                        ins.then_inc(self.dsem[o.eng][o.dslot], 16)
                    elif o.sig > 0:
                        ins.then_inc(self.sem[o.eng], 1)
                if fence or final:
                    for e2 in ENGS:
                        v = self.sigcnt[e2]
                        if v > 0 and seen.get(e2, 0) < v:
                            seen[e2] = v
                            eng.wait_ge(self.sem[e2], v)
                    for q in self.dcnt:
                        n = self.dcnt[q]
                        for i in range(min(NDMA, n)):
                            last = 16 * ((n - 1 - i) // NDMA + 1)
                            key = ("d", q, i)
                            if seen.get(key, 0) < last:
                                seen[key] = last
                                eng.wait_ge(self.dsem[q][i], last)
            return body

        self.scope_i = getattr(self, "scope_i", 0) + 1
        with nc.named_scope(f"{getattr(self, 'phase_name', 'x')}_{self.scope_i}"), nc.Block() as block:
            block.tensor(emit_engine("pe"))
            block.scalar(emit_engine("act"))
            block.vector(emit_engine("dve"))
            block.gpsimd(emit_engine("pool"))
            block.sync(emit_engine("sp"))
        if fence or final:
            self.last_w = {}
            self.readers = {}


def _bc(ap, shape):
    return ap.broadcast_to(list(shape))


class Ctx:
    pass


def build_program(stage=99, dbg=None):
    nc = bass.Bass("TRN2", target_bir_lowering=False)
    P = Prog(nc)
    C = Ctx()
    C.nc = nc; C.P = P; C.stage = stage

    def din(name, shape, dt=F32):
        return nc.dram_tensor(name, list(shape), dt, kind="ExternalInput").ap()

    def dout(name, shape, dt=F32):
        return nc.dram_tensor(name, list(shape), dt, kind="ExternalOutput").ap()

    def dscr(name, shape, dt=F32):
        return nc.dram_tensor(name, list(shape), dt).ap()

    I = {}
    I["x"] = din("x", [T, D])
    I["cond"] = din("cond", [3, D])
    I["cak"] = din("cak", [2, 8, 256, 64]); I["cav"] = din("cav", [2, 8, 256, 64])
    I["sb"] = din("sb", [2, 2, 8, 64, 64])
    I["cck"] = din("cck", [2, 2, 256, 64]); I["ccv"] = din("ccv", [2, 2, 256, 64])
    I["sd"] = din("sd", [2, 2, 8, 64, 64])
    I["ada_w"] = din("ada_w", [DEPTH, D, 6 * D]); I["ada_b"] = din("ada_b", [DEPTH, 6 * D])
    I["norm1_g"] = din("norm1_g", [DEPTH, D]); I["norm2_g"] = din("norm2_g", [DEPTH, D])
    I["ffn_up"] = din("ffn_up", [DEPTH, D, 2 * DFF]); I["ffn_conv"] = din("ffn_conv", [DEPTH, 3, 2 * DFF])
    I["ffn_down"] = din("ffn_down", [DEPTH, DFF, D])
    I["ev_w_in"] = din("ev_w_in", [2, D, EV_IN]); I["ev_w_out"] = din("ev_w_out", [2, D, D])
    I["a_rpb"] = din("a_rpb", [2, 8, 15, 31]); I["b_w_g2"] = din("b_w_g2", [2, 2, 16, 512])
    I["b_b_g"] = din("b_b_g", [2, 2, 512]); I["b_norm_g"] = din("b_norm_g", [2, 512])
    I["od_w_in"] = din("od_w_in", [2, D, OD_IN]); I["od_w_out"] = din("od_w_out", [2, D, D])
    I["c_sink"] = din("c_sink", [2, 8]); I["d_conv"] = din("d_conv", [2, 3, 1536])
    I["d_a_log"] = din("d_a_log", [2, 2, 8]); I["d_dt_bias"] = din("d_dt_bias", [2, 2, 8])
    I["d_norm_g"] = din("d_norm_g", [2, 64]); I["final_g"] = din("final_g", [D])
    I["c_ident"] = din("c_ident", [128, 128])
    I["c_maskf"] = din("c_maskf", [128, 512])
    I["c_eoh"] = din("c_eoh", [32, 64, 64])
    I["c_tri"] = din("c_tri", [2, 128, 128])
    I["c_perm"] = din("c_perm", [128, 128])
    I["c_bd"] = din("c_bd", [128, 128])
    I["c_cos"] = din("c_cos", [128, TS])
    I["c_sin"] = din("c_sin", [128, TS])
    I["c_dmask"] = din("c_dmask", [4, 64, 64])
    C.I = I
    O = {}
    O["y"] = dout("y", [T, D])
    O["nak"] = dout("nak", [2, 2, 8, 256, 64]); O["nav"] = dout("nav", [2, 2, 8, 256, 64])
    O["nsb"] = dout("nsb", [2, 2, 2, 8, 64, 64])
    O["nck"] = dout("nck", [2, 2, 2, 256, 64]); O["ncv"] = dout("ncv", [2, 2, 2, 256, 64])
    O["nsd"] = dout("nsd", [2, 2, 2, 8, 64, 64])
    C.O = O
    C.taps = {}
    if dbg:
        for nm, shp, dt in dbg:
            O[nm] = dout(nm, shp, dt)
            C.taps[nm] = O[nm]

    def tap(name, src, keys, dst_view=None):
        if name in C.taps:
            dst = C.taps[name] if dst_view is None else dst_view(C.taps[name])
            P.dma(dst, src, r=keys, w=["o_" + name], isout=True)
    C.tap = tap

    def dump(dst, src, key):
        names = " ".join("abcdef"[:len(src.shape)])
        fs = src.rearrange(f"{names} -> ({names})")
        fd = dst.rearrange(f"{names} -> ({names})")
        n = fs.shape[0]
        per = 128 * 8192
        o = 0
        while o < n:
            m = min(per, n - o)
            assert m % 128 == 0
            P.dma(fd[o:o + m].rearrange("(p f) -> p f", p=128), fs[o:o + m].rearrange("(p f) -> p f", p=128), r=[key], w=["o_dbg_" + key], isout=True)
            o += m
    C.dump = dump
    S = {}
    S["gates"] = dscr("s_gates", [DEPTH, 3, 2, D])
    S["xres"] = dscr("s_xres", [T, D])
    S["qkA"] = dscr("s_qkA", [1024, T], BF16)
    S["vA"] = dscr("s_vA", [T, 8 * 65], BF16)
    S["vB"] = dscr("s_vB", [T, 512], BF16)
    S["gate"] = dscr("s_gate", [T, 512])
    S["qtB"] = dscr("s_qtB", [2, 512, T], BF16)
    S["ktB"] = dscr("s_ktB", [2, 512, T], BF16)
    S["kendB"] = dscr("s_kendB", [2, T, 512], BF16)
    S["decB"] = dscr("s_decB", [2, 512, 72])
    S["oT"] = dscr("s_oT", [1024, T], BF16)
    S["ofb"] = dscr("s_ofb", [2, T, 512])
    S["actT"] = dscr("s_actT", [DFF, T], BF16)
    S["qkC"] = dscr("s_qkC", [640, T], BF16)
    S["vC"] = dscr("s_vC", [T, 2 * 65], BF16)
    S["qnT"] = dscr("s_qnT", [512, T], BF16)
    S["knT"] = dscr("s_knT", [512, T], BF16)
    S["kn_tok"] = dscr("s_kn_tok", [T, 512])
    S["v_tok"] = dscr("s_v_tok", [T, 512])
    S["Gexp"] = dscr("s_Gexp", [2, 8, 8, T])
    S["Hexp"] = dscr("s_Hexp", [2, 8, 8, T])
    S["dsc"] = dscr("s_dsc", [2, T, 40])
    S["dng"] = dscr("s_dng", [2, 512])
    C.S = S

    def gsb(name, shape, dt):
        return P.es.enter_context(nc.sbuf_tensor(name, list(shape), dt))
    C.ident = gsb("ident", [128, 128], F32)
    C.identb = gsb("identb", [128, 128], BF16)
    C.modfm = gsb("modfm", [128, DEPTH, 3, 4, 8], F32)
    C.gsb = gsb

    def scoped(name, shape, dt):
        st = contextlib.ExitStack()
        t = st.enter_context(nc.sbuf_tensor(P.uname(name), list(shape), dt))
        return t, st
    C.scoped = scoped
    C.epsb = gsb("epsb", [128, 4], F32)

    phase0(C)
    nlayers = int(os.environ.get("MK_LAYERS", DEPTH))
    xsrc, xkey = I["x"], "x_in"
    for l in range(nlayers):
        j = l // 2
        C.hT, hst = scoped("hT", [128, 8, T], BF16)
        make_hT(C, l, 0, xsrc, xkey)
        if l % 2 == 0:
            l2_even(C, j)
        else:
            l2_odd(C, j)
        hst.close()
        if l % 2 == 0:
            C.Mp, mst = scoped("Mp", [128, 8, 19, 64], F32)
            build_mp(C, j)
            attn_even(C, j)
            mst.close()
            gla_scan(C, j)
            scan_finalize(C, I["b_norm_g"][j:j + 1, :], "gla")
            wout = I["ev_w_out"][j]
        else:
            attn_odd(C, j)
            delta_scan(C, j)
            scan_finalize(C, S["dng"][j:j + 1, :], "dl")
            wout = I["od_w_out"][j]
        if dbg and stage == 10 + l:
            P.begin()
            C.dump(C.taps["dbg_oT"], S["oT"], "oT")
            P.end()
        proj_residual(C, l, 0, wout, 8, S["oT"], "oT", xsrc, xkey)
        xsrc, xkey = S["xres"], "xres_w"
        if dbg and stage == 20 + l:
            P.begin()
            C.dump(C.taps["dbg_x"], S["xres"], "xres_w")
            P.end()
        stop = os.environ.get("MK_STOP", "")
        if stop == "l4":
            break
        C.hT, hst = scoped("hT", [128, 8, T], BF16)
        make_hT(C, l, 1, xsrc, xkey)
        if stop != "hT2":
            ffn_up(C, l)
        hst.close()
        if stop in ("hT2", "ffn_up"):
            break
        proj_residual(C, l, 1, I["ffn_down"][l], 22, S["actT"], "actT", xsrc, xkey)
        if dbg and stage == 30 + l:
            P.begin()
            C.dump(C.taps["dbg_x"], S["xres"], "xres_w")
            P.end()
    if not os.environ.get("MK_STOP"):
        final_norm(C)
    P.flush(final=True)
    P.es.close()
    return nc


def phase0(C):
    P, nc, I, S = C.P, C.nc, C.I, C.S
    P.begin()
    rows = P.sb("p0_rows", [64, 128], F32)
    rows2 = P.sb("p0_rows2", [24, 128], F32)
    vecfm = P.sb("p0_vecfm", [128, 64], F32)
    scT = P.sb("p0_scT", [128, 3, 8], F32)
    wts = [P.sb(f"p0_wt{i}", [128, 8, 1024], F32) for i in range(2)]
    brow = P.sb("p0_brow", [3, 1024], F32)
    grow = P.sb("p0_grow", [3, 1024], F32)
    tmp = P.sb("p0_tmp", [128, 8], F32)
    ps_t = P.ps("p0_pst", [128, 128], F32)
    ps_fm = P.ps("p0_psfm", [128, 4, 8, 3], F32)
    ps_row = P.ps("p0_psrow", [3, 1024], F32)

    P.dma(C.ident[:, :], I["c_ident"][:, :], w=["ident"])
    P.op("dve", "memset", w=["epsb"], ap=C.epsb[:, 0:1], constant=EPS)
    P.op("dve", "memset", w=["epsb"], ap=C.epsb[:, 1:2], constant=1.0)
    P.op("dve", "memset", w=["epsb"], ap=C.epsb[:, 2:4], constant=0.0)
    P.op("dve", "tensor_copy", r=["ident"], w=["identb"], out=C.identb[:, :], in_=C.ident[:, :])
    P.dma(rows2[:, :], I["cond"].rearrange("s (k p) -> (s k) p", p=128), w=["rows2"])
    P.op("pe", "transpose", r=["rows2", "ident"], w=["ps_t"], out=ps_t[:, 0:24], in_=rows2[:, :], identity=C.ident[0:24, 0:24])
    P.op("act", "activation", r=["ps_t"], w=["scT"], out=scT[:, :, :].rearrange("p s k -> p (s k)"), in_=ps_t[:, 0:24], func=AF.Silu)
    for j_ in range(2):
        for h_ in range(8):
            P.dma(S["dng"][j_:j_ + 1, h_ * 64:(h_ + 1) * 64], I["d_norm_g"][j_:j_ + 1, :], w=["dng"])
    wcnt = 0
    for l in range(DEPTH):
        P.dma(rows[0:48, :], I["ada_b"][l].rearrange("(r p) -> r p", p=128), w=["rows"])
        P.dma(rows[48:56, :], I["norm1_g"][l].rearrange("(r p) -> r p", p=128), w=["rows"])
        P.dma(rows[56:64, :], I["norm2_g"][l].rearrange("(r p) -> r p", p=128), w=["rows"])
        P.op("pe", "transpose", r=["rows", "ident"], w=["ps_t"], out=ps_t[:, 0:64], in_=rows[:, :], identity=C.ident[0:64, 0:64])
        P.op("dve", "tensor_copy", r=["ps_t"], w=["vecfm"], out=vecfm[:, :], in_=ps_t[:, 0:64])
        for j in range(6):
            wt = wts[wcnt % 2]; wkey = f"p0wt{wcnt % 2}"; wcnt += 1
            for k in range(8):
                P.dma(wt[:, k, :], I["ada_w"][l, k * 128:(k + 1) * 128, j * 1024:(j + 1) * 1024], w=[wkey + f"_{k}"])
            if j in (0, 1, 3, 4):
                jq = (0, 1, None, 2, 3)[j]
                for c in range(8):
                    for k in range(8):
                        P.op("pe", "matmul", r=[wkey + f"_{k}", "scT"], w=["ps_fm"],
                             out=ps_fm[:, jq, c, :], lhsT=wt[:, k, c * 128:(c + 1) * 128], rhs=scT[:, :, k],
                             start=(k == 0), stop=(k == 7))
            else:
                jg = 0 if j == 2 else 1
                P.dma(brow[:, :], I["ada_b"][l:l + 1, j * 1024:(j + 1) * 1024].partition_broadcast(3), w=["brow"])
                for half in range(2):
                    for k in range(8):
                        P.op("pe", "matmul", r=[wkey + f"_{k}", "scT"], w=["ps_row"],
                             out=ps_row[:, half * 512:(half + 1) * 512], lhsT=scT[:, :, k], rhs=wt[:, k, half * 512:(half + 1) * 512],
                             start=(k == 0), stop=(k == 7))
                P.op("dve", "tensor_tensor", r=["ps_row", "brow"], w=["grow"], out=grow[:, :], in0=ps_row[:, :], in1=brow[:, :], op=ALU.add)
                P.dma(S["gates"][l, :, jg, :], grow[:, :], r=["grow"], w=[f"gates{l}"])
        for s in range(3):
            for half in range(2):
                q_sh, q_sc = (0, 1) if half == 0 else (2, 3)
                j_sh, j_sc = (0, 1) if half == 0 else (3, 4)
                gcol = 48 if half == 0 else 56
                P.op("dve", "tensor_tensor", r=["ps_fm", "vecfm"], w=["modfm"],
                     out=C.modfm[:, l, s, 2 * half + 1, :], in0=ps_fm[:, q_sh, :, s], in1=vecfm[:, j_sh * 8:(j_sh + 1) * 8], op=ALU.add)
                P.op("dve", "scalar_tensor_tensor", r=["ps_fm", "vecfm"], w=["p0tmp"],
                     out=tmp[:, :], in0=ps_fm[:, q_sc, :, s], scalar=1.0, in1=vecfm[:, j_sc * 8:(j_sc + 1) * 8], op0=ALU.add, op1=ALU.add)
                P.op("dve", "tensor_tensor", r=["p0tmp", "vecfm"], w=["modfm"],
                     out=C.modfm[:, l, s, 2 * half, :], in0=tmp[:, :], in1=vecfm[:, gcol:gcol + 8], op=ALU.mult)
    P.end()


def make_hT(C, l, which, xsrc, xkey="xres"):
    P, nc = C.P, C.nc
    P.begin()
    NB = 3
    xts = [P.sb(f"mh_x{i}", [128, D], F32) for i in range(NB)]
    junk = P.sb("mh_junk", [128, D], BF16)
    xns = [P.sb(f"mh_xn{i}", [128, D], BF16) for i in range(NB)]
    sss = [P.sb(f"mh_ss{i}", [128, 4], F32) for i in range(NB)]
    pss = [[P.ps(f"mh_ps{i}_{e}", [128, 4, 128], BF16) for e in range(2)] for i in range(2)]
    P.psum_keys.update([f"mh_ps{i}_{e}" for i in range(2) for e in range(2)])
    for i in range(NT):
        b = i % NB
        seq = 0 if i < 32 else (1 if i < 34 else 2)
        xt, xn, ss = xts[b], xns[b], sss[b]
        ps = pss[i % 2]; pk = [f"mh_ps{i % 2}_0", f"mh_ps{i % 2}_1"]
        P.dma(xt[:, :], xsrc[i * 128:(i + 1) * 128, :], r=[xkey], w=[f"mh_x{b}"])
        P.op("act", "activation", r=[f"mh_x{b}"], w=["mh_junk", f"mh_ss{b}"],
             out=junk[:, :], in_=xt[:, :], func=AF.Square, accum_out=ss[:, 0:1])
        P.op("act", "activation", r=[f"mh_ss{b}"], w=[f"mh_ss{b}"], out=ss[:, 1:2], in_=ss[:, 0:1], func=AF.Ln, scale=1.0 / D, bias=C.epsb[:, 0:1])
        P.op("act", "activation", r=[f"mh_ss{b}"], w=[f"mh_ss{b}"], out=ss[:, 2:3], in_=ss[:, 1:2], func=AF.Exp, scale=-0.5)
        P.op("dve", "tensor_scalar", r=[f"mh_x{b}", f"mh_ss{b}"], w=[f"mh_xn{b}"],
             out=xn[:, :], in0=xt[:, :], scalar1=ss[:, 2:3], scalar2=None, op0=ALU.mult)
        for k in range(8):
            P.op("pe", "transpose", r=[f"mh_xn{b}", "identb"], w=[pk[k % 2]], out=ps[k % 2][:, k // 2, :], in_=xn[:, k * 128:(k + 1) * 128], identity=C.identb[:, :])
        for k in range(8):
            A = C.modfm[:, l, seq, 2 * which, k:k + 1]
            B = C.modfm[:, l, seq, 2 * which + 1, k:k + 1]
            dst = C.hT[:, k, i * 128:(i + 1) * 128]
            if k % 2 == 0:
                P.op("dve", "tensor_scalar", r=[pk[0], "modfm"], w=[f"hT{i}_d"], out=dst, in0=ps[0][:, k // 2, :], scalar1=A, scalar2=B, op0=ALU.mult, op1=ALU.add)
            else:
                P.op("act", "activation", r=[pk[1], "modfm"], w=[f"hT{i}_a"], out=dst, in_=ps[1][:, k // 2, :], func=AF.Identity, scale=A, bias=B)
    P.end()


def host_consts():
    c = {}
    c["c_ident"] = np.eye(128, dtype=np.float32)
    mf = np.ones((128, 512), np.float32); mf[:, 0::64] = 0.0
    c["c_maskf"] = mf
    eoh = np.zeros((32, 64, 64), np.float32)
    for w in range(64):
        cs = min(max(w - 8, 0), 48)
        for cc in range(64):
            if cs <= cc < cs + 16:
                eoh[cc - w + 15, w, cc] = 1.0
            else:
                eoh[31, w, cc] = -30000.0
    c["c_eoh"] = eoh
    perm = np.zeros((128, 128), np.float32)
    for m_ in range(128):
        src = m_ + 16 if (m_ % 32) < 16 else m_ - 16
        perm[src, m_] = 1.0
    c["c_perm"] = perm
    bd = np.zeros((128, 128), np.float32); bd[:64, :64] = 1.0; bd[64:, 64:] = 1.0
    c["c_bd"] = bd
    tpos = np.arange(TS)
    inv = (1.0 / (10000.0 ** (np.arange(16, dtype=np.float32) / 16.0))).astype(np.float32)
    cos_t = np.zeros((128, TS), np.float32); sin_t = np.zeros((128, TS), np.float32)
    for p_ in range(128):
        i_ = p_ % 64
        pos = (tpos // 64) if i_ < 32 else (tpos % 64)
        ang = pos.astype(np.float32) * inv[i_ % 16]
        cos_t[p_] = np.cos(ang)
        sin_t[p_] = np.sin(ang) * (-1.0 if (i_ % 32) < 16 else 1.0)
    c["c_cos"] = cos_t; c["c_sin"] = sin_t
    i64 = np.arange(64)
    NEG = -30000.0
    dm = np.stack([np.where(i64[:, None] <= i64[None, :], 0.0, NEG), np.where(i64[:, None] < i64[None, :], 0.0, NEG),
                   np.where(i64[:, None] >= i64[None, :], 0.0, NEG), np.where(i64[:, None] > i64[None, :], 0.0, NEG)]).astype(np.float32)
    c["c_dmask"] = dm
    ii = np.arange(128)
    c["c_tri"] = np.stack([(ii[:, None] <= ii[None, :]), (ii[:, None] >= ii[None, :])]).astype(np.float32)
    return c


def make_in_maps(inp, cores):
    g = lambda k: np.asarray(inp[k])
    consts = host_consts()
    maps = []
    for c in cores:
        m = {}
        m["x"] = np.ascontiguousarray(np.concatenate([g("x_sample")[c], g("x_prompt")[2 * c], g("x_prompt")[2 * c + 1]], axis=0))
        m["cond"] = np.ascontiguousarray(np.stack([g("c")[c], g("c_ctx"), g("c_ctx")], axis=0))
        m["cak"] = np.ascontiguousarray(g("cache_a_k")[c]); m["cav"] = np.ascontiguousarray(g("cache_a_v")[c])
        m["sb"] = np.ascontiguousarray(g("state_b")[c])
        m["cck"] = np.ascontiguousarray(g("cache_c_k")[c]); m["ccv"] = np.ascontiguousarray(g("cache_c_v")[c])
        m["sd"] = np.ascontiguousarray(g("state_d")[c])
        for k in ("ada_w", "ada_b", "norm1_g", "norm2_g", "ffn_up", "ffn_conv", "ffn_down", "ev_w_in", "ev_w_out",
                  "a_rpb", "b_w_g2", "b_b_g", "b_norm_g", "od_w_in", "od_w_out", "c_sink", "d_conv", "d_a_log",
                  "d_dt_bias", "d_norm_g", "final_g"):
            m[k] = g(k)
        m.update(consts)
        maps.append(m)
    return maps


_NC_CACHE = {}


def kernel(**inputs):
    if "nc" not in _NC_CACHE:
        _NC_CACHE["nc"] = build_program()
    nc = _NC_CACHE["nc"]
    maps = make_in_maps(inputs, list(range(NCORES)))
    res = run_bass_kernel_spmd(nc, maps, core_ids=list(range(NCORES)))
    R = res.results
    y_sample = np.stack([R[c]["y"][:TS] for c in range(NCORES)], 0)
    y_prompt = np.concatenate([R[c]["y"][TS:].reshape(2, TP, D) for c in range(NCORES)], 0)
    cat = lambda k: np.concatenate([R[c][k] for c in range(NCORES)], 0)
    return (y_prompt, y_sample, cat("nak"), cat("nav"), cat("nsb"), cat("nck"), cat("ncv"), cat("nsd"))


def load_w_bf16(C, dst, src, ncols, key):
    P = C.P
    for k in range(src.shape[0] // 128):
        c0 = 0
        while c0 < ncols:
            n = min(2048, ncols - c0)
            P.dma(dst[:, k, c0:c0 + n], src[k * 128:(k + 1) * 128, c0:c0 + n], w=[f"{key}_{k}"], q="pool")
            c0 += n


def load_fm(C, dst, src_rows, n, ps_t, rows_tile, key):
    P = C.P
    P.dma(rows_tile[0:n, :], src_rows, w=["lfm_rows"])
    P.op("pe", "transpose", r=["lfm_rows", "ident"], w=["lfm_pst"], out=ps_t[:, 0:n], in_=rows_tile[0:n, :], identity=C.ident[0:n, 0:n])
    P.op("dve", "tensor_copy", r=["lfm_pst"], w=[key], out=dst, in_=ps_t[:, 0:n])


class Rot:
    def __init__(self, P, name, n, shape, dt, psum=False):
        self.tiles = [(P.ps if psum else P.sb)(f"{name}{i}", shape, dt) for i in range(n)]
        self.keys = [f"{name}{i}" for i in range(n)]
        if psum:
            P.psum_keys.update(self.keys)
        self.i = -1

    def next(self):
        self.i = (self.i + 1) % len(self.tiles)
        return self.tiles[self.i], self.keys[self.i]


def l2_even(C, j):
    P, nc, I, S, O = C.P, C.nc, C.I, C.S, C.O
    P.begin()
    W = P.sb("l2_w", [128, 8, EV_IN], BF16)
    load_w_bf16(C, W, I["ev_w_in"][j], EV_IN, "l2w")
    wkeys = [f"l2w_{k}" for k in range(8)]
    rows = P.sb("l2_rows", [16, 128], F32)
    ps_t = P.ps("l2_pst", [128, 128], F32)
    negb = P.sb("l2_negb", [128, 8], F32)
    load_fm(C, negb[:, :], I["b_b_g"][j].rearrange("d (m p) -> (d m) p", p=128), 8, ps_t, rows, "l2negb")
    P.op("dve", "tensor_scalar", r=["l2negb"], w=["l2negb"], out=negb[:, :], in0=negb[:, :], scalar1=-1.0, scalar2=None, op0=ALU.mult)
    wg2 = P.sb("l2_wg2", [16, 2, 512], F32)
    P.dma(wg2[:, :, :], I["b_w_g2"][j].rearrange("d r c -> r d c"), w=["l2wg2"])
    maskf = P.sb("l2_maskf", [128, 512], F32)
    P.dma(maskf[:, :], I["c_maskf"][:, :], w=["maskf"])

    psq = Rot(P, "l2_psq", 1, [128, 512], F32, psum=True)
    psk = Rot(P, "l2_psk", 1, [128, 512], F32, psum=True)
    psz = Rot(P, "l2_psz", 2, [128, 512], F32, psum=True)
    psg = Rot(P, "l2_psg", 2, [128, 512], F32, psum=True)
    psT = P.ps("l2_psT", [128, 2, 4, 128], BF16)
    tA = Rot(P, "l2_tA", 2, [128, 512], F32)
    tB = Rot(P, "l2_tB", 2, [128, 512], F32)
    tC = Rot(P, "l2_tC", 2, [128, 512], F32)
    tD = Rot(P, "l2_tD", 2, [128, 512], F32)
    tE = Rot(P, "l2_tE", 2, [128, 512], F32)
    sbf = Rot(P, "l2_sbf", 4, [128, 512], BF16)
    sf32 = Rot(P, "l2_sf", 3, [128, 512], F32)
    ketok = Rot(P, "l2_ketok", 2, [128, 4, 128], BF16)
    vst = Rot(P, "l2_vst", 2, [128, 8, 65], BF16)
    for t_, k_ in zip(vst.tiles, vst.keys):
        P.op("pool", "memset", w=[k_], ap=t_[:, :, 64:65], constant=1.0)
    dect = Rot(P, "l2_dec", 2, [128, 8], F32)
    glr_sb = [Rot(P, f"l2_glr{d}_", 2, [16, 512], F32) for d in range(2)]
    ev = [0]

    def evac_engine():
        ev[0] += 1
        return "act" if ev[0] % 2 else "dve"

    def evac(dst, dkey, src, skey):
        e = evac_engine()
        if e == "act":
            P.op("act", "activation", r=[skey], w=[dkey], out=dst, in_=src, func=AF.Identity)
        else:
            P.op("dve", "tensor_copy", r=[skey], w=[dkey], out=dst, in_=src)

    def proj_fm(ps, pkey, c0, M, tt):
        for k in range(8):
            P.op("pe", "matmul", r=[wkeys[k]] + [f"hT{4 * tt + u}" for u in range(4)], w=[pkey],
                 out=ps[0:M, :], lhsT=W[:, k, c0:c0 + M], rhs=C.hT[:, k, tt * 512:(tt + 1) * 512], start=(k == 0), stop=(k == 7))

    def proj_tm(ps, pkey, c0, N, i):
        for k in range(8):
            P.op("pe", "matmul", r=[wkeys[k], f"hT{i}"], w=[pkey],
                 out=ps[:, 0:N], lhsT=C.hT[:, k, i * 128:(i + 1) * 128], rhs=W[:, k, c0:c0 + N], start=(k == 0), stop=(k == 7))

    parts = os.environ.get("L2P", "qk,tm,gla").split(",")
    for tt in range(int(os.environ.get("L2TT", T // 512))):
        tok = slice(tt * 512, (tt + 1) * 512)
        for m in range(8 if "qk" in parts else 0):
            ps, pk = psg.next()
            proj_fm(ps, pk, m * 128, 128, tt)
            st, sk = sbf.next()
            evac(st[:, :], sk, ps[:, :], pk)
            P.dma(S["qkA"][m * 128:(m + 1) * 128, tok], st[:, :], r=[sk], w=["qkA"])
        for u in range(4 if "tm" in parts else 0):
            i = 4 * tt + u
            rowsl = slice(i * 128, (i + 1) * 128)
            ps, pk = psg.next()
            proj_tm(ps, pk, 1024, 512, i)
            st, sk = vst.next()
            if i < 32:
                evac(st[:, :, 0:64], sk, ps[:, :].rearrange("p (h d) -> p h d", d=64), pk)
            else:
                seq = (i - 32) // 2; t0 = ((i - 32) % 2) * 128
                sf, sfk = sf32.next()
                evac(sf[:, :], sfk, ps[:, :], pk)
                P.op("pool", "tensor_copy", r=[sfk], w=[sk], out=st[:, :, 0:64], in_=sf[:, :].rearrange("p (h d) -> p h d", d=64))
                for h in range(8):
                    P.dma(O["nav"][seq, j, h, t0:t0 + 128, :], sf[:, h * 64:(h + 1) * 64], r=[sfk], w=["o_nav"], isout=True)
                ps2, pk2 = psg.next()
                proj_tm(ps2, pk2, 512, 512, i)
                sf, sfk = sf32.next()
                evac(sf[:, :], sfk, ps2[:, :], pk2)
                for h in range(8):
                    P.dma(O["nak"][seq, j, h, t0:t0 + 128, :], sf[:, h * 64:(h + 1) * 64], r=[sfk], w=["o_nak"], isout=True)
            P.dma(S["vA"][rowsl, :], st[:, :, :].rearrange("p h d -> p (h d)"), r=[sk], w=["vA"])
            ps, pk = psg.next()
            proj_tm(ps, pk, 2560, 512, i)
            st, sk = sbf.next()
            evac(st[:, :], sk, ps[:, :], pk)
            P.dma(S["vB"][rowsl, :], st[:, :], r=[sk], w=["vB"])
            ps, pk = psg.next()
            proj_tm(ps, pk, 3104, 512, i)
            sf, sfk = sf32.next()
            evac(sf[:, :], sfk, ps[:, :], pk)
            P.dma(S["gate"][rowsl, :], sf[:, :], r=[sfk], w=["gate"])
        glr = []
        if "gla" not in parts:
            continue
        for d in range(2):
            ps, pk = psg.next()
            proj_fm(ps, pk, 3072 + 16 * d, 16, tt)
            g, gk = glr_sb[d].next()
            evac(g[:, :], gk, ps[0:16, :], pk)
            glr.append((g, gk))
        for m in range(4):
            pq, pqk = psq.next()
            proj_fm(pq, pqk, 1536 + m * 128, 128, tt)
            pkk, pkkk = psk.next()
            proj_fm(pkk, pkkk, 2048 + m * 128, 128, tt)
            for d in range(2):
                pz, pzk = psz.next()
                P.op("pe", "matmul", r=["l2wg2", glr[d][1]], w=[pzk], out=pz[:, :], lhsT=wg2[:, d, m * 128:(m + 1) * 128], rhs=glr[d][0][:, :],
                     start=True, stop=True)
                t1, k1 = tA.next(); t2, k2 = tB.next(); t3, k3 = tC.next(); t4, k4 = tD.next(); t5, k5 = tE.next()
                P.op("act", "activation", r=[pzk, "l2negb"], w=[k1], out=t1[:, :], in_=pz[:, :], func=AF.Exp, scale=-1.0, bias=negb[:, 4 * d + m:4 * d + m + 1])
                P.op("act", "activation", r=[k1, "epsb"], w=[k2], out=t2[:, :], in_=t1[:, :], func=AF.Ln, bias=C.epsb[:, 1:2])
                if d == 0:
                    P.op("dve", "tensor_tensor_scan", r=["maskf", k2], w=[k3], out=t3[:, :], data0=maskf[:, :], data1=t2[:, :], initial=0.0,
                         op0=ALU.mult, op1=ALU.add)
                    last = t3[:, 63::64]
                else:
                    P.op("dve", "tensor_tensor_scan", r=["maskf", k2], w=[k3], out=t3[:, ::-1], data0=maskf[:, :], data1=t2[:, ::-1], initial=0.0,
                         op0=ALU.mult, op1=ALU.add)
                    last = t3[:, 0::64]
                P.op("act", "activation", r=[k3], w=[k4], out=t4[:, :], in_=t3[:, :], func=AF.Exp, scale=-1.0 / 16)
                P.op("act", "activation", r=[k3], w=[k5], out=t5[:, :], in_=t3[:, :], func=AF.Exp, scale=1.0 / 16)
                st, sk = sbf.next()
                P.op("dve", "scalar_tensor_tensor", r=[pqk, k4], w=[sk], out=st[:, :], in0=pq[:, :], scalar=0.125, in1=t4[:, :], op0=ALU.mult, op1=ALU.mult)
                P.dma(S["qtB"][d, m * 128:(m + 1) * 128, tok], st[:, :], r=[sk], w=["qtB"])
                st, sk = sbf.next()
                P.op("dve", "tensor_tensor", r=[pkkk, k5], w=[sk], out=st[:, :], in0=pkk[:, :], in1=t5[:, :], op=ALU.mult)
                P.dma(S["ktB"][d, m * 128:(m + 1) * 128, tok], st[:, :], r=[sk], w=["ktB"])
                P.op("dve", "tensor_tensor", r=[k3], w=[k1], out=t1[:, :].rearrange("p (c s) -> p c s", s=64),
                     in0=t3[:, :].rearrange("p (c s) -> p c s", s=64), in1=_bc(last.unsqueeze(2), [128, 8, 64]), op=ALU.subtract)
                P.op("act", "activation", r=[k1], w=[k1], out=t1[:, :], in_=t1[:, :], func=AF.Exp, scale=1.0 / 16)
                st, sk = sbf.next()
                P.op("dve", "tensor_tensor", r=[pkkk, k1], w=[sk], out=st[:, :], in0=pkk[:, :], in1=t1[:, :], op=ALU.mult)
                for c in range(4):
                    P.op("pe", "transpose", r=[sk, "identb"], w=[f"l2psT{d}"], out=psT[:, d, c, :], in_=st[:, c * 128:(c + 1) * 128], identity=C.identb[:, :])
                kt_, ktk = ketok.next()
                evac(kt_[:, :, :], ktk, psT[:, d, :, :], f"l2psT{d}")
                P.dma(S["kendB"][d, tok, m * 128:(m + 1) * 128].rearrange("(c p) f -> p c f", p=128), kt_[:, :, :], r=[ktk], w=["kendB"])
                dc, dck = dect.next()
                P.op("act", "activation", r=[k3], w=[dck], out=dc[:, :], in_=last, func=AF.Exp, scale=-1.0 / 16)
                P.dma(S["decB"][d, m * 128:(m + 1) * 128, tt * 8:(tt + 1) * 8], dc[:, :], r=[dck], w=["decB"])
    P.end()


class AttnRes:
    pass


def attn_setup(C, nq_max):
    P = C.P
    R = AttnRes()
    R.ps_s = Rot(P, "at_pss", 2, [128, 512], F32, psum=True)
    R.ps_o = Rot(P, "at_pso", 2, [128, 2, 4 * 65], F32, psum=False) if False else None
    R.pso = [P.ps(f"at_pso{i}", [128, 512], F32) for i in range(4)]
    R.ps_T = Rot(P, "at_psT", 1, [128, 4, 128], BF16, psum=True)
    R.E = Rot(P, "at_E", 2, [128, 512], F32)
    R.Pt = Rot(P, "at_Pt", 3, [128, 7, 512 // 4], BF16) if False else None
    R.rden = Rot(P, "at_rden", 2, [128, 8], F32)
    R.o = Rot(P, "at_o", 2, [128, 512], BF16)
    R.oT = Rot(P, "at_oT", 2, [128, 4, 512], BF16)
    return R


def build_mp(C, j):
    P, nc, I, S, O = C.P, C.nc, C.I, C.S, C.O
    P.begin()
    ps_m = P.ps("mp_psm", [128, 512], F32)
    Mp = C.Mp
    eoh = P.sb("ae_eoh", [32, 64, 128], F32)
    rrows = P.sb("ae_rrows", [120, 32], F32)
    rpbT = P.sb("ae_rpbT", [32, 120], F32)
    P.dma(eoh[:, :, 0:64], I["c_eoh"][:, :, :], w=["ae_eoh"])
    P.dma(eoh[:, :, 64:128], I["c_eoh"][:, :, :], w=["ae_eoh"])
    P.op("pool", "memset", w=["ae_rrows"], ap=rrows[:, :], constant=1.0)
    P.dma(rrows[:, 0:31], I["a_rpb"][j].rearrange("h r k -> (h r) k"), r=[], w=["ae_rrows"])
    P.op("pe", "transpose", r=["ae_rrows", "ident"], w=["ae_psm"], out=ps_m[0:32, 0:120], in_=rrows[:, :], identity=C.ident[0:120, 0:120])
    P.op("dve", "tensor_copy", r=["ae_psm"], w=["ae_rpbT"], out=rpbT[:, :], in_=ps_m[0:32, 0:120])
    P.op("pool", "memset", w=["ae_Mp"], ap=Mp[:, :, :, :], constant=0.0)
    for w0 in range(0, 64, 4):
        for u in range(4):
            P.op("pe", "matmul", r=["ae_eoh", "ae_rpbT"], w=["ae_psm"], out=ps_m[:, u * 120:(u + 1) * 120], lhsT=eoh[:, w0 + u, :], rhs=rpbT[:, :],
                 start=True, stop=True)
        src = ps_m[:, 0:480].rearrange("p (w h r) -> p h r w", w=4, h=8)
        wsl = slice(w0, w0 + 4)
        P.op("act", "activation", r=["ae_psm"], w=["ae_Mp"], out=Mp[0:64, :, 0:14, wsl], in_=src[0:64, :, 0:14, :], func=AF.Exp)
        P.op("act", "activation", r=["ae_psm"], w=["ae_Mp"], out=Mp[64:128, :, 0:14, wsl], in_=src[64:128, :, 1:15, :], func=AF.Exp)
        P.op("act", "activation", r=["ae_psm"], w=["ae_Mp"], out=Mp[0:64, :, 15:19, wsl], in_=src[0:64, :, 4:11:2, :], func=AF.Exp)
        P.op("act", "activation", r=["ae_psm"], w=["ae_Mp"], out=Mp[64:128, :, 14:18, wsl], in_=src[64:128, :, 3:10:2, :], func=AF.Exp)

    P.end()


def attn_even(C, j):
    P, nc, I, S, O = C.P, C.nc, C.I, C.S, C.O
    P.begin()
    R = attn_setup(C, 128)
    kT = P.sb("ae_kT", [64, 8, T], BF16)
    V = P.sb("ae_V", [128, NT, 8 * 65], BF16)
    for h in range(8):
        P.dma(kT[:, h, :], S["qkA"][512 + h * 64:512 + (h + 1) * 64, :], r=["qkA"], w=["ae_kT"])
    for n0 in range(0, NT, 4):
        P.dma(V[:, n0:n0 + 4, :], S["vA"][n0 * 128:(n0 + 4) * 128, :].rearrange("(n p) f -> p n f", p=128), r=["vA"], w=["ae_V"])
    ckT = P.sb("ae_ckT", [64, 8, 256], BF16)
    cV = P.sb("ae_cV", [128, 2, 8 * 65], BF16)
    ctmp = P.sb("ae_ctmp", [128, 2, 8, 64], F32)
    ps_m = P.ps("ae_psm", [128, 512], F32)
    for half in range(2):
        for h in range(8):
            P.dma(ctmp[:, half, h, :], I["cak"][j, h, half * 128:(half + 1) * 128, :], w=["ae_ctmp"])
    for half in range(2):
        for h in range(8):
            P.op("pe", "transpose", r=["ae_ctmp", "ident"], w=["ae_psm"], out=ps_m[0:64, h * 64:h * 64 + 128] if False else ps_m[0:64, (h % 4) * 128:(h % 4) * 128 + 128],
                 in_=ctmp[:, half, h, :], identity=C.ident[:, :])
            if h % 4 == 3:
                h0 = h - 3
                P.op("dve", "tensor_copy", r=["ae_psm"], w=["ae_ckT"], out=ckT[:, h0:h0 + 4, half * 128:(half + 1) * 128],
                     in_=ps_m[0:64, :].rearrange("p (h t) -> p h t", t=128))
    ctmp2 = P.sb("ae_ctmp2", [128, 2, 8, 64], F32)
    for half in range(2):
        for h in range(8):
            P.dma(ctmp2[:, half, h, :], I["cav"][j, h, half * 128:(half + 1) * 128, :], w=["ae_ctmp2"])
    P.op("pool", "memset", w=["ae_cV"], ap=cV[:, :, :], constant=1.0)
    P.op("dve", "tensor_copy", r=["ae_ctmp2"], w=["ae_cV"], out=cV[:, :, :].rearrange("p a (h e) -> p a h e", e=65)[:, :, :, 0:64], in_=ctmp2[:, :, :, :])
    Mp = C.Mp
    qblk = Rot(P, "ae_q", 2, [64, 8, 512], BF16)
    Pt = Rot(P, "ae_Pt", 3, [128, 7, 64], BF16)
    PtP = Rot(P, "ae_PtP", 3, [128, 2, 128], BF16)
    stage_cnt = [0]

    def finish_block(pso_pair, pkeys, nq, tok0, oT_t, oTk, col0):
        rd, rdk = R.rden.next()
        ot, otk = R.o.next()
        for b in range(2):
            v = pso_pair[b][0:nq, 0:260].rearrange("p (h e) -> p h e", e=65)
            P.op("dve", "reciprocal", r=[pkeys[b]], w=[rdk], out=rd[0:nq, 4 * b:4 * b + 4], in_=v[:, :, 64])
            P.op("dve", "tensor_tensor", r=[pkeys[b], rdk], w=[otk], out=ot[0:nq, 256 * b:256 * (b + 1)].rearrange("p (h d) -> p h d", d=64),
                 in0=v[:, :, 0:64], in1=_bc(rd[0:nq, 4 * b:4 * b + 4].unsqueeze(2), [nq, 4, 64]), op=ALU.mult)
        pT, pTk = R.ps_T.next()
        for m in range(4):
            P.op("pe", "transpose", r=[otk, "identb"], w=[pTk], out=pT[:, m, 0:nq], in_=ot[0:nq, m * 128:(m + 1) * 128], identity=C.identb[0:nq, 0:nq])
        P.op("act", "activation", r=[pTk], w=[oTk], out=oT_t[:, :, col0:col0 + nq], in_=pT[:, :, 0:nq], func=AF.Identity)

    pso_i = 0
    for blk in range(8):
        q, qk = qblk.next()
        for h in range(8):
            P.dma(q[:, h, :], S["qkA"][h * 64:(h + 1) * 64, blk * 512:(blk + 1) * 512], r=["qkA"], w=[qk])
        oT_t, oTk = R.oT.next()
        for ri in range(8):
            i = blk * 8 + ri
            r0 = min(max(i - 4, 0), 56)
            if 4 <= i <= 60 and i % 2 == 1:
                a0 = (i - 5) // 2; nl = 5; slots = slice(14, 19)
            else:
                a0 = r0 // 2; nl = 4; s0 = 2 * a0 - i + 7; slots = slice(s0, s0 + 7, 2)
            pso_pair = (R.pso[2 * (pso_i % 2)], R.pso[2 * (pso_i % 2) + 1])
            pkeys = (f"at_pso{2 * (pso_i % 2)}", f"at_pso{2 * (pso_i % 2) + 1}")
            pso_i += 1
            def scores(h):
                ps, psk = R.ps_s.next()
                qrhs = q[:, h, ri * 64:(ri + 1) * 64]
                for x in range(nl):
                    P.op("pe", "matmul", r=["ae_kT", qk], w=[psk], out=ps[:, x * 64:(x + 1) * 64], lhsT=kT[:, h, (a0 + x) * 128:(a0 + x + 1) * 128], rhs=qrhs,
                         start=True, stop=True)
                for x in range(2):
                    P.op("pe", "matmul", r=["ae_ckT", qk], w=[psk], out=ps[:, (nl + x) * 64:(nl + x + 1) * 64], lhsT=ckT[:, h, x * 128:(x + 1) * 128], rhs=qrhs,
                         start=True, stop=True)
                E, Ek = R.E.next()
                pt, ptk = Pt.next()
                P.op("act", "activation", r=[psk], w=[Ek], out=E[:, 0:nl * 64], in_=ps[:, 0:nl * 64], func=AF.Exp, scale=0.125)
                P.op("act", "activation", r=[psk], w=[ptk], out=pt[:, nl:nl + 2, :], in_=ps[:, nl * 64:(nl + 2) * 64].rearrange("p (x q) -> p x q", q=64),
                     func=AF.Exp, scale=0.125)
                P.op("dve", "tensor_tensor", r=[Ek, "ae_Mp"], w=[ptk], out=pt[:, 0:nl, :], in0=E[:, 0:nl * 64].rearrange("p (x q) -> p x q", q=64),
                     in1=Mp[:, h, slots, :], op=ALU.mult)
                return pt, ptk

            def pv(h, pt, ptk):
                po = pso_pair[h // 4]; pok = pkeys[h // 4]
                hs = (h % 4) * 65
                for x in range(nl):
                    P.op("pe", "matmul", r=[ptk, "ae_V"], w=[pok], out=po[0:64, hs:hs + 65], lhsT=pt[:, x, :], rhs=V[:, a0 + x, h * 65:(h + 1) * 65],
                         start=(x == 0), stop=False)
                for x in range(2):
                    P.op("pe", "matmul", r=[ptk, "ae_cV"], w=[pok], out=po[0:64, hs:hs + 65], lhsT=pt[:, nl + x, :], rhs=cV[:, x, h * 65:(h + 1) * 65],
                         start=False, stop=(x == 1))
            pend = scores(0)
            for h in range(8):
                nxt = scores(h + 1) if h < 7 else None
                pv(h, *pend)
                pend = nxt
            finish_block(pso_pair, pkeys, 64, i * 64, oT_t, oTk, ri * 64)
        P.dma(S["oT"][0:512, blk * 512:(blk + 1) * 512].rearrange("(m p) t -> p m t", p=128), oT_t[:, :, :], r=[oTk], w=["oT_a"])
    q, qk = qblk.next()
    for h in range(8):
        P.dma(q[:, h, :], S["qkA"][h * 64:(h + 1) * 64, TS:TS + 512], r=["qkA"], w=[qk])
    oT_t, oTk = R.oT.next()
    for sq in range(2):
        for qb in range(2):
            pso_pair = (R.pso[2 * (pso_i % 2)], R.pso[2 * (pso_i % 2) + 1])
            pkeys = (f"at_pso{2 * (pso_i % 2)}", f"at_pso{2 * (pso_i % 2) + 1}")
            pso_i += 1
            col = sq * 256 + qb * 128
            for h in range(8):
                ps, psk = R.ps_s.next()
                qrhs = q[:, h, col:col + 128]
                for x in range(2):
                    kc = TS + sq * 256 + x * 128
                    P.op("pe", "matmul", r=["ae_kT", qk], w=[psk], out=ps[:, x * 128:(x + 1) * 128], lhsT=kT[:, h, kc:kc + 128], rhs=qrhs, start=True, stop=True)
                pt, ptk = PtP.next()
                P.op("act", "activation", r=[psk], w=[ptk], out=pt[:, :, :], in_=ps[:, 0:256].rearrange("p (x q) -> p x q", q=128), func=AF.Exp, scale=0.125)
                po = pso_pair[h // 4]; pok = pkeys[h // 4]
                hs = (h % 4) * 65
                for x in range(2):
                    vn = (TS + sq * 256 + x * 128) // 128
                    P.op("pe", "matmul", r=[ptk, "ae_V"], w=[pok], out=po[:, hs:hs + 65], lhsT=pt[:, x, :], rhs=V[:, vn, h * 65:(h + 1) * 65],
                         start=(x == 0), stop=(x == 1))
            finish_block(pso_pair, pkeys, 128, 0, oT_t, oTk, col)
    P.dma(S["oT"][0:512, TS:TS + 512].rearrange("(m p) t -> p m t", p=128), oT_t[:, :, :], r=[oTk], w=["oT_a"])
    P.end()


def scan_finalize(C, ng_row, key_prefix):
    P, nc, I, S, O = C.P, C.nc, C.I, C.S, C.O
    P.begin()
    ngb = P.sb("fz_ng", [128, 512], F32)
    P.dma(ngb[:, :], ng_row.partition_broadcast(128), w=["fz_ng"])
    of = Rot(P, "fz_of", 2, [128, 512], F32)
    ob = Rot(P, "fz_ob", 2, [128, 512], F32)
    gt = Rot(P, "fz_gt", 2, [128, 512], F32)
    sq = Rot(P, "fz_sq", 2, [128, 512], F32)
    ss = Rot(P, "fz_ss", 2, [128, 16], F32)
    ob16 = Rot(P, "fz_o16", 2, [128, 512], BF16)
    psT = Rot(P, "fz_psT", 2, [128, 4, 128], BF16, psum=True)
    oTs = Rot(P, "fz_oT", 2, [128, 4, 512], BF16)
    oT_t = None
    for i in range(NT):
        rows = slice(i * 128, (i + 1) * 128)
        a, ak = of.next(); b, bk = ob.next(); g, gk = gt.next(); q, qk = sq.next(); s_, sk = ss.next(); o16, o16k = ob16.next()
        P.dma(a[:, :], S["ofb"][0, rows, :], r=["ofb0"], w=[ak])
        P.dma(b[:, :], S["ofb"][1, rows, :], r=["ofb1"], w=[bk])
        P.dma(g[:, :], S["gate"][rows, :], r=["gate"], w=[gk])
        P.op("dve", "tensor_tensor", r=[ak, bk], w=[ak], out=a[:, :], in0=a[:, :], in1=b[:, :], op=ALU.add)
        P.op("dve", "tensor_tensor", r=[ak], w=[qk], out=q[:, :], in0=a[:, :], in1=a[:, :], op=ALU.mult)
        P.op("dve", "tensor_reduce", r=[qk], w=[sk], out=s_[:, 0:8], in_=q[:, :].rearrange("p (h d) -> p h d", d=64), axis=AX.X, op=ALU.add)
        P.op("act", "activation", r=[sk, "epsb"], w=[sk], out=s_[:, 8:16], in_=s_[:, 0:8], func=AF.Ln, scale=1.0 / 64, bias=C.epsb[:, 0:1])
        P.op("act", "activation", r=[sk], w=[sk], out=s_[:, 0:8], in_=s_[:, 8:16], func=AF.Exp, scale=-0.5)
        P.op("dve", "tensor_tensor", r=[ak, sk], w=[ak], out=a[:, :].rearrange("p (h d) -> p h d", d=64), in0=a[:, :].rearrange("p (h d) -> p h d", d=64),
             in1=_bc(s_[:, 0:8].unsqueeze(2), [128, 8, 64]), op=ALU.mult)
        P.op("dve", "tensor_tensor", r=[ak, "fz_ng"], w=[ak], out=a[:, :], in0=a[:, :], in1=ngb[:, :], op=ALU.mult)
        P.op("act", "activation", r=[gk], w=[bk], out=b[:, :], in_=g[:, :], func=AF.Exp, scale=-1.0)
        P.op("dve", "tensor_scalar", r=[bk], w=[bk], out=b[:, :], in0=b[:, :], scalar1=1.0, scalar2=None, op0=ALU.add)
        P.op("dve", "reciprocal", r=[bk], w=[bk], out=b[:, :], in_=b[:, :])
        P.op("dve", "tensor_tensor", r=[gk, bk], w=[gk], out=g[:, :], in0=g[:, :], in1=b[:, :], op=ALU.mult)
        P.op("dve", "tensor_tensor", r=[ak, gk], w=[o16k], out=o16[:, :], in0=a[:, :], in1=g[:, :], op=ALU.mult)
        pT, pTk = psT.next()
        for m in range(4):
            P.op("pe", "transpose", r=[o16k, "identb"], w=[pTk], out=pT[:, m, :], in_=o16[:, m * 128:(m + 1) * 128], identity=C.identb[:, :])
        if i % 4 == 0:
            oT_t, oTk = oTs.next()
        P.op("act", "activation", r=[pTk], w=[oTk], out=oT_t[:, :, (i % 4) * 128:(i % 4 + 1) * 128], in_=pT[:, :, :], func=AF.Identity)
        if i % 4 == 3:
            blk = i // 4
            P.dma(S["oT"][512:1024, blk * 512:(blk + 1) * 512].rearrange("(m p) t -> p m t", p=128), oT_t[:, :, :], r=[oTk], w=["oT_b"])
    P.end()


def gla_scan(C, j):
    P, nc, I, S, O = C.P, C.nc, C.I, C.S, C.O
    P.begin()
    tri = P.sb("gs_tri", [64, 2, 64], F32)
    P.dma(tri[:, 0, :], I["c_tri"][0, 0:64, 0:64], w=["gs_tri"])
    P.dma(tri[:, 1, :], I["c_tri"][1, 0:64, 0:64], w=["gs_tri"])
    dec = P.sb("gs_dec", [64, 2, 8, 72], F32)
    for d in range(2):
        for h in range(8):
            P.dma(dec[:, d, h, :], S["decB"][d, h * 64:(h + 1) * 64, :], r=["decB"], w=["gs_dec"])
    Sst = [P.sb(f"gs_S{d}", [64, 8, 64], F32) for d in range(2)]
    Sbf = [P.sb(f"gs_Sbf{d}", [64, 8, 64], BF16) for d in range(2)]
    Stmp = [P.sb(f"gs_St{d}", [64, 8, 64], F32) for d in range(2)]
    qtb = [Rot(P, f"gs_qt{d}_", 2, [64, 8, 512], BF16) for d in range(2)]
    ktb = [Rot(P, f"gs_kt{d}_", 2, [64, 8, 512], BF16) for d in range(2)]
    keb = [Rot(P, f"gs_ke{d}_", 2, [64, 8, 512], BF16) for d in range(2)]
    vbl = [Rot(P, f"gs_v{d}_", 2, [64, 8, 512], BF16) for d in range(2)]
    ost = [Rot(P, f"gs_o{d}_", 1, [64, 8, 512], F32) for d in range(2)]
    attm = [Rot(P, f"gs_att{d}_", 2, [64, 8, 64], BF16) for d in range(2)]
    ps_att = [P.ps(f"gs_psatt{d}", [128, 512], F32) for d in range(2)]
    ps_o = [P.ps(f"gs_pso{d}", [128, 512], F32) for d in range(2)]
    ps_s = [P.ps(f"gs_pss{d}", [128, 512], F32) for d in range(2)]

    for sqi, (t0, tl) in enumerate(SEQS):
        nch = tl // 64
        nblk = tl // 512 if tl >= 512 else 1
        bl = min(tl, 512)
        cpb = bl // 64
        for d in range(2):
            if sqi == 0:
                for h in range(8):
                    P.dma(Sst[d][:, h, :], I["sb"][j, d, h, :, :], w=[f"gs_S{d}"])
            else:
                P.op("pool", "memset", w=[f"gs_S{d}"], ap=Sst[d][:, :, :], constant=0.0)
            P.op("act", "activation", r=[f"gs_S{d}"], w=[f"gs_Sbf{d}"], out=Sbf[d][:, :, :], in_=Sst[d][:, :, :], func=AF.Identity)
        cur = [None, None]

        def gchunk(step, d):
                c = step if d == 0 else nch - 1 - step
                blk = c // cpb
                cc = c % cpb
                first_in_blk = (cc == 0) if d == 0 else (cc == cpb - 1)
                last_in_blk = (cc == cpb - 1) if d == 0 else (cc == 0)
                tb = t0 + blk * bl
                if first_in_blk:
                    qt, qtk = qtb[d].next(); kt, ktk = ktb[d].next(); ke, kek = keb[d].next(); vv, vk = vbl[d].next(); oo, ook = ost[d].next()
                    for h in range(8):
                        P.dma(qt[:, h, 0:bl], S["qtB"][d, h * 64:(h + 1) * 64, tb:tb + bl], r=["qtB"], w=[qtk])
                        P.dma(kt[:, h, 0:bl], S["ktB"][d, h * 64:(h + 1) * 64, tb:tb + bl], r=["ktB"], w=[ktk])
                    P.dma(ke[:, 0:cpb, :], S["kendB"][d, tb:tb + bl, :].rearrange("(c p) f -> p c f", p=64), r=["kendB"], w=[kek])
                    P.dma(vv[:, 0:cpb, :], S["vB"][tb:tb + bl, :].rearrange("(c p) f -> p c f", p=64), r=["vB"], w=[vk])
                    cur[d] = (qt, qtk, kt, ktk, ke, kek, vv, vk, oo, ook)
                qt, qtk, kt, ktk, ke, kek, vv, vk, oo, ook = cur[d]
                cs = slice(cc * 64, (cc + 1) * 64)
                gch = t0 // 64 + c
                pa = ps_att[d]; pak = f"gs_psatt{d}"
                for h in range(8):
                    P.op("pe", "matmul", r=[ktk, qtk], w=[pak], out=pa[0:64, h * 64:(h + 1) * 64], lhsT=kt[:, h, cs], rhs=qt[:, h, cs], start=True, stop=True)
                am, amk = attm[d].next()
                P.op("dve", "tensor_tensor", r=[pak, "gs_tri"], w=[amk], out=am[:, :, :], in0=pa[0:64, :].rearrange("p (h c) -> p h c", c=64),
                     in1=_bc(tri[:, d:d + 1, :], [64, 8, 64]), op=ALU.mult)
                po = ps_o[d]; pok = f"gs_pso{d}"
                for h in range(8):
                    P.op("pe", "matmul", r=[amk, vk], w=[pok], out=po[0:64, h * 64:(h + 1) * 64], lhsT=am[:, h, :], rhs=vv[:, cc, h * 64:(h + 1) * 64], start=True, stop=False)
                    P.op("pe", "matmul", r=[qtk, f"gs_Sbf{d}"], w=[pok], out=po[0:64, h * 64:(h + 1) * 64], lhsT=qt[:, h, cs], rhs=Sbf[d][:, h, :], start=False, stop=True)
                P.op("act", "activation", r=[pok], w=[ook], out=oo[:, cc, :], in_=po[0:64, :], func=AF.Identity)
                pS = ps_s[d]; pSk = f"gs_pss{d}"
                for h in range(8):
                    P.op("pe", "matmul", r=[kek, vk], w=[pSk], out=pS[0:64, h * 64:(h + 1) * 64], lhsT=ke[:, cc, h * 64:(h + 1) * 64], rhs=vv[:, cc, h * 64:(h + 1) * 64],
                         start=True, stop=True)
                P.op("dve", "tensor_tensor", r=[f"gs_S{d}", "gs_dec"], w=[f"gs_St{d}"], out=Stmp[d][:, :, :], in0=Sst[d][:, :, :],
                     in1=_bc(dec[:, d, :, gch:gch + 1], [64, 8, 64]), op=ALU.mult)
                P.op("dve", "tensor_tensor", r=[f"gs_St{d}", pSk], w=[f"gs_S{d}"], out=Sst[d][:, :, :], in0=Stmp[d][:, :, :],
                     in1=pS[0:64, :].rearrange("p (h v) -> p h v", v=64), op=ALU.add)
                P.op("act", "activation", r=[f"gs_S{d}"], w=[f"gs_Sbf{d}"], out=Sbf[d][:, :, :], in_=Sst[d][:, :, :], func=AF.Identity)
                if last_in_blk:
                    P.dma(S["ofb"][d, tb:tb + bl, :].rearrange("(c p) f -> p c f", p=64), oo[:, 0:cpb, :], r=[ook], w=[f"ofb{d}"])

        for step in range(nch):
            lists = []
            for d in range(2):
                ops, _ = capture(P, lambda d=d: gchunk(step, d))
                lists.append(ops)
            P.ops.extend(zipper(lists))
        if sqi > 0:
            for d in range(2):
                for h in range(8):
                    P.dma(O["nsb"][sqi - 1, j, d, h, :, :], Sst[d][:, h, :], r=[f"gs_S{d}"], w=["o_nsb"], isout=True)
    P.end()


def proj_residual(C, l, which, wsrc, nk, act_src, act_key, xsrc, xkey):
    P, nc, I, S, O = C.P, C.nc, C.I, C.S, C.O
    P.begin()
    W = P.sb("pr_w", [128, nk, 1024], BF16)
    load_w_bf16(C, W, wsrc, 1024, "prw")
    gbc = P.sb("pr_g", [128, 3, 1024], F32)
    for s_ in range(3):
        P.dma(gbc[:, s_, :], S["gates"][l, s_:s_ + 1, which, :].partition_broadcast(128), r=[f"gates{l}"], w=["pr_g"])
    ablk = Rot(P, "pr_a", 2, [128, nk, 512], BF16)
    xt = Rot(P, "pr_x", 3, [128, 1024], F32)
    tmp = Rot(P, "pr_t", 2, [128, 1024], F32)
    ps = Rot(P, "pr_ps", 4, [128, 512], F32, psum=True)
    for blk in range(T // 512):
        a, ak = ablk.next()
        for k0 in range(0, nk, 8):
            kn = min(8, nk - k0)
            P.dma(a[:, k0:k0 + kn, :], act_src[k0 * 128:(k0 + kn) * 128, blk * 512:(blk + 1) * 512].rearrange("(k p) t -> p k t", p=128), r=[act_key], w=[ak])
        for u in range(4):
            i = blk * 4 + u
            seq = 0 if i < 32 else (1 if i < 34 else 2)
            x, xk = xt.next()
            t_, tk = tmp.next()
            P.dma(x[:, :], xsrc[i * 128:(i + 1) * 128, :], r=[xkey], w=[xk])
            for half in range(2):
                p_, pk = ps.next()
                for k in range(nk):
                    P.op("pe", "matmul", r=[ak, f"prw_{k}"], w=[pk], out=p_[:, :], lhsT=a[:, k, u * 128:(u + 1) * 128], rhs=W[:, k, half * 512:(half + 1) * 512],
                         start=(k == 0), stop=(k == nk - 1))
                P.op("dve", "tensor_tensor", r=[pk, "pr_g"], w=[tk], out=t_[:, half * 512:(half + 1) * 512], in0=p_[:, :], in1=gbc[:, seq, half * 512:(half + 1) * 512], op=ALU.mult)
            P.op("pool", "tensor_tensor", r=[xk, tk], w=[xk], out=x[:, :], in0=x[:, :], in1=t_[:, :], op=ALU.add)
            P.dma(S["xres"][i * 128:(i + 1) * 128, :], x[:, :], r=[xk], w=["xres_w"])
    P.end()


def ffn_up(C, l):
    P, nc, I, S, O = C.P, C.nc, C.I, C.S, C.O
    P.begin()
    NC_ = T + 4
    rows = P.sb("fu_rows", [128, 128], F32)
    ps_t = P.ps("fu_pst", [128, 512], F32)
    wc = P.sb("fu_wc", [128, 3, 44], F32)
    load_fm(C, wc[:, 0:2, :].rearrange("p i c -> p (i c)"), I["ffn_conv"][l, 0:2, :].rearrange("i (c p) -> (i c) p", p=128), 88, ps_t, rows, "fuwc01")
    load_fm(C, wc[:, 2, :], I["ffn_conv"][l, 2, :].rearrange("(c p) -> c p", p=128), 44, ps_t, rows, "fuwc2")
    wkeys_c = ["fuwc01", "fuwc2"]
    U = [P.sb(f"fu_U{g}", [128, NC_], F32) for g in range(2)]
    Cv = [P.sb(f"fu_C{g}", [128, NC_], F32) for g in range(2)]
    for g in range(2):
        P.op("pool", "memset", w=[f"fu_U{g}"], ap=U[g][:, :], constant=0.0)
    wt = Rot(P, "fu_w", 4, [128, 8, 128], BF16)
    wst = Rot(P, "fu_wst", 2, [128, 8, 128], F32)
    act = Rot(P, "fu_act", 2, [128, NC_], BF16)
    ps = Rot(P, "fu_ps", 6, [128, 512], F32, psum=True)
    colof = lambda t: t + 1 if t < TS else (t + 2 if t < TS + TP else t + 3)
    ev = 0
    skip = os.environ.get("FU_SKIP", "").split(",")
    for m in range(int(os.environ.get("FU_M", DFF // 128))):
        ws = []
        for g in range(2):
            w_, wk = wt.next()
            c0 = g * DFF + m * 128
            wf, wfk = wst.next()
            P.dma(wf[:, :, :], I["ffn_up"][l, :, c0:c0 + 128].rearrange("(k p) n -> p k n", p=128), w=[wfk])
            P.op("pool", "tensor_copy", r=[wfk], w=[wk], out=w_[:, :, :], in_=wf[:, :, :])
            ws.append((w_, wk))
        for g in range(2):
            w_, wk = ws[g]
            for tt in range(T // 512):
                p_, pk = ps.next()
                for k in range(8):
                    P.op("pe", "matmul", r=[wk] + [f"hT{4 * tt + u}" for u in range(4)], w=[pk], out=p_[:, :], lhsT=w_[:, k, :], rhs=C.hT[:, k, tt * 512:(tt + 1) * 512],
                         start=(k == 0), stop=(k == 7))
                pieces = [(0, 512)] if tt < 8 else [(0, 256), (256, 512)]
                for (a0, a1) in pieces:
                    c_ = colof(tt * 512 + a0)
                    P.op("act", "activation", r=[pk], w=[f"fu_U{g}"], out=U[g][:, c_:c_ + (a1 - a0)], in_=p_[:, a0:a1], func=AF.Identity)
            ci = g * 22 + m
            n = NC_ - 2
            if "conv" in skip:
                continue
            P.op("dve", "tensor_scalar", r=[f"fu_U{g}"] + wkeys_c, w=[f"fu_C{g}"], out=Cv[g][:, 1:1 + n], in0=U[g][:, 0:n], scalar1=wc[:, 0, ci:ci + 1], scalar2=None, op0=ALU.mult)
            P.op("dve", "scalar_tensor_tensor", r=[f"fu_U{g}", f"fu_C{g}"] + wkeys_c, w=[f"fu_C{g}"], out=Cv[g][:, 1:1 + n], in0=U[g][:, 1:1 + n], scalar=wc[:, 1, ci:ci + 1],
                 in1=Cv[g][:, 1:1 + n], op0=ALU.mult, op1=ALU.add)
            P.op("dve", "scalar_tensor_tensor", r=[f"fu_U{g}", f"fu_C{g}"] + wkeys_c, w=[f"fu_C{g}"], out=Cv[g][:, 1:1 + n], in0=U[g][:, 2:2 + n], scalar=wc[:, 2, ci:ci + 1],
                 in1=Cv[g][:, 1:1 + n], op0=ALU.mult, op1=ALU.add)
        if "silu" not in skip:
            P.op("act", "activation", r=["fu_C1"], w=["fu_C1"], out=Cv[1][:, 1:NC_ - 1], in_=Cv[1][:, 1:NC_ - 1], func=AF.Silu)
        if "mul" in skip:
            continue
        a_, ak = act.next()
        for (t0, tl) in SEQS:
            c_ = colof(t0)
            P.op("dve", "tensor_tensor", r=["fu_C0", "fu_C1"], w=[ak], out=a_[:, t0:t0 + tl], in0=Cv[0][:, c_:c_ + tl], in1=Cv[1][:, c_:c_ + tl], op=ALU.mult)
        P.dma(S["actT"][m * 128:(m + 1) * 128, :], a_[:, 0:T], r=[ak], w=["actT"])
    P.end()


def final_norm(C):
    P, nc, I, S, O = C.P, C.nc, C.I, C.S, C.O
    P.begin()
    gb = P.sb("fn_g", [128, 1024], F32)
    P.dma(gb[:, :], I["final_g"].rearrange("(o n) -> o n", o=1).partition_broadcast(128), w=["fn_g"])
    xt = Rot(P, "fn_x", 3, [128, 1024], F32)
    junk = P.sb("fn_junk", [128, 1024], BF16)
    ss = Rot(P, "fn_ss", 3, [128, 4], F32)
    yt = Rot(P, "fn_y", 3, [128, 1024], F32)
    for i in range(NT):
        x, xk = xt.next(); s_, sk = ss.next(); y, yk = yt.next()
        P.dma(x[:, :], S["xres"][i * 128:(i + 1) * 128, :], r=["xres_w"], w=[xk])
        P.op("act", "activation", r=[xk], w=["fn_junk", sk], out=junk[:, :], in_=x[:, :], func=AF.Square, accum_out=s_[:, 0:1])
        P.op("act", "activation", r=[sk, "epsb"], w=[sk], out=s_[:, 1:2], in_=s_[:, 0:1], func=AF.Ln, scale=1.0 / D, bias=C.epsb[:, 0:1])
        P.op("act", "activation", r=[sk], w=[sk], out=s_[:, 2:3], in_=s_[:, 1:2], func=AF.Exp, scale=-0.5)
        P.op("dve", "scalar_tensor_tensor", r=[xk, sk, "fn_g"], w=[yk], out=y[:, :], in0=x[:, :], scalar=s_[:, 2:3], in1=gb[:, :], op0=ALU.mult, op1=ALU.mult)
        P.dma(O["y"][i * 128:(i + 1) * 128, :], y[:, :], r=[yk], w=["o_y"], isout=True)
    P.end()


def l2_odd(C, j):
    P, nc, I, S, O = C.P, C.nc, C.I, C.S, C.O
    P.begin()
    W = P.sb("lo_w", [128, 8, OD_IN], BF16)
    load_w_bf16(C, W, I["od_w_in"][j], OD_IN, "low")
    wkeys = [f"low_{k}" for k in range(8)]
    rows = P.sb("lo_rows", [64, 128], F32)
    ps_t = P.ps("lo_pst", [128, 512], F32)
    wc = P.sb("lo_wc", [128, 3, 12], F32)
    load_fm(C, wc[:, :, :].rearrange("p i c -> p (i c)"), I["d_conv"][j].rearrange("i (c p) -> (i c) p", p=128), 36, ps_t, rows, "lowc")
    perm = P.sb("lo_perm", [128, 128], F32)
    P.dma(perm[:, :], I["c_perm"][:, :], w=["lo_perm"])
    bd = P.sb("lo_bd", [128, 128], F32)
    P.dma(bd[:, :], I["c_bd"][:, :], w=["lo_bd"])
    maskf = P.sb("lo_maskf", [8, 512], F32)
    P.dma(maskf[:, :], I["c_maskf"][0:8, :], w=["lo_maskf"])
    diag8 = P.sb("lo_diag8", [8, 8], F32)
    P.dma(diag8[:, :], I["c_ident"][0:8, 0:8], w=["lo_diag8"])
    par = P.sb("lo_par", [8, 4], F32)
    P.dma(par[:, 0:2], I["d_dt_bias"][j].rearrange("d h -> h d"), w=["lo_par"], allow_slow_non_contiguous=True)
    P.dma(par[:, 2:4], I["d_a_log"][j].rearrange("d h -> h d"), w=["lo_par"], allow_slow_non_contiguous=True)
    P.op("act", "activation", r=["lo_par"], w=["lo_par"], out=par[:, 2:4], in_=par[:, 2:4], func=AF.Exp)
    P.op("dve", "tensor_scalar", r=["lo_par"], w=["lo_par"], out=par[:, 2:4], in0=par[:, 2:4], scalar1=-1.0, scalar2=None, op0=ALU.mult)

    psg = Rot(P, "lo_psg", 4, [128, 512], F32, psum=True)
    psT = Rot(P, "lo_psT", 2, [128, 512], F32, psum=True)
    sbf = Rot(P, "lo_sbf", 3, [128, 512], BF16)
    sf32 = Rot(P, "lo_sf", 5, [128, 512], F32)
    vst = Rot(P, "lo_vst", 2, [128, 2, 65], BF16)
    for t_, k_ in zip(vst.tiles, vst.keys):
        P.op("pool", "memset", w=[k_], ap=t_[:, :, 64:65], constant=1.0)
    cs_t = Rot(P, "lo_cs", 2, [128, 2, 512], F32)
    ev = [0]

    def evac(dst, dkey, src, skey, extra_r=()):
        ev[0] += 1
        if ev[0] % 2:
            P.op("act", "activation", r=[skey] + list(extra_r), w=[dkey], out=dst, in_=src, func=AF.Identity)
        else:
            P.op("dve", "tensor_copy", r=[skey] + list(extra_r), w=[dkey], out=dst, in_=src)

    def proj_fm(ps, pkey, c0, M, tt):
        for k in range(8):
            P.op("pe", "matmul", r=[wkeys[k]] + [f"hT{4 * tt + u}" for u in range(4)], w=[pkey],
                 out=ps[0:M, :], lhsT=W[:, k, c0:c0 + M], rhs=C.hT[:, k, tt * 512:(tt + 1) * 512], start=(k == 0), stop=(k == 7))

    def proj_tm(ps, pkey, c0, N, i):
        for k in range(8):
            P.op("pe", "matmul", r=[wkeys[k], f"hT{i}"], w=[pkey],
                 out=ps[:, 0:N], lhsT=C.hT[:, k, i * 128:(i + 1) * 128], rhs=W[:, k, c0:c0 + N], start=(k == 0), stop=(k == 7))

    for tt in range(T // 512):
        tok = slice(tt * 512, (tt + 1) * 512)
        if tt < 8:
            cs, csk = cs_t.next()
            P.dma(cs[:, 0, :], I["c_cos"][:, tok], w=[csk])
            P.dma(cs[:, 1, :], I["c_sin"][:, tok], w=[csk])
        for m in range(5):
            ps, pk = psg.next()
            proj_fm(ps, pk, m * 128, 128, tt)
            st, sk = sbf.next()
            if tt == 8:
                evac(st[:, :], sk, ps[:, :], pk)
            else:
                xf, xfk = sf32.next()
                evac(xf[:, :], xfk, ps[:, :], pk)
                pr, prk = psg.next()
                P.op("pe", "matmul", r=[xfk, "lo_perm"], w=[prk], out=pr[:, :], lhsT=perm[:, :], rhs=xf[:, :], start=True, stop=True)
                t2, t2k = sf32.next()
                P.op("dve", "tensor_tensor", r=[prk, csk], w=[t2k], out=t2[:, :], in0=pr[:, :], in1=cs[:, 1, :], op=ALU.mult)
                P.op("dve", "tensor_tensor", r=[xfk, csk], w=[xfk], out=xf[:, :], in0=xf[:, :], in1=cs[:, 0, :], op=ALU.mult)
                P.op("pool", "tensor_tensor", r=[xfk, t2k], w=[sk], out=st[:, :], in0=xf[:, :], in1=t2[:, :], op=ALU.add)
            P.dma(S["qkC"][m * 128:(m + 1) * 128, tok], st[:, :], r=[sk], w=["qkC"])
        for u in range(4):
            i = 4 * tt + u
            rowsl = slice(i * 128, (i + 1) * 128)
            ps, pk = psg.next()
            proj_tm(ps, pk, 512, 256, i)
            st, sk = vst.next()
            if i < 32:
                evac(st[:, :, 0:64], sk, ps[:, 128:256].rearrange("p (g d) -> p g d", d=64), pk)
            else:
                seq = (i - 32) // 2; t0 = ((i - 32) % 2) * 128
                sf, sfk = sf32.next()
                evac(sf[:, 0:256], sfk, ps[:, 0:256], pk)
                P.op("pool", "tensor_copy", r=[sfk], w=[sk], out=st[:, :, 0:64], in_=sf[:, 128:256].rearrange("p (g d) -> p g d", d=64))
                for g in range(2):
                    P.dma(O["nck"][seq, j, g, t0:t0 + 128, :], sf[:, g * 64:(g + 1) * 64], r=[sfk], w=["o_nck"], isout=True)
                    P.dma(O["ncv"][seq, j, g, t0:t0 + 128, :], sf[:, 128 + g * 64:128 + (g + 1) * 64], r=[sfk], w=["o_ncv"], isout=True)
            P.dma(S["vC"][rowsl, :], st[:, :, :].rearrange("p g d -> p (g d)"), r=[sk], w=["vC"])
            ps, pk = psg.next()
            proj_tm(ps, pk, 2336, 512, i)
            sf, sfk = sf32.next()
            evac(sf[:, :], sfk, ps[:, :], pk)
            P.dma(S["gate"][rowsl, :], sf[:, :], r=[sfk], w=["gate"])
    P.sub_begin()
    sc8 = Rot(P, "lo_sc8", 8, [8, 512], F32)
    gex = Rot(P, "lo_gex", 2, [8, 8, 512], F32)
    sct = Rot(P, "lo_sct", 2, [128, 4, 40], F32)
    for tt in range(T // 512):
        tok = slice(tt * 512, (tt + 1) * 512)
        for d in range(2):
            pa, pak = psg.next()
            proj_fm(pa, pak, 2304 + 8 * d, 8, tt)
            pb, pbk = psg.next()
            proj_fm(pb, pbk, 2320 + 8 * d, 8, tt)
            t1, k1 = sc8.next(); Gt, Gk = sc8.next(); Lt, Lk = sc8.next(); Ht, Hk = sc8.next(); Bt, Bk = sc8.next(); BEt, BEk = sc8.next(); ELt, ELk = sc8.next()
            P.op("act", "activation", r=[pak, "lo_par"], w=[k1], out=t1[:, :], in_=pa[0:8, :], func=AF.Exp, bias=par[:, d:d + 1])
            P.op("act", "activation", r=[k1, "epsb"], w=[k1], out=t1[:, :], in_=t1[:, :], func=AF.Ln, bias=C.epsb[0:8, 1:2])
            if d == 0:
                P.op("dve", "tensor_tensor_scan", r=["lo_maskf", k1], w=[Gk], out=Gt[:, :], data0=maskf[:, :], data1=t1[:, :], initial=0.0, op0=ALU.mult, op1=ALU.add)
                lastv = Gt[:, 63::64]
            else:
                P.op("dve", "tensor_tensor_scan", r=["lo_maskf", k1], w=[Gk], out=Gt[:, ::-1], data0=maskf[:, :], data1=t1[:, ::-1], initial=0.0, op0=ALU.mult, op1=ALU.add)
                lastv = Gt[:, 0::64]
            P.op("dve", "tensor_scalar", r=[Gk, "lo_par"], w=[Gk], out=Gt[:, :], in0=Gt[:, :], scalar1=par[:, 2 + d:3 + d], scalar2=None, op0=ALU.mult)
            P.op("act", "activation", r=[pbk], w=[Lk], out=Lt[:, :], in_=pb[0:8, :], func=AF.Exp, scale=-1.0)
            P.op("act", "activation", r=[Lk, "epsb"], w=[Lk], out=Lt[:, :], in_=Lt[:, :], func=AF.Ln, bias=C.epsb[0:8, 1:2])
            P.op("dve", "tensor_tensor", r=[Gk, Lk], w=[Hk], out=Ht[:, :], in0=Gt[:, :], in1=Lt[:, :], op=ALU.subtract)
            P.op("act", "activation", r=[Lk], w=[Bk], out=Bt[:, :], in_=Lt[:, :], func=AF.Exp, scale=-1.0)
            P.op("act", "activation", r=[Hk], w=[BEk], out=BEt[:, :], in_=Ht[:, :], func=AF.Exp)
            P.op("dve", "tensor_tensor", r=[Gk], w=[ELk], out=ELt[:, :].rearrange("p (c s) -> p c s", s=64), in0=_bc(lastv.unsqueeze(2), [8, 8, 64]),
                 in1=Gt[:, :].rearrange("p (c s) -> p c s", s=64), op=ALU.subtract)
            P.op("act", "activation", r=[ELk], w=[ELk], out=ELt[:, :], in_=ELt[:, :], func=AF.Exp)
            for (src, srck, name) in ((Gt, Gk, "Gexp"), (Ht, Hk, "Hexp")):
                gx, gxk = gex.next()
                P.op("pool", "tensor_tensor", r=[srck, "lo_diag8"], w=[gxk], out=gx[:, :, :], in0=_bc(src[:, :].unsqueeze(1), [8, 8, 512]),
                     in1=_bc(diag8[:, :].unsqueeze(2), [8, 8, 512]), op=ALU.mult)
                P.dma(S[name][d, :, :, tok], gx[:, :, :], r=[gxk], w=[name])
            pT, pTk = psT.next()
            for u in range(4):
                for qi, (src, srck) in enumerate(((Gt, Gk), (Ht, Hk), (Bt, Bk), (BEt, BEk), (ELt, ELk))):
                    P.op("pe", "transpose", r=[srck, "ident"], w=[pTk], out=pT[:, u * 40 + qi * 8:u * 40 + qi * 8 + 8], in_=src[:, u * 128:(u + 1) * 128],
                         identity=C.ident[0:8, 0:8])
            stt_, sttk = sct.next()
            evac(stt_[:, :, :], sttk, pT[:, 0:160].rearrange("p (u f) -> p u f", f=40), pTk)
            P.dma(S["dsc"][d, tok, :].rearrange("(u p) f -> p u f", p=128), stt_[:, :, :], r=[sttk], w=["dsc"])
    P.sub_end()
    NC_ = T + 4
    U = P.sb("lo_U", [128, NC_], F32)
    Cv = P.sb("lo_C", [128, NC_], F32)
    P.op("pool", "memset", w=["lo_U"], ap=U[:, :], constant=0.0)
    colof = lambda t: t + 1 if t < TS else (t + 2 if t < TS + TP else t + 3)
    rs_t = Rot(P, "lo_rs", 2, [128, 512], F32)
    for m in range(12):
        kind = m // 4
        mm = m % 4
        for tt in range(T // 512):
            ps, pk = psg.next()
            proj_fm(ps, pk, 768 + m * 128, 128, tt)
            pieces = [(0, 512)] if tt < 8 else [(0, 256), (256, 512)]
            for (a0, a1) in pieces:
                c_ = colof(tt * 512 + a0)
                evac(U[:, c_:c_ + (a1 - a0)], "lo_U", ps[:, a0:a1], pk)
        n = NC_ - 2
        P.op("act", "activation", r=["lo_U", "lowc"], w=["lo_C"], out=Cv[:, 1:1 + n], in_=U[:, 0:n], func=AF.Identity, scale=wc[:, 0, m:m + 1])
        P.op("dve", "scalar_tensor_tensor", r=["lo_U", "lo_C", "lowc"], w=["lo_C"], out=Cv[:, 1:1 + n], in0=U[:, 1:1 + n], scalar=wc[:, 1, m:m + 1], in1=Cv[:, 1:1 + n],
             op0=ALU.mult, op1=ALU.add)
        P.op("dve", "scalar_tensor_tensor", r=["lo_U", "lo_C", "lowc"], w=["lo_C"], out=Cv[:, 1:1 + n], in0=U[:, 2:2 + n], scalar=wc[:, 2, m:m + 1], in1=Cv[:, 1:1 + n],
             op0=ALU.mult, op1=ALU.add)
        P.op("act", "activation", r=["lo_C"], w=["lo_C"], out=Cv[:, 1:1 + n], in_=Cv[:, 1:1 + n], func=AF.Silu)
        for tt in range(T // 512):
            pieces = [(0, 512)] if tt < 8 else [(0, 256), (256, 512)]
            tok = slice(tt * 512, (tt + 1) * 512)
            xn, xnk = sf32.next()
            for (a0, a1) in pieces:
                c_ = colof(tt * 512 + a0); w_ = a1 - a0
                if kind == 2:
                    P.op("act", "activation", r=["lo_C"], w=[xnk], out=xn[:, a0:a1], in_=Cv[:, c_:c_ + w_], func=AF.Identity)
                else:
                    sq, sqk = sf32.next()
                    P.op("act", "activation", r=["lo_C"], w=[sqk], out=sq[:, 0:w_], in_=Cv[:, c_:c_ + w_], func=AF.Square)
                    pn, pnk = psg.next()
                    P.op("pe", "matmul", r=[sqk, "lo_bd"], w=[pnk], out=pn[:, 0:w_], lhsT=bd[:, :], rhs=sq[:, 0:w_], start=True, stop=True)
                    rs, rsk = rs_t.next()
                    P.op("act", "activation", r=[pnk, "epsb"], w=[rsk], out=rs[:, 0:w_], in_=pn[:, 0:w_], func=AF.Ln, bias=C.epsb[:, 0:1])
                    P.op("act", "activation", r=[rsk], w=[rsk], out=rs[:, 0:w_], in_=rs[:, 0:w_], func=AF.Exp, scale=-0.5)
                    P.op("dve", "scalar_tensor_tensor", r=["lo_C", rsk], w=[xnk], out=xn[:, a0:a1], in0=Cv[:, c_:c_ + w_], scalar=(0.125 if kind == 0 else 1.0), in1=rs[:, 0:w_],
                         op0=ALU.mult, op1=ALU.mult)
            if kind < 2:
                st, sk = sbf.next()
                P.op("dve", "tensor_copy", r=[xnk], w=[sk], out=st[:, :], in_=xn[:, :])
                P.dma(S["qnT" if kind == 0 else "knT"][mm * 128:(mm + 1) * 128, tok], st[:, :], r=[sk], w=["qknT"])
            if kind >= 1:
                pT, pTk = psT.next()
                for u in range(4):
                    P.op("pe", "transpose", r=[xnk, "ident"], w=[pTk], out=pT[:, u * 128:(u + 1) * 128], in_=xn[:, u * 128:(u + 1) * 128], identity=C.ident[:, :])
                tk_, tkk = sf32.next()
                evac(tk_[:, :], tkk, pT[:, :], pTk)
                dstn = "kn_tok" if kind == 1 else "v_tok"
                P.dma(S[dstn][tok, mm * 128:(mm + 1) * 128].rearrange("(u p) f -> p u f", p=128), tk_[:, :].rearrange("p (u f) -> p u f", f=128), r=[tkk], w=[dstn])
    P.end()


def attn_finish(C, R, pso_pair, pkeys, nq, oT_t, oTk, col0, esink=None):
    P = C.P
    rd, rdk = R.rden.next()
    ot, otk = R.o.next()
    for b in range(2):
        v = pso_pair[b][0:nq, 0:260].rearrange("p (h e) -> p h e", e=65)
        if esink is not None:
            P.op("dve", "tensor_tensor", r=[pkeys[b], "esink"], w=[rdk], out=rd[0:nq, 4 * b:4 * b + 4], in0=v[:, :, 64], in1=esink[0:nq, 4 * b:4 * b + 4], op=ALU.add)
            P.op("dve", "reciprocal", r=[rdk], w=[rdk], out=rd[0:nq, 4 * b:4 * b + 4], in_=rd[0:nq, 4 * b:4 * b + 4])
        else:
            P.op("dve", "reciprocal", r=[pkeys[b]], w=[rdk], out=rd[0:nq, 4 * b:4 * b + 4], in_=v[:, :, 64])
        P.op("dve", "tensor_tensor", r=[pkeys[b], rdk], w=[otk], out=ot[0:nq, 256 * b:256 * (b + 1)].rearrange("p (h d) -> p h d", d=64),
             in0=v[:, :, 0:64], in1=_bc(rd[0:nq, 4 * b:4 * b + 4].unsqueeze(2), [nq, 4, 64]), op=ALU.mult)
    pT, pTk = R.ps_T.next()
    for m in range(4):
        P.op("pe", "transpose", r=[otk, "identb"], w=[pTk], out=pT[:, m, 0:nq], in_=ot[0:nq, m * 128:(m + 1) * 128], identity=C.identb[0:nq, 0:nq])
    P.op("act", "activation", r=[pTk], w=[oTk], out=oT_t[:, :, col0:col0 + nq], in_=pT[:, :, 0:nq], func=AF.Identity)


def attn_odd(C, j):
    P, nc, I, S, O = C.P, C.nc, C.I, C.S, C.O
    P.begin()
    R = attn_setup(C, 128)
    kT = P.sb("ao_kT", [64, 2, T], BF16)
    V = P.sb("ao_V", [128, NT, 2 * 65], BF16)
    for g in range(2):
        P.dma(kT[:, g, :], S["qkC"][512 + g * 64:512 + (g + 1) * 64, :], r=["qkC"], w=["ao_kT"])
    for n0 in range(0, NT, 4):
        P.dma(V[:, n0:n0 + 4, :], S["vC"][n0 * 128:(n0 + 4) * 128, :].rearrange("(n p) f -> p n f", p=128), r=["vC"], w=["ao_V"])
    ckT = P.sb("ao_ckT", [64, 2, 256], BF16)
    cV = P.sb("ao_cV", [128, 2, 2 * 65], BF16)
    ctmp = P.sb("ao_ctmp", [128, 2, 2, 64], F32)
    ctmp2 = P.sb("ao_ctmp2", [128, 2, 2, 64], F32)
    ps_m = P.ps("ao_psm", [128, 512], F32)
    for half in range(2):
        for g in range(2):
            P.dma(ctmp[:, half, g, :], I["cck"][j, g, half * 128:(half + 1) * 128, :], w=["ao_ctmp"])
            P.dma(ctmp2[:, half, g, :], I["ccv"][j, g, half * 128:(half + 1) * 128, :], w=["ao_ctmp2"])
    for half in range(2):
        for g in range(2):
            P.op("pe", "transpose", r=["ao_ctmp", "ident"], w=["ao_psm"], out=ps_m[0:64, (half * 2 + g) * 128:(half * 2 + g + 1) * 128],
                 in_=ctmp[:, half, g, :], identity=C.ident[:, :])
    for half in range(2):
        P.op("dve", "tensor_copy", r=["ao_psm"], w=["ao_ckT"], out=ckT[:, :, half * 128:(half + 1) * 128],
             in_=ps_m[0:64, half * 256:(half + 1) * 256].rearrange("p (g t) -> p g t", t=128))
    P.op("pool", "memset", w=["ao_cV"], ap=cV[:, :, :], constant=1.0)
    P.op("dve", "tensor_copy", r=["ao_ctmp2"], w=["ao_cV"], out=cV[:, :, :].rearrange("p a (g e) -> p a g e", e=65)[:, :, :, 0:64], in_=ctmp2[:, :, :, :])
    esink = P.sb("ao_esink", [128, 8], F32)
    P.dma(esink[:, :], I["c_sink"][j:j + 1, :].partition_broadcast(128), w=["esink"])
    P.op("act", "activation", r=["esink"], w=["esink"], out=esink[:, :], in_=esink[:, :], func=AF.Exp)
    tri = P.sb("ao_tri", [128, 2, 128], F32)
    P.dma(tri[:, 0, :], I["c_tri"][0], w=["ao_tri"])
    P.dma(tri[:, 1, :], I["c_tri"][1], w=["ao_tri"])

    qblk = Rot(P, "ao_q", 2, [64, 8, 512], BF16)
    Pt = Rot(P, "ao_Pt", 10, [128, 4, 128], BF16)
    pso_i = [0]

    def unit(q, qk, qcol, chunks, oT_t, oTk, ocol):
        pso_pair = (R.pso[2 * (pso_i[0] % 2)], R.pso[2 * (pso_i[0] % 2) + 1])
        pkeys = (f"at_pso{2 * (pso_i[0] % 2)}", f"at_pso{2 * (pso_i[0] % 2) + 1}")
        pso_i[0] += 1
        allpts = []
        for g in range(2):
            pts = []
            for (kfn, kkey, vfn, vkey, mask) in chunks:
                ps, psk = R.ps_s.next()
                P.op("pe", "matmul", r=[kkey, qk], w=[psk], out=ps[:, :], lhsT=kfn(g), rhs=q[:, 4 * g:4 * g + 4, qcol:qcol + 128], start=True, stop=True)
                pt, ptk = Pt.next()
                if mask is None:
                    P.op("act", "activation", r=[psk], w=[ptk], out=pt[:, :, :], in_=ps[:, :].rearrange("p (h q) -> p h q", q=128), func=AF.Exp, scale=0.125)
                else:
                    E, Ek = R.E.next()
                    P.op("act", "activation", r=[psk], w=[Ek], out=E[:, :], in_=ps[:, :], func=AF.Exp, scale=0.125)
                    P.op("dve", "tensor_tensor", r=[Ek, "ao_tri"], w=[ptk], out=pt[:, :, :], in0=E[:, :].rearrange("p (h q) -> p h q", q=128),
                         in1=_bc(tri[:, mask:mask + 1, :], [128, 4, 128]), op=ALU.mult)
                pts.append((pt, ptk, vfn, vkey))
            allpts.append(pts)
        for g in range(2):
            pts = allpts[g]
            po = pso_pair[g]; pok = pkeys[g]
            for hh in range(4):
                for x, (pt, ptk, vfn, vkey) in enumerate(pts):
                    P.op("pe", "matmul", r=[ptk, vkey], w=[pok], out=po[:, hh * 65:(hh + 1) * 65], lhsT=pt[:, hh, :], rhs=vfn(g),
                         start=(x == 0), stop=(x == len(pts) - 1))
        attn_finish(C, R, pso_pair, pkeys, 128, oT_t, oTk, ocol, esink=esink)

    def kchunk(tok0):
        return (lambda g, tok0=tok0: kT[:, g, tok0:tok0 + 128])

    def vchunk(n):
        return (lambda g, n=n: V[:, n, g * 65:(g + 1) * 65])

    ctx_chunks = [((lambda g, x=x: ckT[:, g, x * 128:(x + 1) * 128]), "ao_ckT", (lambda g, x=x: cV[:, x, g * 65:(g + 1) * 65]), "ao_cV", None) for x in range(2)]
    for blk in range(8):
        q, qk = qblk.next()
        for h in range(8):
            P.dma(q[:, h, :], S["qkC"][h * 64:(h + 1) * 64, blk * 512:(blk + 1) * 512], r=["qkC"], w=[qk])
        oT_t, oTk = R.oT.next()
        for u in range(4):
            n = blk * 4 + u
            chunks = []
            if n > 0:
                chunks.append((kchunk((n - 1) * 128), "ao_kT", vchunk(n - 1), "ao_V", 1))
            chunks.append((kchunk(n * 128), "ao_kT", vchunk(n), "ao_V", None))
            if n < 31:
                chunks.append((kchunk((n + 1) * 128), "ao_kT", vchunk(n + 1), "ao_V", 0))
            chunks += ctx_chunks
            unit(q, qk, u * 128, chunks, oT_t, oTk, u * 128)
        P.dma(S["oT"][0:512, blk * 512:(blk + 1) * 512].rearrange("(m p) t -> p m t", p=128), oT_t[:, :, :], r=[oTk], w=["oT_a"])
    q, qk = qblk.next()
    for h in range(8):
        P.dma(q[:, h, :], S["qkC"][h * 64:(h + 1) * 64, TS:TS + 512], r=["qkC"], w=[qk])
    oT_t, oTk = R.oT.next()
    for sq in range(2):
        for qb in range(2):
            col = sq * 256 + qb * 128
            chunks = [(kchunk(TS + sq * 256 + x * 128), "ao_kT", vchunk((TS + sq * 256 + x * 128) // 128), "ao_V", None) for x in range(2)]
            unit(q, qk, col, chunks, oT_t, oTk, col)
    P.dma(S["oT"][0:512, TS:TS + 512].rearrange("(m p) t -> p m t", p=128), oT_t[:, :, :], r=[oTk], w=["oT_a"])
    P.end()


def zipper(lists):
    lists = [l for l in lists if l]
    pos = [0] * len(lists)
    out = []
    total = sum(len(l) for l in lists)
    while len(out) < total:
        best = None
        for i, l in enumerate(lists):
            if pos[i] < len(l):
                f = pos[i] / len(l)
                if best is None or f < best[0]:
                    best = (f, i)
        i = best[1]
        out.append(lists[i][pos[i]])
        pos[i] += 1
    return out


def capture(P, fn):
    saved = P.ops
    P.ops = []
    ret = fn()
    out = P.ops
    P.ops = saved
    return out, ret


F32R = mybir.dt.float32r


def delta_scan(C, j):
    P, nc, I, S, O = C.P, C.nc, C.I, C.S, C.O
    P.begin()
    BL = 128
    CPB = BL // 64
    dm = P.sb("ds_dm", [64, 4, 64], F32)
    for q_ in range(4):
        P.dma(dm[:, q_, :], I["c_dmask"][q_], w=["ds_dm"])
    ones8 = P.sb("ds_ones8", [8, 64], F32)
    P.op("pool", "memset", w=["ds_ones8"], ap=ones8[:, :], constant=1.0)
    id64 = P.sb("ds_id64", [64, 64], F32)
    P.dma(id64[:, :], I["c_ident"][0:64, 0:64], w=["ds_id64"])
    PSP = [Rot(P, f"ds_psp{d}_", 3, [128, 512], F32, psum=True) for d in range(2)]
    PSR = [Rot(P, f"ds_psr{d}_", 1, [128, 512], F32, psum=True) for d in range(2)]
    DP = F32

    def mk(name, n, shape, dt):
        return [Rot(P, f"ds_{name}{d}_", n, shape, dt) for d in range(2)]
    knTb = mk("knT", 1, [64, 8, BL], BF16); qnTb = mk("qnT", 1, [64, 8, BL], BF16)
    kntok = mk("kntok", 1, [64, CPB, 512], F32); vtok = mk("vtok", 1, [64, CPB, 512], F32)
    dscb = mk("dsc", 1, [64, CPB, 40], F32)
    gexb = mk("gex", 1, [8, 8, BL], F32); hexb = mk("hex", 1, [8, 8, BL], F32)
    ost = mk("o", 2, [64, CPB, 512], F32)
    D1 = mk("D1", 1, [64, 8, 64], F32); D2 = mk("D2", 1, [64, 8, 64], F32); D3 = mk("D3", 1, [64, 8, 64], F32)
    aqk = mk("aqk", 2, [64, 8, 64], BF16)
    Npw = mk("N", 2, [64, 8, 64], DP); Mpw = mk("M", 2, [64, 8, 64], DP)
    QR = F32R if os.environ.get("MK_QR", "0") == "1" else F32
    Qf = mk("Qf", 1, [64, 8, 64], F32); Qb = mk("Qb", 2, [64, 8, 64], QR)
    Mr = mk("Mr", 2, [64, 8, 64], QR)
    vbt = mk("vb", 1, [64, 8, 64], QR); kbg = mk("kbg", 1, [64, 8, 64], QR); kend = mk("kend", 2, [64, 8, 64], BF16)
    egb = mk("eg", 1, [64, 8, 64], F32); qg = mk("qg", 2, [64, 8, 64], BF16)
    wval = mk("wval", 2, [64, 8, 64], F32); kcT = mk("kcT", 2, [64, 8, 64], F32); vnew = mk("vnew", 1, [64, 8, 64], BF16)
    dl = mk("dl", 2, [64, 8], F32)
    Sst = [P.sb(f"ds_S{d}", [64, 8, 64], F32) for d in range(2)]
    Sbf = [P.sb(f"ds_Sbf{d}", [64, 8, 64], BF16) for d in range(2)]
    Sr = [P.sb(f"ds_Sr{d}", [64, 8, 64], F32) for d in range(2)]
    Stmp = [P.sb(f"ds_St{d}", [64, 8, 64], F32) for d in range(2)]
    evc = [0]

    def evac(dst, dkey, src, skey):
        evc[0] += 1
        if evc[0] % 2:
            P.op("act", "activation", r=[skey], w=[dkey], out=dst, in_=src, func=AF.Identity)
        else:
            P.op("dve", "tensor_copy", r=[skey], w=[dkey], out=dst, in_=src)

    def v3(ps):
        return ps[0:64, :].rearrange("p (h c) -> p h c", c=64)

    def mm8(ps, psk, lhs_fn, lkeys, rhs_fn, rkeys):
        for h in range(8):
            P.op("pe", "matmul", r=list(lkeys) + list(rkeys), w=[psk], out=ps[0:64, h * 64:(h + 1) * 64], lhsT=lhs_fn(h), rhs=rhs_fn(h), start=True, stop=True)

    cur = [None, None]
    curo = [None, None]

    def prep(d, t0, nch, c):
        blk = c // CPB; cc = c % CPB
        first_in_blk = (cc == 0) if d == 0 else (cc == CPB - 1)
        tb = t0 + blk * BL
        if first_in_blk:
            kn, knk = knTb[d].next(); qn, qnk = qnTb[d].next(); kt, ktk = kntok[d].next(); vt, vtk = vtok[d].next()
            sc, sck = dscb[d].next(); gx, gxk = gexb[d].next(); hx, hxk = hexb[d].next()
            for h in range(8):
                P.dma(kn[:, h, :], S["knT"][h * 64:(h + 1) * 64, tb:tb + BL], r=["knT"], w=[knk])
                P.dma(qn[:, h, :], S["qnT"][h * 64:(h + 1) * 64, tb:tb + BL], r=["qnT"], w=[qnk])
            P.dma(kt[:, :, :], S["kn_tok"][tb:tb + BL, :].rearrange("(c p) f -> p c f", p=64), r=["kn_tok"], w=[ktk])
            P.dma(vt[:, :, :], S["v_tok"][tb:tb + BL, :].rearrange("(c p) f -> p c f", p=64), r=["v_tok"], w=[vtk])
            P.dma(sc[:, :, :], S["dsc"][d, tb:tb + BL, :].rearrange("(c p) f -> p c f", p=64), r=["dsc"], w=[sck])
            for k8 in range(8):
                P.dma(gx[:, k8, :], S["Gexp"][d, :, k8, tb:tb + BL], r=["Gexp"], w=[gxk])
                P.dma(hx[:, k8, :], S["Hexp"][d, :, k8, tb:tb + BL], r=["Hexp"], w=[hxk])
            cur[d] = (kn, knk, qn, qnk, kt, ktk, vt, vtk, sc, sck, gx, gxk, hx, hxk)
        kn, knk, qn, qnk, kt, ktk, vt, vtk, sc, sck, gx, gxk, hx, hxk = cur[d]
        cs = slice(cc * 64, (cc + 1) * 64)
        Gs = sc[:, cc, 0:8]; Hs = sc[:, cc, 8:16]; Bs = sc[:, cc, 16:24]; BEs = sc[:, cc, 24:32]; ELs = sc[:, cc, 32:40]
        m_incl, m_strict, m_strictT = (0, 1, 3) if d == 0 else (2, 3, 1)
        lastc = 63 if d == 0 else 0
        pG, pGk = PSP[d].next()
        P.op("pe", "matmul", r=["ds_ones8", gxk], w=[pGk], out=pG[0:64, :], lhsT=ones8[:, :], rhs=gx[:, :, cs], start=True, stop=True)
        pH, pHk = PSP[d].next()
        P.op("pe", "matmul", r=["ds_ones8", hxk], w=[pHk], out=pH[0:64, :], lhsT=ones8[:, :], rhs=hx[:, :, cs], start=True, stop=True)
        pKK, pKKk = PSP[d].next()
        mm8(pKK, pKKk, lambda h: kn[:, h, cs], [knk], lambda h: kn[:, h, cs], [])
        d1, d1k = D1[d].next(); d2, d2k = D2[d].next(); d3, d3k = D3[d].next()
        Gs_bc = _bc(Gs.unsqueeze(2), [64, 8, 64]); Hs_bc = _bc(Hs.unsqueeze(2), [64, 8, 64])
        eg, egk = egb[d].next(); qgt, qgk = qg[d].next(); dlt, dlk = dl[d].next()
        P.op("dve", "tensor_tensor", r=[pGk, sck], w=[d1k], out=d1[:, :, :], in0=v3(pG), in1=Gs_bc, op=ALU.subtract)
        P.op("dve", "scalar_tensor_tensor", r=[pGk, sck], w=[d3k], out=d3[:, :, :], in0=v3(pG), scalar=-1.0, in1=Hs_bc, op0=ALU.mult, op1=ALU.add)
        P.op("act", "activation", r=[pGk], w=[egk], out=eg[:, :, :], in_=v3(pG), func=AF.Exp)
        P.op("dve", "tensor_tensor", r=[pHk, sck], w=[d2k], out=d2[:, :, :], in0=v3(pH), in1=Gs_bc, op=ALU.subtract)
        pQK, pQKk = PSP[d].next()
        mm8(pQK, pQKk, lambda h: kn[:, h, cs], [knk], lambda h: qn[:, h, cs], [qnk])
        P.op("dve", "tensor_tensor", r=[qnk, egk], w=[qgk], out=qgt[:, :, :], in0=qn[:, :, cs], in1=eg[:, :, :], op=ALU.mult)
        P.op("act", "activation", r=[egk], w=[dlk], out=dlt[:, :], in_=eg[:, :, lastc], func=AF.Identity)
        P.op("dve", "tensor_tensor", r=[d1k, "ds_dm"], w=[d1k], out=d1[:, :, :], in0=d1[:, :, :], in1=_bc(dm[:, m_incl:m_incl + 1, :], [64, 8, 64]), op=ALU.add)
        P.op("act", "activation", r=[d1k], w=[d1k], out=d1[:, :, :], in_=d1[:, :, :], func=AF.Exp)
        aq, aqk_ = aqk[d].next()
        P.op("dve", "tensor_tensor", r=[pQKk, d1k], w=[aqk_], out=aq[:, :, :], in0=v3(pQK), in1=d1[:, :, :], op=ALU.mult)
        P.op("dve", "tensor_tensor", r=[d2k, "ds_dm"], w=[d2k], out=d2[:, :, :], in0=d2[:, :, :], in1=_bc(dm[:, m_strict:m_strict + 1, :], [64, 8, 64]), op=ALU.add)
        P.op("act", "activation", r=[d2k], w=[d2k], out=d2[:, :, :], in_=d2[:, :, :], func=AF.Exp)
        N1, N1k = Npw[d].next()
        P.op("dve", "scalar_tensor_tensor", r=[pKKk, d2k], w=[N1k], out=N1[:, :, :], in0=v3(pKK), scalar=-1.0, in1=d2[:, :, :], op0=ALU.mult, op1=ALU.mult)
        P.op("dve", "tensor_tensor", r=[d3k, "ds_dm"], w=[d3k], out=d3[:, :, :], in0=d3[:, :, :], in1=_bc(dm[:, m_strictT:m_strictT + 1, :], [64, 8, 64]), op=ALU.add)
        P.op("act", "activation", r=[d3k], w=[d3k], out=d3[:, :, :], in_=d3[:, :, :], func=AF.Exp)
        M1, M1k = Mpw[d].next()
        P.op("dve", "scalar_tensor_tensor", r=[pKKk, d3k], w=[M1k], out=M1[:, :, :], in0=v3(pKK), scalar=-1.0, in1=d3[:, :, :], op0=ALU.mult, op1=ALU.mult)
        vb_, vbk = vbt[d].next(); kb_, kbk = kbg[d].next(); ke_, kek = kend[d].next()
        P.op("dve", "tensor_tensor", r=[vtk, sck], w=[vbk], out=vb_[:, :, :], in0=vt[:, cc, :].rearrange("p (h v) -> p h v", v=64), in1=_bc(Bs.unsqueeze(2), [64, 8, 64]), op=ALU.mult)
        P.op("dve", "tensor_tensor", r=[ktk, sck], w=[kbk], out=kb_[:, :, :], in0=kt[:, cc, :].rearrange("p (h v) -> p h v", v=64), in1=_bc(BEs.unsqueeze(2), [64, 8, 64]), op=ALU.mult)
        P.op("dve", "tensor_tensor", r=[ktk, sck], w=[kek], out=ke_[:, :, :], in0=kt[:, cc, :].rearrange("p (h v) -> p h v", v=64), in1=_bc(ELs.unsqueeze(2), [64, 8, 64]), op=ALU.mult)
        qf, qfk = Qf[d].next(); qb, qbk = Qb[d].next()
        P.op("dve", "tensor_tensor", r=[N1k, "ds_id64"], w=[qfk], out=qf[:, :, :], in0=N1[:, :, :], in1=_bc(id64[:, :].unsqueeze(1), [64, 8, 64]), op=ALU.add)
        P.op("act", "activation", r=[qfk], w=[qbk], out=qb[:, :, :], in_=qf[:, :, :], func=AF.Identity)
        Nc, Nck, Mc, Mck = N1, N1k, M1, M1k
        for lev in range(5):
            Mn, Mnk = Mpw[d].next()
            pM, pMk = PSP[d].next()
            mm8(pM, pMk, lambda h: Nc[:, h, :], [Nck], lambda h: Mc[:, h, :], [Mck])
            if lev < 4:
                Nn, Nnk = Npw[d].next()
                pN, pNk = PSP[d].next()
                mm8(pN, pNk, lambda h: Mc[:, h, :], [Mck], lambda h: Nc[:, h, :], [Nck])
                evac(Nn[:, :, :], Nnk, v3(pN), pNk)
            evac(Mn[:, :, :], Mnk, v3(pM), pMk)
            if QR == F32R:
                mr, mrk = Mr[d].next()
                evac(mr[:, :, :], mrk, v3(pM), pMk)
            else:
                mr, mrk = Mn, Mnk
            pQ, pQk = PSP[d].next()
            mm8(pQ, pQk, lambda h: mr[:, h, :], [mrk], lambda h: qb[:, h, :], [qbk])
            P.op("dve", "tensor_tensor", r=[pQk, qfk], w=[qfk], out=qf[:, :, :], in0=qf[:, :, :], in1=v3(pQ), op=ALU.add)
            qb, qbk = Qb[d].next()
            P.op("act", "activation", r=[qfk], w=[qbk], out=qb[:, :, :], in_=qf[:, :, :], func=AF.Identity)
            Mc, Mck = Mn, Mnk
            if lev < 4:
                Nc, Nck = Nn, Nnk
        pW, pWk = PSP[d].next()
        mm8(pW, pWk, lambda h: qb[:, h, :], [qbk], lambda h: vb_[:, h, :], [vbk])
        wv, wvk = wval[d].next()
        evac(wv[:, :, :], wvk, v3(pW), pWk)
        pK, pKk = PSP[d].next()
        mm8(pK, pKk, lambda h: kb_[:, h, :], [kbk], lambda h: qb[:, h, :], [qbk])
        kc, kck = kcT[d].next()
        evac(kc[:, :, :], kck, v3(pK), pKk)
        return dict(wv=wv, wvk=wvk, kc=kc, kck=kck, qgt=qgt, qgk=qgk, aq=aq, aqk=aqk_, ke=ke_, kek=kek, dlt=dlt, dlk=dlk)

    def rec(d, t0, nch, c, H):
        blk = c // CPB; cc = c % CPB
        first_in_blk = (cc == 0) if d == 0 else (cc == CPB - 1)
        last_in_blk = (cc == CPB - 1) if d == 0 else (cc == 0)
        tb = t0 + blk * BL
        if first_in_blk:
            curo[d] = ost[d].next()
        oo, ook = curo[d]
        Sk = f"ds_S{d}"; Sbk = f"ds_Sbf{d}"; Srk = f"ds_Sr{d}"
        pV, pVk = PSR[d].next()
        mm8(pV, pVk, lambda h: H["kc"][:, h, :], [H["kck"]], lambda h: Sr[d][:, h, :], [Srk])
        vn, vnk = vnew[d].next()
        P.op("dve", "tensor_tensor", r=[H["wvk"], pVk], w=[vnk], out=vn[:, :, :], in0=H["wv"][:, :, :], in1=v3(pV), op=ALU.subtract)
        pO, pOk = PSR[d].next()
        for h in range(8):
            P.op("pe", "matmul", r=[H["qgk"], Sbk], w=[pOk], out=pO[0:64, h * 64:(h + 1) * 64], lhsT=H["qgt"][:, h, :], rhs=Sbf[d][:, h, :], start=True, stop=False)
            P.op("pe", "matmul", r=[H["aqk"], vnk], w=[pOk], out=pO[0:64, h * 64:(h + 1) * 64], lhsT=H["aq"][:, h, :], rhs=vn[:, h, :], start=False, stop=True)
        P.op("act", "activation", r=[pOk], w=[ook], out=oo[:, cc, :], in_=pO[0:64, :], func=AF.Identity)
        pS, pSk = PSR[d].next()
        mm8(pS, pSk, lambda h: H["ke"][:, h, :], [H["kek"]], lambda h: vn[:, h, :], [vnk])
        P.op("dve", "tensor_tensor", r=[Sk, H["dlk"]], w=[f"ds_St{d}"], out=Stmp[d][:, :, :], in0=Sst[d][:, :, :], in1=_bc(H["dlt"][:, :].unsqueeze(2), [64, 8, 64]), op=ALU.mult)
        P.op("dve", "tensor_tensor", r=[f"ds_St{d}", pSk], w=[Sk], out=Sst[d][:, :, :], in0=Stmp[d][:, :, :], in1=v3(pS), op=ALU.add)
        P.op("act", "activation", r=[Sk], w=[Sbk], out=Sbf[d][:, :, :], in_=Sst[d][:, :, :], func=AF.Identity)
        P.op("act", "activation", r=[Sk], w=[Srk], out=Sr[d][:, :, :], in_=Sst[d][:, :, :], func=AF.Identity)
        if last_in_blk:
            P.dma(S["ofb"][d, tb:tb + BL, :].rearrange("(c p) f -> p c f", p=64), oo[:, :, :], r=[ook], w=[f"ofb{d}"])

    for sqi, (t0, tl) in enumerate(SEQS):
        nch = tl // 64
        for d in range(2):
            if sqi == 0:
                for h in range(8):
                    P.dma(Sst[d][:, h, :], I["sd"][j, d, h, :, :], w=[f"ds_S{d}"])
            else:
                P.op("pool", "memset", w=[f"ds_S{d}"], ap=Sst[d][:, :, :], constant=0.0)
            P.op("act", "activation", r=[f"ds_S{d}"], w=[f"ds_Sbf{d}"], out=Sbf[d][:, :, :], in_=Sst[d][:, :, :], func=AF.Identity)
            P.op("pool", "tensor_copy", r=[f"ds_S{d}"], w=[f"ds_Sr{d}"], out=Sr[d][:, :, :], in_=Sst[d][:, :, :])
        Hprev = [None, None]
        chunk_of = lambda step, d: step if d == 0 else nch - 1 - step
        for step in range(nch + 1):
            lists = []
            Hnew = [None, None]
            for d in range(2):
                if step < nch:
                    ops, Hnew[d] = capture(P, lambda d=d: prep(d, t0, nch, chunk_of(step, d)))
                    lists.append(ops)
                if step > 0:
                    ops, _ = capture(P, lambda d=d: rec(d, t0, nch, chunk_of(step - 1, d), Hprev[d]))
                    lists.append(ops)
            P.ops.extend(zipper(lists))
            Hprev = Hnew
        if sqi > 0:
            for d in range(2):
                for h in range(8):
                    P.dma(O["nsd"][sqi - 1, j, d, h, :, :], Sst[d][:, h, :], r=[f"ds_S{d}"], w=["o_nsd"], isout=True)
    P.end()
```

```python
import contextlib
import os
import numpy as np
import ml_dtypes
import concourse.bass as bass
import concourse.mybir as mybir
from concourse.bass_utils import run_bass_kernel_spmd

F32 = mybir.dt.float32
BF16 = mybir.dt.bfloat16
AF = mybir.ActivationFunctionType
ALU = mybir.AluOpType
AX = mybir.AxisListType

NCORES = 8
D = 1024
TS = 4096
TP = 256
T = TS + 2 * TP
NT = T // 128
DEPTH = 4
DFF = 2816
EV_IN = 3616
OD_IN = 2848
EPS = 1e-6
SEQS = [(0, TS), (TS, TP), (TS + TP, TP)]

ENGS = ("pe", "act", "dve", "pool", "sp")
NDMA = 24


class Op:
    __slots__ = ("eng", "fn", "r", "w", "dma", "id", "waits", "sig", "dslot", "dval", "prev_dma")


class Prog:
    def __init__(self, nc):
        self.nc = nc
        self.es = contextlib.ExitStack()
        self.ops = []
        self.nid = 0
        self.engobj = {"pe": nc.tensor, "act": nc.scalar, "dve": nc.vector, "pool": nc.gpsimd, "sp": nc.sync}
        self.sem = {e: self.es.enter_context(nc.semaphore("sem_" + e)) for e in ENGS}
        self.dsem = {e: [self.es.enter_context(nc.semaphore(f"dsem_{e}_{i}")) for i in range(NDMA)]
                     for e in ("sp", "act", "pool")}
        self.sigcnt = {e: 0 for e in ENGS}
        self.dcnt = {e: 0 for e in ("sp", "act", "pool")}
        self.dhist = {e: [] for e in ("sp", "act", "pool")}
        self.last_w = {}
        self.readers = {}
        self.seen = {e: {} for e in ENGS}
        self.out_dmas = []
        self.phase_allocs = None
        self.psum_keys = set(["ps_t", "ps_fm", "ps_row", "mh_ps0", "mh_ps1", "lfm_pst", "l2psT0", "l2psT1", "ae_psm", "ao_psm",
                              "at_pso0", "at_pso1", "at_pso2", "at_pso3", "gs_psatt0", "gs_psatt1", "gs_pso0", "gs_pso1", "gs_pss0", "gs_pss1"])

    def begin(self, name=None):
        import inspect
        self.phase_name = name or inspect.stack()[1].function
        self.phase_allocs = contextlib.ExitStack()

    def uname(self, name):
        self.uid = getattr(self, "uid", 0) + 1
        return f"{name}_u{self.uid}"

    def sb(self, name, shape, dt):
        return self.phase_allocs.enter_context(self.nc.sbuf_tensor(self.uname(name), list(shape), dt))

    def ps(self, name, shape, dt=F32):
        return self.phase_allocs.enter_context(self.nc.psum_tensor(self.uname(name), list(shape), dt))

    def sub_begin(self):
        self._outer = self.phase_allocs
        self.phase_allocs = contextlib.ExitStack()

    def sub_end(self):
        self.flush(fence=True)
        self.phase_allocs.close()
        self.phase_allocs = self._outer

    def end(self):
        self.flush(fence=True)
        self.phase_allocs.close()
        self.phase_allocs = None

    def op(self, eng, method, r=(), w=(), dma=False, isout=False, **kw):
        o = Op()
        o.eng = eng; o.fn = (method, kw); o.r = tuple(r); o.w = tuple(w); o.dma = dma
        o.id = self.nid; self.nid += 1
        o.waits = None; o.sig = 0; o.dslot = None; o.dval = 0; o.prev_dma = None
        self.ops.append(o)
        if isout:
            self.out_dmas.append(o)
        return o

    def dma(self, out, in_, r=(), w=(), q="sp", isout=False, **kw):
        return self._dma(q, out, in_, r, w, isout, kw)

    def _dma(self, q, out, in_, r, w, isout, kw):
        o = Op()
        kk = dict(kw); kk["out"] = out; kk["in_"] = in_
        o.eng = q; o.fn = ("dma_start", kk); o.r = tuple(r); o.w = tuple(w); o.dma = True
        o.id = self.nid; self.nid += 1
        o.waits = None; o.sig = 0; o.dslot = None; o.dval = 0; o.prev_dma = None
        self.ops.append(o)
        if isout:
            self.out_dmas.append(o)
        return o

    def flush(self, fence=False, final=False):
        ops = self.ops
        self.ops = []
        for o in ops:
            deps = {}
            def add(p):
                if p is None or p is o:
                    return
                deps[p.id] = p
            for k in o.r:
                add(self.last_w.get(k))
                if k in self.psum_keys:
                    rd = self.readers.get(k)
                    if rd:
                        for kk, p in rd.items():
                            if kk != "dma" and kk != o.eng:
                                add(p)
            for k in o.w:
                add(self.last_w.get(k))
                rd = self.readers.get(k)
                if rd:
                    for kk, p in rd.items():
                        if kk == "dma":
                            for pp in p:
                                add(pp)
                        else:
                            add(p)
            if o.dma:
                q = o.eng
                n = self.dcnt[q]
                self.dcnt[q] += 1
                o.dslot = n % NDMA
                o.dval = 16 * (n // NDMA + 1)
                if n >= NDMA:
                    add(self.dhist[q][n - NDMA])
                self.dhist[q].append(o)
            o.waits = list(deps.values())
            for p in o.waits:
                if not p.dma:
                    if not (p.eng == "pe" and o.eng == "pe" and not o.dma):
                        p.sig = -1
            for k in o.w:
                self.last_w[k] = o
                self.readers[k] = {}
            for k in o.r:
                rd = self.readers.setdefault(k, {})
                if o.dma:
                    rd.setdefault("dma", []).append(o)
                else:
                    rd[o.eng] = o
        lastop = {}
        for o in ops:
            lastop[o.eng] = o
        if fence or final:
            for e, o in lastop.items():
                if not o.dma:
                    o.sig = -1
        for o in ops:
            if o.dma:
                continue
            if o.sig == -1:
                self.sigcnt[o.eng] += 1
                o.sig = self.sigcnt[o.eng]
        per_eng = {e: [] for e in ENGS}
        for o in ops:
            per_eng[o.eng].append(o)
        nc = self.nc
        fence_dmas = [(q, i) for q in self.dcnt for i in range(min(NDMA, self.dcnt[q]))]

        def emit_engine(ename):
            def body(eng):
                seen = self.seen[ename]
                for o in per_eng[ename]:
                    need = {}
                    for p in o.waits:
                        if p.dma:
                            key = ("d", p.eng, p.dslot)
                            if seen.get(key, 0) < p.dval:
                                seen[key] = p.dval
                                eng.wait_ge(self.dsem[p.eng][p.dslot], p.dval)
                        else:
                            if p.eng == "pe" and ename == "pe" and not o.dma:
                                continue
                            if p.sig > need.get(p.eng, 0):
                                need[p.eng] = p.sig
                    for pe_, v in need.items():
                        if seen.get(pe_, 0) < v:
                            seen[pe_] = v
                            eng.wait_ge(self.sem[pe_], v)
                    ins = getattr(eng, o.fn[0])(**o.fn[1])
                    if o.dma:
                        ins.then_inc(self.dsem[o.eng][o.dslot], 16)
                    elif o.sig > 0:
                        ins.then_inc(self.sem[o.eng], 1)
                if fence or final:
                    for e2 in ENGS:
                        v = self.sigcnt[e2]
                        if v > 0 and seen.get(e2, 0) < v:
                            seen[e2] = v
                            eng.wait_ge(self.sem[e2], v)
                    for q in self.dcnt:
                        n = self.dcnt[q]
                        for i in range(min(NDMA, n)):
                            last = 16 * ((n - 1 - i) // NDMA + 1)
                            key = ("d", q, i)
                            if seen.get(key, 0) < last:
                                seen[key] = last
                                eng.wait_ge(self.dsem[q][i], last)
            return body

        self.scope_i = getattr(self, "scope_i", 0) + 1
        with nc.named_scope(f"{getattr(self, 'phase_name', 'x')}_{self.scope_i}"), nc.Block() as block:
            block.tensor(emit_engine("pe"))
            block.scalar(emit_engine("act"))
            block.vector(emit_engine("dve"))
            block.gpsimd(emit_engine("pool"))
            block.sync(emit_engine("sp"))
        if fence or final:
            self.last_w = {}
            self.readers = {}


def _bc(ap, shape):
    return ap.broadcast_to(list(shape))


class Ctx:
    pass


def build_program(stage=99, dbg=None):
    nc = bass.Bass("TRN2", target_bir_lowering=False)
    P = Prog(nc)
    C = Ctx()
    C.nc = nc; C.P = P; C.stage = stage

    def din(name, shape, dt=F32):
        return nc.dram_tensor(name, list(shape), dt, kind="ExternalInput").ap()

    def dout(name, shape, dt=F32):
        return nc.dram_tensor(name, list(shape), dt, kind="ExternalOutput").ap()

    def dscr(name, shape, dt=F32):
        return nc.dram_tensor(name, list(shape), dt).ap()

    I = {}
    I["x"] = din("x", [T, D])
    I["cond"] = din("cond", [3, D])
    I["cak"] = din("cak", [2, 8, 256, 64]); I["cav"] = din("cav", [2, 8, 256, 64])
    I["sb"] = din("sb", [2, 2, 8, 64, 64])
    I["cck"] = din("cck", [2, 2, 256, 64]); I["ccv"] = din("ccv", [2, 2, 256, 64])
    I["sd"] = din("sd", [2, 2, 8, 64, 64])
    I["ada_w"] = din("ada_w", [DEPTH, D, 6 * D]); I["ada_b"] = din("ada_b", [DEPTH, 6 * D])
    I["norm1_g"] = din("norm1_g", [DEPTH, D]); I["norm2_g"] = din("norm2_g", [DEPTH, D])
    I["ffn_up"] = din("ffn_up", [DEPTH, D, 2 * DFF]); I["ffn_conv"] = din("ffn_conv", [DEPTH, 3, 2 * DFF])
    I["ffn_down"] = din("ffn_down", [DEPTH, DFF, D])
    I["ev_w_in"] = din("ev_w_in", [2, D, EV_IN]); I["ev_w_out"] = din("ev_w_out", [2, D, D])
    I["a_rpb"] = din("a_rpb", [2, 8, 15, 31]); I["b_w_g2"] = din("b_w_g2", [2, 2, 16, 512])
    I["b_b_g"] = din("b_b_g", [2, 2, 512]); I["b_norm_g"] = din("b_norm_g", [2, 512])
    I["od_w_in"] = din("od_w_in", [2, D, OD_IN]); I["od_w_out"] = din("od_w_out", [2, D, D])
    I["c_sink"] = din("c_sink", [2, 8]); I["d_conv"] = din("d_conv", [2, 3, 1536])
    I["d_a_log"] = din("d_a_log", [2, 2, 8]); I["d_dt_bias"] = din("d_dt_bias", [2, 2, 8])
    I["d_norm_g"] = din("d_norm_g", [2, 64]); I["final_g"] = din("final_g", [D])
    I["c_ident"] = din("c_ident", [128, 128])
    I["c_maskf"] = din("c_maskf", [128, 512])
    I["c_eoh"] = din("c_eoh", [32, 64, 64])
    I["c_tri"] = din("c_tri", [2, 128, 128])
    I["c_perm"] = din("c_perm", [128, 128])
    I["c_bd"] = din("c_bd", [128, 128])
    I["c_cos"] = din("c_cos", [128, TS])
    I["c_sin"] = din("c_sin", [128, TS])
    I["c_dmask"] = din("c_dmask", [4, 64, 64])
    C.I = I
    O = {}
    O["y"] = dout("y", [T, D])
    O["nak"] = dout("nak", [2, 2, 8, 256, 64]); O["nav"] = dout("nav", [2, 2, 8, 256, 64])
    O["nsb"] = dout("nsb", [2, 2, 2, 8, 64, 64])
    O["nck"] = dout("nck", [2, 2, 2, 256, 64]); O["ncv"] = dout("ncv", [2, 2, 2, 256, 64])
    O["nsd"] = dout("nsd", [2, 2, 2, 8, 64, 64])
    C.O = O
    C.taps = {}
    if dbg:
        for nm, shp, dt in dbg:
            O[nm] = dout(nm, shp, dt)
            C.taps[nm] = O[nm]

    def tap(name, src, keys, dst_view=None):
        if name in C.taps:
            dst = C.taps[name] if dst_view is None else dst_view(C.taps[name])
            P.dma(dst, src, r=keys, w=["o_" + name], isout=True)
    C.tap = tap

    def dump(dst, src, key):
        names = " ".join("abcdef"[:len(src.shape)])
        fs = src.rearrange(f"{names} -> ({names})")
        fd = dst.rearrange(f"{names} -> ({names})")
        n = fs.shape[0]
        per = 128 * 8192
        o = 0
        while o < n:
            m = min(per, n - o)
            assert m % 128 == 0
            P.dma(fd[o:o + m].rearrange("(p f) -> p f", p=128), fs[o:o + m].rearrange("(p f) -> p f", p=128), r=[key], w=["o_dbg_" + key], isout=True)
            o += m
    C.dump = dump
    S = {}
    S["gates"] = dscr("s_gates", [DEPTH, 3, 2, D])
    S["xres"] = dscr("s_xres", [T, D])
    S["qkA"] = dscr("s_qkA", [1024, T], BF16)
    S["vA"] = dscr("s_vA", [T, 8 * 65], BF16)
    S["vB"] = dscr("s_vB", [T, 512], BF16)
    S["gate"] = dscr("s_gate", [T, 512])
    S["qtB"] = dscr("s_qtB", [2, 512, T], BF16)
    S["ktB"] = dscr("s_ktB", [2, 512, T], BF16)
    S["kendB"] = dscr("s_kendB", [2, T, 512], BF16)
    S["decB"] = dscr("s_decB", [2, 512, 72])
    S["oT"] = dscr("s_oT", [1024, T], BF16)
    S["ofb"] = dscr("s_ofb", [2, T, 512])
    S["actT"] = dscr("s_actT", [DFF, T], BF16)
    S["qkC"] = dscr("s_qkC", [640, T], BF16)
    S["vC"] = dscr("s_vC", [T, 2 * 65], BF16)
    S["qnT"] = dscr("s_qnT", [512, T], BF16)
    S["knT"] = dscr("s_knT", [512, T], BF16)
    S["kn_tok"] = dscr("s_kn_tok", [T, 512])
    S["v_tok"] = dscr("s_v_tok", [T, 512])
    S["Gexp"] = dscr("s_Gexp", [2, 8, 8, T])
    S["Hexp"] = dscr("s_Hexp", [2, 8, 8, T])
    S["dsc"] = dscr("s_dsc", [2, T, 40])
    S["dng"] = dscr("s_dng", [2, 512])
    C.S = S

    def gsb(name, shape, dt):
        return P.es.enter_context(nc.sbuf_tensor(name, list(shape), dt))
    C.ident = gsb("ident", [128, 128], F32)
    C.identb = gsb("identb", [128, 128], BF16)
    C.modfm = gsb("modfm", [128, DEPTH, 3, 4, 8], F32)
    C.gsb = gsb

    def scoped(name, shape, dt):
        st = contextlib.ExitStack()
        t = st.enter_context(nc.sbuf_tensor(P.uname(name), list(shape), dt))
        return t, st
    C.scoped = scoped
    C.epsb = gsb("epsb", [128, 4], F32)

    phase0(C)
    nlayers = int(os.environ.get("MK_LAYERS", DEPTH))
    xsrc, xkey = I["x"], "x_in"
    for l in range(nlayers):
        j = l // 2
        C.hT, hst = scoped("hT", [128, 8, T], BF16)
        make_hT(C, l, 0, xsrc, xkey)
        if l % 2 == 0:
            l2_even(C, j)
        else:
            l2_odd(C, j)
        hst.close()
        if l % 2 == 0:
            C.Mp, mst = scoped("Mp", [128, 8, 19, 64], F32)
            build_mp(C, j)
            attn_even(C, j)
            mst.close()
            gla_scan(C, j)
            scan_finalize(C, I["b_norm_g"][j:j + 1, :], "gla")
            wout = I["ev_w_out"][j]
        else:
            attn_odd(C, j)
            delta_scan(C, j)
            scan_finalize(C, S["dng"][j:j + 1, :], "dl")
            wout = I["od_w_out"][j]
        if dbg and stage == 10 + l:
            P.begin()
            C.dump(C.taps["dbg_oT"], S["oT"], "oT")
            P.end()
        proj_residual(C, l, 0, wout, 8, S["oT"], "oT", xsrc, xkey)
        xsrc, xkey = S["xres"], "xres_w"
        if dbg and stage == 20 + l:
            P.begin()
            C.dump(C.taps["dbg_x"], S["xres"], "xres_w")
            P.end()
        stop = os.environ.get("MK_STOP", "")
        if stop == "l4":
            break
        C.hT, hst = scoped("hT", [128, 8, T], BF16)
        make_hT(C, l, 1, xsrc, xkey)
        if stop != "hT2":
            ffn_up(C, l)
        hst.close()
        if stop in ("hT2", "ffn_up"):
            break
        proj_residual(C, l, 1, I["ffn_down"][l], 22, S["actT"], "actT", xsrc, xkey)
        if dbg and stage == 30 + l:
            P.begin()
            C.dump(C.taps["dbg_x"], S["xres"], "xres_w")
            P.end()
    if not os.environ.get("MK_STOP"):
        final_norm(C)
    P.flush(final=True)
    P.es.close()
    return nc


def phase0(C):
    P, nc, I, S = C.P, C.nc, C.I, C.S
    P.begin()
    rows = P.sb("p0_rows", [64, 128], F32)
    rows2 = P.sb("p0_rows2", [24, 128], F32)
    vecfm = P.sb("p0_vecfm", [128, 64], F32)
    scT = P.sb("p0_scT", [128, 3, 8], F32)
    wts = [P.sb(f"p0_wt{i}", [128, 8, 1024], F32) for i in range(2)]
    brow = P.sb("p0_brow", [3, 1024], F32)
    grow = P.sb("p0_grow", [3, 1024], F32)
    tmp = P.sb("p0_tmp", [128, 8], F32)
    ps_t = P.ps("p0_pst", [128, 128], F32)
    ps_fm = P.ps("p0_psfm", [128, 4, 8, 3], F32)
    ps_row = P.ps("p0_psrow", [3, 1024], F32)

    P.dma(C.ident[:, :], I["c_ident"][:, :], w=["ident"])
    P.op("dve", "memset", w=["epsb"], ap=C.epsb[:, 0:1], constant=EPS)
    P.op("dve", "memset", w=["epsb"], ap=C.epsb[:, 1:2], constant=1.0)
    P.op("dve", "memset", w=["epsb"], ap=C.epsb[:, 2:4], constant=0.0)
    P.op("dve", "tensor_copy", r=["ident"], w=["identb"], out=C.identb[:, :], in_=C.ident[:, :])
    P.dma(rows2[:, :], I["cond"].rearrange("s (k p) -> (s k) p", p=128), w=["rows2"])
    P.op("pe", "transpose", r=["rows2", "ident"], w=["ps_t"], out=ps_t[:, 0:24], in_=rows2[:, :], identity=C.ident[0:24, 0:24])
    P.op("act", "activation", r=["ps_t"], w=["scT"], out=scT[:, :, :].rearrange("p s k -> p (s k)"), in_=ps_t[:, 0:24], func=AF.Silu)
    for j_ in range(2):
        for h_ in range(8):
            P.dma(S["dng"][j_:j_ + 1, h_ * 64:(h_ + 1) * 64], I["d_norm_g"][j_:j_ + 1, :], w=["dng"])
    wcnt = 0
    for l in range(DEPTH):
        P.dma(rows[0:48, :], I["ada_b"][l].rearrange("(r p) -> r p", p=128), w=["rows"])
        P.dma(rows[48:56, :], I["norm1_g"][l].rearrange("(r p) -> r p", p=128), w=["rows"])
        P.dma(rows[56:64, :], I["norm2_g"][l].rearrange("(r p) -> r p", p=128), w=["rows"])
        P.op("pe", "transpose", r=["rows", "ident"], w=["ps_t"], out=ps_t[:, 0:64], in_=rows[:, :], identity=C.ident[0:64, 0:64])
        P.op("dve", "tensor_copy", r=["ps_t"], w=["vecfm"], out=vecfm[:, :], in_=ps_t[:, 0:64])
        for j in range(6):
            wt = wts[wcnt % 2]; wkey = f"p0wt{wcnt % 2}"; wcnt += 1
            for k in range(8):
                P.dma(wt[:, k, :], I["ada_w"][l, k * 128:(k + 1) * 128, j * 1024:(j + 1) * 1024], w=[wkey + f"_{k}"])
            if j in (0, 1, 3, 4):
                jq = (0, 1, None, 2, 3)[j]
                for c in range(8):
                    for k in range(8):
                        P.op("pe", "matmul", r=[wkey + f"_{k}", "scT"], w=["ps_fm"],
                             out=ps_fm[:, jq, c, :], lhsT=wt[:, k, c * 128:(c + 1) * 128], rhs=scT[:, :, k],
                             start=(k == 0), stop=(k == 7))
            else:
                jg = 0 if j == 2 else 1
                P.dma(brow[:, :], I["ada_b"][l:l + 1, j * 1024:(j + 1) * 1024].partition_broadcast(3), w=["brow"])
                for half in range(2):
                    for k in range(8):
                        P.op("pe", "matmul", r=[wkey + f"_{k}", "scT"], w=["ps_row"],
                             out=ps_row[:, half * 512:(half + 1) * 512], lhsT=scT[:, :, k], rhs=wt[:, k, half * 512:(half + 1) * 512],
                             start=(k == 0), stop=(k == 7))
                P.op("dve", "tensor_tensor", r=["ps_row", "brow"], w=["grow"], out=grow[:, :], in0=ps_row[:, :], in1=brow[:, :], op=ALU.add)
                P.dma(S["gates"][l, :, jg, :], grow[:, :], r=["grow"], w=[f"gates{l}"])
        for s in range(3):
            for half in range(2):
                q_sh, q_sc = (0, 1) if half == 0 else (2, 3)
                j_sh, j_sc = (0, 1) if half == 0 else (3, 4)
                gcol = 48 if half == 0 else 56
                P.op("dve", "tensor_tensor", r=["ps_fm", "vecfm"], w=["modfm"],
                     out=C.modfm[:, l, s, 2 * half + 1, :], in0=ps_fm[:, q_sh, :, s], in1=vecfm[:, j_sh * 8:(j_sh + 1) * 8], op=ALU.add)
                P.op("dve", "scalar_tensor_tensor", r=["ps_fm", "vecfm"], w=["p0tmp"],
                     out=tmp[:, :], in0=ps_fm[:, q_sc, :, s], scalar=1.0, in1=vecfm[:, j_sc * 8:(j_sc + 1) * 8], op0=ALU.add, op1=ALU.add)
                P.op("dve", "tensor_tensor", r=["p0tmp", "vecfm"], w=["modfm"],
                     out=C.modfm[:, l, s, 2 * half, :], in0=tmp[:, :], in1=vecfm[:, gcol:gcol + 8], op=ALU.mult)
    P.end()


def make_hT(C, l, which, xsrc, xkey="xres"):
    P, nc = C.P, C.nc
    P.begin()
    NB = 3
    xts = [P.sb(f"mh_x{i}", [128, D], F32) for i in range(NB)]
    junk = P.sb("mh_junk", [128, D], BF16)
    xns = [P.sb(f"mh_xn{i}", [128, D], BF16) for i in range(NB)]
    sss = [P.sb(f"mh_ss{i}", [128, 4], F32) for i in range(NB)]
    pss = [[P.ps(f"mh_ps{i}_{e}", [128, 4, 128], BF16) for e in range(2)] for i in range(2)]
    P.psum_keys.update([f"mh_ps{i}_{e}" for i in range(2) for e in range(2)])
    for i in range(NT):
        b = i % NB
        seq = 0 if i < 32 else (1 if i < 34 else 2)
        xt, xn, ss = xts[b], xns[b], sss[b]
        ps = pss[i % 2]; pk = [f"mh_ps{i % 2}_0", f"mh_ps{i % 2}_1"]
        P.dma(xt[:, :], xsrc[i * 128:(i + 1) * 128, :], r=[xkey], w=[f"mh_x{b}"])
        P.op("act", "activation", r=[f"mh_x{b}"], w=["mh_junk", f"mh_ss{b}"],
             out=junk[:, :], in_=xt[:, :], func=AF.Square, accum_out=ss[:, 0:1])
        P.op("act", "activation", r=[f"mh_ss{b}"], w=[f"mh_ss{b}"], out=ss[:, 1:2], in_=ss[:, 0:1], func=AF.Ln, scale=1.0 / D, bias=C.epsb[:, 0:1])
        P.op("act", "activation", r=[f"mh_ss{b}"], w=[f"mh_ss{b}"], out=ss[:, 2:3], in_=ss[:, 1:2], func=AF.Exp, scale=-0.5)
        P.op("dve", "tensor_scalar", r=[f"mh_x{b}", f"mh_ss{b}"], w=[f"mh_xn{b}"],
             out=xn[:, :], in0=xt[:, :], scalar1=ss[:, 2:3], scalar2=None, op0=ALU.mult)
        for k in range(8):
            P.op("pe", "transpose", r=[f"mh_xn{b}", "identb"], w=[pk[k % 2]], out=ps[k % 2][:, k // 2, :], in_=xn[:, k * 128:(k + 1) * 128], identity=C.identb[:, :])
        for k in range(8):
            A = C.modfm[:, l, seq, 2 * which, k:k + 1]
            B = C.modfm[:, l, seq, 2 * which + 1, k:k + 1]
            dst = C.hT[:, k, i * 128:(i + 1) * 128]
            if k % 2 == 0:
                P.op("dve", "tensor_scalar", r=[pk[0], "modfm"], w=[f"hT{i}_d"], out=dst, in0=ps[0][:, k // 2, :], scalar1=A, scalar2=B, op0=ALU.mult, op1=ALU.add)
            else:
                P.op("act", "activation", r=[pk[1], "modfm"], w=[f"hT{i}_a"], out=dst, in_=ps[1][:, k // 2, :], func=AF.Identity, scale=A, bias=B)
    P.end()


def host_consts():
    c = {}
    c["c_ident"] = np.eye(128, dtype=np.float32)
    mf = np.ones((128, 512), np.float32); mf[:, 0::64] = 0.0
    c["c_maskf"] = mf
    eoh = np.zeros((32, 64, 64), np.float32)
    for w in range(64):
        cs = min(max(w - 8, 0), 48)
        for cc in range(64):
            if cs <= cc < cs + 16:
                eoh[cc - w + 15, w, cc] = 1.0
            else:
                eoh[31, w, cc] = -30000.0
    c["c_eoh"] = eoh
    perm = np.zeros((128, 128), np.float32)
    for m_ in range(128):
        src = m_ + 16 if (m_ % 32) < 16 else m_ - 16
        perm[src, m_] = 1.0
    c["c_perm"] = perm
    bd = np.zeros((128, 128), np.float32); bd[:64, :64] = 1.0; bd[64:, 64:] = 1.0
    c["c_bd"] = bd
    tpos = np.arange(TS)
    inv = (1.0 / (10000.0 ** (np.arange(16, dtype=np.float32) / 16.0))).astype(np.float32)
    cos_t = np.zeros((128, TS), np.float32); sin_t = np.zeros((128, TS), np.float32)
    for p_ in range(128):
        i_ = p_ % 64
        pos = (tpos // 64) if i_ < 32 else (tpos % 64)
        ang = pos.astype(np.float32) * inv[i_ % 16]
        cos_t[p_] = np.cos(ang)
        sin_t[p_] = np.sin(ang) * (-1.0 if (i_ % 32) < 16 else 1.0)
    c["c_cos"] = cos_t; c["c_sin"] = sin_t
    i64 = np.arange(64)
    NEG = -30000.0
    dm = np.stack([np.where(i64[:, None] <= i64[None, :], 0.0, NEG), np.where(i64[:, None] < i64[None, :], 0.0, NEG),
                   np.where(i64[:, None] >= i64[None, :], 0.0, NEG), np.where(i64[:, None] > i64[None, :], 0.0, NEG)]).astype(np.float32)
    c["c_dmask"] = dm
    ii = np.arange(128)
    c["c_tri"] = np.stack([(ii[:, None] <= ii[None, :]), (ii[:, None] >= ii[None, :])]).astype(np.float32)
    return c


def make_in_maps(inp, cores):
    g = lambda k: np.asarray(inp[k])
    consts = host_consts()
    maps = []
    for c in cores:
        m = {}
        m["x"] = np.ascontiguousarray(np.concatenate([g("x_sample")[c], g("x_prompt")[2 * c], g("x_prompt")[2 * c + 1]], axis=0))
        m["cond"] = np.ascontiguousarray(np.stack([g("c")[c], g("c_ctx"), g("c_ctx")], axis=0))
        m["cak"] = np.ascontiguousarray(g("cache_a_k")[c]); m["cav"] = np.ascontiguousarray(g("cache_a_v")[c])
        m["sb"] = np.ascontiguousarray(g("state_b")[c])
        m["cck"] = np.ascontiguousarray(g("cache_c_k")[c]); m["ccv"] = np.ascontiguousarray(g("cache_c_v")[c])
        m["sd"] = np.ascontiguousarray(g("state_d")[c])
        for k in ("ada_w", "ada_b", "norm1_g", "norm2_g", "ffn_up", "ffn_conv", "ffn_down", "ev_w_in", "ev_w_out",
                  "a_rpb", "b_w_g2", "b_b_g", "b_norm_g", "od_w_in", "od_w_out", "c_sink", "d_conv", "d_a_log",
                  "d_dt_bias", "d_norm_g", "final_g"):
            m[k] = g(k)
        m.update(consts)
        maps.append(m)
    return maps


_NC_CACHE = {}


def kernel(**inputs):
    if "nc" not in _NC_CACHE:
        _NC_CACHE["nc"] = build_program()
    nc = _NC_CACHE["nc"]
    maps = make_in_maps(inputs, list(range(NCORES)))
    res = run_bass_kernel_spmd(nc, maps, core_ids=list(range(NCORES)))
    R = res.results
    y_sample = np.stack([R[c]["y"][:TS] for c in range(NCORES)], 0)
    y_prompt = np.concatenate([R[c]["y"][TS:].reshape(2, TP, D) for c in range(NCORES)], 0)
    cat = lambda k: np.concatenate([R[c][k] for c in range(NCORES)], 0)
    return (y_prompt, y_sample, cat("nak"), cat("nav"), cat("nsb"), cat("nck"), cat("ncv"), cat("nsd"))


def load_w_bf16(C, dst, src, ncols, key):
    P = C.P
    for k in range(src.shape[0] // 128):
        c0 = 0
        while c0 < ncols:
            n = min(2048, ncols - c0)
            P.dma(dst[:, k, c0:c0 + n], src[k * 128:(k + 1) * 128, c0:c0 + n], w=[f"{key}_{k}"], q="pool")
            c0 += n


def load_fm(C, dst, src_rows, n, ps_t, rows_tile, key):
    P = C.P
    P.dma(rows_tile[0:n, :], src_rows, w=["lfm_rows"])
    P.op("pe", "transpose", r=["lfm_rows", "ident"], w=["lfm_pst"], out=ps_t[:, 0:n], in_=rows_tile[0:n, :], identity=C.ident[0:n, 0:n])
    P.op("dve", "tensor_copy", r=["lfm_pst"], w=[key], out=dst, in_=ps_t[:, 0:n])


class Rot:
    def __init__(self, P, name, n, shape, dt, psum=False):
        self.tiles = [(P.ps if psum else P.sb)(f"{name}{i}", shape, dt) for i in range(n)]
        self.keys = [f"{name}{i}" for i in range(n)]
        if psum:
            P.psum_keys.update(self.keys)
        self.i = -1

    def next(self):
        self.i = (self.i + 1) % len(self.tiles)
        return self.tiles[self.i], self.keys[self.i]


def l2_even(C, j):
    P, nc, I, S, O = C.P, C.nc, C.I, C.S, C.O
    P.begin()
    W = P.sb("l2_w", [128, 8, EV_IN], BF16)
    load_w_bf16(C, W, I["ev_w_in"][j], EV_IN, "l2w")
    wkeys = [f"l2w_{k}" for k in range(8)]
    rows = P.sb("l2_rows", [16, 128], F32)
    ps_t = P.ps("l2_pst", [128, 128], F32)
    negb = P.sb("l2_negb", [128, 8], F32)
    load_fm(C, negb[:, :], I["b_b_g"][j].rearrange("d (m p) -> (d m) p", p=128), 8, ps_t, rows, "l2negb")
    P.op("dve", "tensor_scalar", r=["l2negb"], w=["l2negb"], out=negb[:, :], in0=negb[:, :], scalar1=-1.0, scalar2=None, op0=ALU.mult)
    wg2 = P.sb("l2_wg2", [16, 2, 512], F32)
    P.dma(wg2[:, :, :], I["b_w_g2"][j].rearrange("d r c -> r d c"), w=["l2wg2"])
    maskf = P.sb("l2_maskf", [128, 512], F32)
    P.dma(maskf[:, :], I["c_maskf"][:, :], w=["maskf"])

    psq = Rot(P, "l2_psq", 1, [128, 512], F32, psum=True)
    psk = Rot(P, "l2_psk", 1, [128, 512], F32, psum=True)
    psz = Rot(P, "l2_psz", 2, [128, 512], F32, psum=True)
    psg = Rot(P, "l2_psg", 2, [128, 512], F32, psum=True)
    psT = P.ps("l2_psT", [128, 2, 4, 128], BF16)
    tA = Rot(P, "l2_tA", 2, [128, 512], F32)
    tB = Rot(P, "l2_tB", 2, [128, 512], F32)
    tC = Rot(P, "l2_tC", 2, [128, 512], F32)
    tD = Rot(P, "l2_tD", 2, [128, 512], F32)
    tE = Rot(P, "l2_tE", 2, [128, 512], F32)
    sbf = Rot(P, "l2_sbf", 4, [128, 512], BF16)
    sf32 = Rot(P, "l2_sf", 3, [128, 512], F32)
    ketok = Rot(P, "l2_ketok", 2, [128, 4, 128], BF16)
    vst = Rot(P, "l2_vst", 2, [128, 8, 65], BF16)
    for t_, k_ in zip(vst.tiles, vst.keys):
        P.op("pool", "memset", w=[k_], ap=t_[:, :, 64:65], constant=1.0)
    dect = Rot(P, "l2_dec", 2, [128, 8], F32)
    glr_sb = [Rot(P, f"l2_glr{d}_", 2, [16, 512], F32) for d in range(2)]
    ev = [0]

    def evac_engine():
        ev[0] += 1
        return "act" if ev[0] % 2 else "dve"

    def evac(dst, dkey, src, skey):
        e = evac_engine()
        if e == "act":
            P.op("act", "activation", r=[skey], w=[dkey], out=dst, in_=src, func=AF.Identity)
        else:
            P.op("dve", "tensor_copy", r=[skey], w=[dkey], out=dst, in_=src)

    def proj_fm(ps, pkey, c0, M, tt):
        for k in range(8):
            P.op("pe", "matmul", r=[wkeys[k]] + [f"hT{4 * tt + u}" for u in range(4)], w=[pkey],
                 out=ps[0:M, :], lhsT=W[:, k, c0:c0 + M], rhs=C.hT[:, k, tt * 512:(tt + 1) * 512], start=(k == 0), stop=(k == 7))

    def proj_tm(ps, pkey, c0, N, i):
        for k in range(8):
            P.op("pe", "matmul", r=[wkeys[k], f"hT{i}"], w=[pkey],
                 out=ps[:, 0:N], lhsT=C.hT[:, k, i * 128:(i + 1) * 128], rhs=W[:, k, c0:c0 + N], start=(k == 0), stop=(k == 7))

    parts = os.environ.get("L2P", "qk,tm,gla").split(",")
    for tt in range(int(os.environ.get("L2TT", T // 512))):
        tok = slice(tt * 512, (tt + 1) * 512)
        for m in range(8 if "qk" in parts else 0):
            ps, pk = psg.next()
            proj_fm(ps, pk, m * 128, 128, tt)
            st, sk = sbf.next()
            evac(st[:, :], sk, ps[:, :], pk)
            P.dma(S["qkA"][m * 128:(m + 1) * 128, tok], st[:, :], r=[sk], w=["qkA"])
        for u in range(4 if "tm" in parts else 0):
            i = 4 * tt + u
            rowsl = slice(i * 128, (i + 1) * 128)
            ps, pk = psg.next()
            proj_tm(ps, pk, 1024, 512, i)
            st, sk = vst.next()
            if i < 32:
                evac(st[:, :, 0:64], sk, ps[:, :].rearrange("p (h d) -> p h d", d=64), pk)
            else:
                seq = (i - 32) // 2; t0 = ((i - 32) % 2) * 128
                sf, sfk = sf32.next()
                evac(sf[:, :], sfk, ps[:, :], pk)
                P.op("pool", "tensor_copy", r=[sfk], w=[sk], out=st[:, :, 0:64], in_=sf[:, :].rearrange("p (h d) -> p h d", d=64))
                for h in range(8):
                    P.dma(O["nav"][seq, j, h, t0:t0 + 128, :], sf[:, h * 64:(h + 1) * 64], r=[sfk], w=["o_nav"], isout=True)
                ps2, pk2 = psg.next()
                proj_tm(ps2, pk2, 512, 512, i)
                sf, sfk = sf32.next()
                evac(sf[:, :], sfk, ps2[:, :], pk2)
                for h in range(8):
                    P.dma(O["nak"][seq, j, h, t0:t0 + 128, :], sf[:, h * 64:(h + 1) * 64], r=[sfk], w=["o_nak"], isout=True)
            P.dma(S["vA"][rowsl, :], st[:, :, :].rearrange("p h d -> p (h d)"), r=[sk], w=["vA"])
            ps, pk = psg.next()
            proj_tm(ps, pk, 2560, 512, i)
            st, sk = sbf.next()
            evac(st[:, :], sk, ps[:, :], pk)
            P.dma(S["vB"][rowsl, :], st[:, :], r=[sk], w=["vB"])
            ps, pk = psg.next()
            proj_tm(ps, pk, 3104, 512, i)
            sf, sfk = sf32.next()
            evac(sf[:, :], sfk, ps[:, :], pk)
            P.dma(S["gate"][rowsl, :], sf[:, :], r=[sfk], w=["gate"])
        glr = []
        if "gla" not in parts:
            continue
        for d in range(2):
            ps, pk = psg.next()
            proj_fm(ps, pk, 3072 + 16 * d, 16, tt)
            g, gk = glr_sb[d].next()
            evac(g[:, :], gk, ps[0:16, :], pk)
            glr.append((g, gk))
        for m in range(4):
            pq, pqk = psq.next()
            proj_fm(pq, pqk, 1536 + m * 128, 128, tt)
            pkk, pkkk = psk.next()
            proj_fm(pkk, pkkk, 2048 + m * 128, 128, tt)
            for d in range(2):
                pz, pzk = psz.next()
                P.op("pe", "matmul", r=["l2wg2", glr[d][1]], w=[pzk], out=pz[:, :], lhsT=wg2[:, d, m * 128:(m + 1) * 128], rhs=glr[d][0][:, :],
                     start=True, stop=True)
                t1, k1 = tA.next(); t2, k2 = tB.next(); t3, k3 = tC.next(); t4, k4 = tD.next(); t5, k5 = tE.next()
                P.op("act", "activation", r=[pzk, "l2negb"], w=[k1], out=t1[:, :], in_=pz[:, :], func=AF.Exp, scale=-1.0, bias=negb[:, 4 * d + m:4 * d + m + 1])
                P.op("act", "activation", r=[k1, "epsb"], w=[k2], out=t2[:, :], in_=t1[:, :], func=AF.Ln, bias=C.epsb[:, 1:2])
                if d == 0:
                    P.op("dve", "tensor_tensor_scan", r=["maskf", k2], w=[k3], out=t3[:, :], data0=maskf[:, :], data1=t2[:, :], initial=0.0,
                         op0=ALU.mult, op1=ALU.add)
                    last = t3[:, 63::64]
                else:
                    P.op("dve", "tensor_tensor_scan", r=["maskf", k2], w=[k3], out=t3[:, ::-1], data0=maskf[:, :], data1=t2[:, ::-1], initial=0.0,
                         op0=ALU.mult, op1=ALU.add)
                    last = t3[:, 0::64]
                P.op("act", "activation", r=[k3], w=[k4], out=t4[:, :], in_=t3[:, :], func=AF.Exp, scale=-1.0 / 16)
                P.op("act", "activation", r=[k3], w=[k5], out=t5[:, :], in_=t3[:, :], func=AF.Exp, scale=1.0 / 16)
                st, sk = sbf.next()
                P.op("dve", "scalar_tensor_tensor", r=[pqk, k4], w=[sk], out=st[:, :], in0=pq[:, :], scalar=0.125, in1=t4[:, :], op0=ALU.mult, op1=ALU.mult)
                P.dma(S["qtB"][d, m * 128:(m + 1) * 128, tok], st[:, :], r=[sk], w=["qtB"])
                st, sk = sbf.next()
                P.op("dve", "tensor_tensor", r=[pkkk, k5], w=[sk], out=st[:, :], in0=pkk[:, :], in1=t5[:, :], op=ALU.mult)
                P.dma(S["ktB"][d, m * 128:(m + 1) * 128, tok], st[:, :], r=[sk], w=["ktB"])
                P.op("dve", "tensor_tensor", r=[k3], w=[k1], out=t1[:, :].rearrange("p (c s) -> p c s", s=64),
                     in0=t3[:, :].rearrange("p (c s) -> p c s", s=64), in1=_bc(last.unsqueeze(2), [128, 8, 64]), op=ALU.subtract)
                P.op("act", "activation", r=[k1], w=[k1], out=t1[:, :], in_=t1[:, :], func=AF.Exp, scale=1.0 / 16)
                st, sk = sbf.next()
                P.op("dve", "tensor_tensor", r=[pkkk, k1], w=[sk], out=st[:, :], in0=pkk[:, :], in1=t1[:, :], op=ALU.mult)
                for c in range(4):
                    P.op("pe", "transpose", r=[sk, "identb"], w=[f"l2psT{d}"], out=psT[:, d, c, :], in_=st[:, c * 128:(c + 1) * 128], identity=C.identb[:, :])
                kt_, ktk = ketok.next()
                evac(kt_[:, :, :], ktk, psT[:, d, :, :], f"l2psT{d}")
                P.dma(S["kendB"][d, tok, m * 128:(m + 1) * 128].rearrange("(c p) f -> p c f", p=128), kt_[:, :, :], r=[ktk], w=["kendB"])
                dc, dck = dect.next()
                P.op("act", "activation", r=[k3], w=[dck], out=dc[:, :], in_=last, func=AF.Exp, scale=-1.0 / 16)
                P.dma(S["decB"][d, m * 128:(m + 1) * 128, tt * 8:(tt + 1) * 8], dc[:, :], r=[dck], w=["decB"])
    P.end()


class AttnRes:
    pass


def attn_setup(C, nq_max):
    P = C.P
    R = AttnRes()
    R.ps_s = Rot(P, "at_pss", 2, [128, 512], F32, psum=True)
    R.ps_o = Rot(P, "at_pso", 2, [128, 2, 4 * 65], F32, psum=False) if False else None
    R.pso = [P.ps(f"at_pso{i}", [128, 512], F32) for i in range(4)]
    R.ps_T = Rot(P, "at_psT", 1, [128, 4, 128], BF16, psum=True)
    R.E = Rot(P, "at_E", 2, [128, 512], F32)
    R.Pt = Rot(P, "at_Pt", 3, [128, 7, 512 // 4], BF16) if False else None
    R.rden = Rot(P, "at_rden", 2, [128, 8], F32)
    R.o = Rot(P, "at_o", 2, [128, 512], BF16)
    R.oT = Rot(P, "at_oT", 2, [128, 4, 512], BF16)
    return R


def build_mp(C, j):
    P, nc, I, S, O = C.P, C.nc, C.I, C.S, C.O
    P.begin()
    ps_m = P.ps("mp_psm", [128, 512], F32)
    Mp = C.Mp
    eoh = P.sb("ae_eoh", [32, 64, 128], F32)
    rrows = P.sb("ae_rrows", [120, 32], F32)
    rpbT = P.sb("ae_rpbT", [32, 120], F32)
    P.dma(eoh[:, :, 0:64], I["c_eoh"][:, :, :], w=["ae_eoh"])
    P.dma(eoh[:, :, 64:128], I["c_eoh"][:, :, :], w=["ae_eoh"])
    P.op("pool", "memset", w=["ae_rrows"], ap=rrows[:, :], constant=1.0)
    P.dma(rrows[:, 0:31], I["a_rpb"][j].rearrange("h r k -> (h r) k"), r=[], w=["ae_rrows"])
    P.op("pe", "transpose", r=["ae_rrows", "ident"], w=["ae_psm"], out=ps_m[0:32, 0:120], in_=rrows[:, :], identity=C.ident[0:120, 0:120])
    P.op("dve", "tensor_copy", r=["ae_psm"], w=["ae_rpbT"], out=rpbT[:, :], in_=ps_m[0:32, 0:120])
    P.op("pool", "memset", w=["ae_Mp"], ap=Mp[:, :, :, :], constant=0.0)
    for w0 in range(0, 64, 4):
        for u in range(4):
            P.op("pe", "matmul", r=["ae_eoh", "ae_rpbT"], w=["ae_psm"], out=ps_m[:, u * 120:(u + 1) * 120], lhsT=eoh[:, w0 + u, :], rhs=rpbT[:, :],
                 start=True, stop=True)
        src = ps_m[:, 0:480].rearrange("p (w h r) -> p h r w", w=4, h=8)
        wsl = slice(w0, w0 + 4)
        P.op("act", "activation", r=["ae_psm"], w=["ae_Mp"], out=Mp[0:64, :, 0:14, wsl], in_=src[0:64, :, 0:14, :], func=AF.Exp)
        P.op("act", "activation", r=["ae_psm"], w=["ae_Mp"], out=Mp[64:128, :, 0:14, wsl], in_=src[64:128, :, 1:15, :], func=AF.Exp)
        P.op("act", "activation", r=["ae_psm"], w=["ae_Mp"], out=Mp[0:64, :, 15:19, wsl], in_=src[0:64, :, 4:11:2, :], func=AF.Exp)
        P.op("act", "activation", r=["ae_psm"], w=["ae_Mp"], out=Mp[64:128, :, 14:18, wsl], in_=src[64:128, :, 3:10:2, :], func=AF.Exp)

    P.end()


def attn_even(C, j):
    P, nc, I, S, O = C.P, C.nc, C.I, C.S, C.O
    P.begin()
    R = attn_setup(C, 128)
    kT = P.sb("ae_kT", [64, 8, T], BF16)
    V = P.sb("ae_V", [128, NT, 8 * 65], BF16)
    for h in range(8):
        P.dma(kT[:, h, :], S["qkA"][512 + h * 64:512 + (h + 1) * 64, :], r=["qkA"], w=["ae_kT"])
    for n0 in range(0, NT, 4):
        P.dma(V[:, n0:n0 + 4, :], S["vA"][n0 * 128:(n0 + 4) * 128, :].rearrange("(n p) f -> p n f", p=128), r=["vA"], w=["ae_V"])
    ckT = P.sb("ae_ckT", [64, 8, 256], BF16)
    cV = P.sb("ae_cV", [128, 2, 8 * 65], BF16)
    ctmp = P.sb("ae_ctmp", [128, 2, 8, 64], F32)
    ps_m = P.ps("ae_psm", [128, 512], F32)
    for half in range(2):
        for h in range(8):
            P.dma(ctmp[:, half, h, :], I["cak"][j, h, half * 128:(half + 1) * 128, :], w=["ae_ctmp"])
    for half in range(2):
        for h in range(8):
            P.op("pe", "transpose", r=["ae_ctmp", "ident"], w=["ae_psm"], out=ps_m[0:64, h * 64:h * 64 + 128] if False else ps_m[0:64, (h % 4) * 128:(h % 4) * 128 + 128],
                 in_=ctmp[:, half, h, :], identity=C.ident[:, :])
            if h % 4 == 3:
                h0 = h - 3
                P.op("dve", "tensor_copy", r=["ae_psm"], w=["ae_ckT"], out=ckT[:, h0:h0 + 4, half * 128:(half + 1) * 128],
                     in_=ps_m[0:64, :].rearrange("p (h t) -> p h t", t=128))
    ctmp2 = P.sb("ae_ctmp2", [128, 2, 8, 64], F32)
    for half in range(2):
        for h in range(8):
            P.dma(ctmp2[:, half, h, :], I["cav"][j, h, half * 128:(half + 1) * 128, :], w=["ae_ctmp2"])
    P.op("pool", "memset", w=["ae_cV"], ap=cV[:, :, :], constant=1.0)
    P.op("dve", "tensor_copy", r=["ae_ctmp2"], w=["ae_cV"], out=cV[:, :, :].rearrange("p a (h e) -> p a h e", e=65)[:, :, :, 0:64], in_=ctmp2[:, :, :, :])
    Mp = C.Mp
    qblk = Rot(P, "ae_q", 2, [64, 8, 512], BF16)
    Pt = Rot(P, "ae_Pt", 3, [128, 7, 64], BF16)
    PtP = Rot(P, "ae_PtP", 3, [128, 2, 128], BF16)
    stage_cnt = [0]

    def finish_block(pso_pair, pkeys, nq, tok0, oT_t, oTk, col0):
        rd, rdk = R.rden.next()
        ot, otk = R.o.next()
        for b in range(2):
            v = pso_pair[b][0:nq, 0:260].rearrange("p (h e) -> p h e", e=65)
            P.op("dve", "reciprocal", r=[pkeys[b]], w=[rdk], out=rd[0:nq, 4 * b:4 * b + 4], in_=v[:, :, 64])
            P.op("dve", "tensor_tensor", r=[pkeys[b], rdk], w=[otk], out=ot[0:nq, 256 * b:256 * (b + 1)].rearrange("p (h d) -> p h d", d=64),
                 in0=v[:, :, 0:64], in1=_bc(rd[0:nq, 4 * b:4 * b + 4].unsqueeze(2), [nq, 4, 64]), op=ALU.mult)
        pT, pTk = R.ps_T.next()
        for m in range(4):
            P.op("pe", "transpose", r=[otk, "identb"], w=[pTk], out=pT[:, m, 0:nq], in_=ot[0:nq, m * 128:(m + 1) * 128], identity=C.identb[0:nq, 0:nq])
        P.op("act", "activation", r=[pTk], w=[oTk], out=oT_t[:, :, col0:col0 + nq], in_=pT[:, :, 0:nq], func=AF.Identity)

    pso_i = 0
    for blk in range(8):
        q, qk = qblk.next()
        for h in range(8):
            P.dma(q[:, h, :], S["qkA"][h * 64:(h + 1) * 64, blk * 512:(blk + 1) * 512], r=["qkA"], w=[qk])
        oT_t, oTk = R.oT.next()
        for ri in range(8):
            i = blk * 8 + ri
            r0 = min(max(i - 4, 0), 56)
            if 4 <= i <= 60 and i % 2 == 1:
                a0 = (i - 5) // 2; nl = 5; slots = slice(14, 19)
            else:
                a0 = r0 // 2; nl = 4; s0 = 2 * a0 - i + 7; slots = slice(s0, s0 + 7, 2)
            pso_pair = (R.pso[2 * (pso_i % 2)], R.pso[2 * (pso_i % 2) + 1])
            pkeys = (f"at_pso{2 * (pso_i % 2)}", f"at_pso{2 * (pso_i % 2) + 1}")
            pso_i += 1
            def scores(h):
                ps, psk = R.ps_s.next()
                qrhs = q[:, h, ri * 64:(ri + 1) * 64]
                for x in range(nl):
                    P.op("pe", "matmul", r=["ae_kT", qk], w=[psk], out=ps[:, x * 64:(x + 1) * 64], lhsT=kT[:, h, (a0 + x) * 128:(a0 + x + 1) * 128], rhs=qrhs,
                         start=True, stop=True)
                for x in range(2):
                    P.op("pe", "matmul", r=["ae_ckT", qk], w=[psk], out=ps[:, (nl + x) * 64:(nl + x + 1) * 64], lhsT=ckT[:, h, x * 128:(x + 1) * 128], rhs=qrhs,
                         start=True, stop=True)
                E, Ek = R.E.next()
                pt, ptk = Pt.next()
                P.op("act", "activation", r=[psk], w=[Ek], out=E[:, 0:nl * 64], in_=ps[:, 0:nl * 64], func=AF.Exp, scale=0.125)
                P.op("act", "activation", r=[psk], w=[ptk], out=pt[:, nl:nl + 2, :], in_=ps[:, nl * 64:(nl + 2) * 64].rearrange("p (x q) -> p x q", q=64),
                     func=AF.Exp, scale=0.125)
                P.op("dve", "tensor_tensor", r=[Ek, "ae_Mp"], w=[ptk], out=pt[:, 0:nl, :], in0=E[:, 0:nl * 64].rearrange("p (x q) -> p x q", q=64),
                     in1=Mp[:, h, slots, :], op=ALU.mult)
                return pt, ptk

            def pv(h, pt, ptk):
                po = pso_pair[h // 4]; pok = pkeys[h // 4]
                hs = (h % 4) * 65
                for x in range(nl):
                    P.op("pe", "matmul", r=[ptk, "ae_V"], w=[pok], out=po[0:64, hs:hs + 65], lhsT=pt[:, x, :], rhs=V[:, a0 + x, h * 65:(h + 1) * 65],
                         start=(x == 0), stop=False)
                for x in range(2):
                    P.op("pe", "matmul", r=[ptk, "ae_cV"], w=[pok], out=po[0:64, hs:hs + 65], lhsT=pt[:, nl + x, :], rhs=cV[:, x, h * 65:(h + 1) * 65],
                         start=False, stop=(x == 1))
            pend = scores(0)
            for h in range(8):
                nxt = scores(h + 1) if h < 7 else None
                pv(h, *pend)
                pend = nxt
            finish_block(pso_pair, pkeys, 64, i * 64, oT_t, oTk, ri * 64)
        P.dma(S["oT"][0:512, blk * 512:(blk + 1) * 512].rearrange("(m p) t -> p m t", p=128), oT_t[:, :, :], r=[oTk], w=["oT_a"])
    q, qk = qblk.next()
    for h in range(8):
        P.dma(q[:, h, :], S["qkA"][h * 64:(h + 1) * 64, TS:TS + 512], r=["qkA"], w=[qk])
    oT_t, oTk = R.oT.next()
    for sq in range(2):
        for qb in range(2):
            pso_pair = (R.pso[2 * (pso_i % 2)], R.pso[2 * (pso_i % 2) + 1])
            pkeys = (f"at_pso{2 * (pso_i % 2)}", f"at_pso{2 * (pso_i % 2) + 1}")
            pso_i += 1
            col = sq * 256 + qb * 128
            for h in range(8):
                ps, psk = R.ps_s.next()
                qrhs = q[:, h, col:col + 128]
                for x in range(2):
                    kc = TS + sq * 256 + x * 128
                    P.op("pe", "matmul", r=["ae_kT", qk], w=[psk], out=ps[:, x * 128:(x + 1) * 128], lhsT=kT[:, h, kc:kc + 128], rhs=qrhs, start=True, stop=True)
                pt, ptk = PtP.next()
                P.op("act", "activation", r=[psk], w=[ptk], out=pt[:, :, :], in_=ps[:, 0:256].rearrange("p (x q) -> p x q", q=128), func=AF.Exp, scale=0.125)
                po = pso_pair[h // 4]; pok = pkeys[h // 4]
                hs = (h % 4) * 65
                for x in range(2):
                    vn = (TS + sq * 256 + x * 128) // 128
                    P.op("pe", "matmul", r=[ptk, "ae_V"], w=[pok], out=po[:, hs:hs + 65], lhsT=pt[:, x, :], rhs=V[:, vn, h * 65:(h + 1) * 65],
                         start=(x == 0), stop=(x == 1))
            finish_block(pso_pair, pkeys, 128, 0, oT_t, oTk, col)
    P.dma(S["oT"][0:512, TS:TS + 512].rearrange("(m p) t -> p m t", p=128), oT_t[:, :, :], r=[oTk], w=["oT_a"])
    P.end()


def scan_finalize(C, ng_row, key_prefix):
    P, nc, I, S, O = C.P, C.nc, C.I, C.S, C.O
    P.begin()
    ngb = P.sb("fz_ng", [128, 512], F32)
    P.dma(ngb[:, :], ng_row.partition_broadcast(128), w=["fz_ng"])
    of = Rot(P, "fz_of", 2, [128, 512], F32)
    ob = Rot(P, "fz_ob", 2, [128, 512], F32)
    gt = Rot(P, "fz_gt", 2, [128, 512], F32)
    sq = Rot(P, "fz_sq", 2, [128, 512], F32)
    ss = Rot(P, "fz_ss", 2, [128, 16], F32)
    ob16 = Rot(P, "fz_o16", 2, [128, 512], BF16)
    psT = Rot(P, "fz_psT", 2, [128, 4, 128], BF16, psum=True)
    oTs = Rot(P, "fz_oT", 2, [128, 4, 512], BF16)
    oT_t = None
    for i in range(NT):
        rows = slice(i * 128, (i + 1) * 128)
        a, ak = of.next(); b, bk = ob.next(); g, gk = gt.next(); q, qk = sq.next(); s_, sk = ss.next(); o16, o16k = ob16.next()
        P.dma(a[:, :], S["ofb"][0, rows, :], r=["ofb0"], w=[ak])
        P.dma(b[:, :], S["ofb"][1, rows, :], r=["ofb1"], w=[bk])
        P.dma(g[:, :], S["gate"][rows, :], r=["gate"], w=[gk])
        P.op("dve", "tensor_tensor", r=[ak, bk], w=[ak], out=a[:, :], in0=a[:, :], in1=b[:, :], op=ALU.add)
        P.op("dve", "tensor_tensor", r=[ak], w=[qk], out=q[:, :], in0=a[:, :], in1=a[:, :], op=ALU.mult)
        P.op("dve", "tensor_reduce", r=[qk], w=[sk], out=s_[:, 0:8], in_=q[:, :].rearrange("p (h d) -> p h d", d=64), axis=AX.X, op=ALU.add)
        P.op("act", "activation", r=[sk, "epsb"], w=[sk], out=s_[:, 8:16], in_=s_[:, 0:8], func=AF.Ln, scale=1.0 / 64, bias=C.epsb[:, 0:1])
        P.op("act", "activation", r=[sk], w=[sk], out=s_[:, 0:8], in_=s_[:, 8:16], func=AF.Exp, scale=-0.5)
        P.op("dve", "tensor_tensor", r=[ak, sk], w=[ak], out=a[:, :].rearrange("p (h d) -> p h d", d=64), in0=a[:, :].rearrange("p (h d) -> p h d", d=64),
             in1=_bc(s_[:, 0:8].unsqueeze(2), [128, 8, 64]), op=ALU.mult)
        P.op("dve", "tensor_tensor", r=[ak, "fz_ng"], w=[ak], out=a[:, :], in0=a[:, :], in1=ngb[:, :], op=ALU.mult)
        P.op("act", "activation", r=[gk], w=[bk], out=b[:, :], in_=g[:, :], func=AF.Exp, scale=-1.0)
        P.op("dve", "tensor_scalar", r=[bk], w=[bk], out=b[:, :], in0=b[:, :], scalar1=1.0, scalar2=None, op0=ALU.add)
        P.op("dve", "reciprocal", r=[bk], w=[bk], out=b[:, :], in_=b[:, :])
        P.op("dve", "tensor_tensor", r=[gk, bk], w=[gk], out=g[:, :], in0=g[:, :], in1=b[:, :], op=ALU.mult)
        P.op("dve", "tensor_tensor", r=[ak, gk], w=[o16k], out=o16[:, :], in0=a[:, :], in1=g[:, :], op=ALU.mult)
        pT, pTk = psT.next()
        for m in range(4):
            P.op("pe", "transpose", r=[o16k, "identb"], w=[pTk], out=pT[:, m, :], in_=o16[:, m * 128:(m + 1) * 128], identity=C.identb[:, :])
        if i % 4 == 0:
            oT_t, oTk = oTs.next()
        P.op("act", "activation", r=[pTk], w=[oTk], out=oT_t[:, :, (i % 4) * 128:(i % 4 + 1) * 128], in_=pT[:, :, :], func=AF.Identity)
        if i % 4 == 3:
            blk = i // 4
            P.dma(S["oT"][512:1024, blk * 512:(blk + 1) * 512].rearrange("(m p) t -> p m t", p=128), oT_t[:, :, :], r=[oTk], w=["oT_b"])
    P.end()


def gla_scan(C, j):
    P, nc, I, S, O = C.P, C.nc, C.I, C.S, C.O
    P.begin()
    tri = P.sb("gs_tri", [64, 2, 64], F32)
    P.dma(tri[:, 0, :], I["c_tri"][0, 0:64, 0:64], w=["gs_tri"])
    P.dma(tri[:, 1, :], I["c_tri"][1, 0:64, 0:64], w=["gs_tri"])
    dec = P.sb("gs_dec", [64, 2, 8, 72], F32)
    for d in range(2):
        for h in range(8):
            P.dma(dec[:, d, h, :], S["decB"][d, h * 64:(h + 1) * 64, :], r=["decB"], w=["gs_dec"])
    Sst = [P.sb(f"gs_S{d}", [64, 8, 64], F32) for d in range(2)]
    Sbf = [P.sb(f"gs_Sbf{d}", [64, 8, 64], BF16) for d in range(2)]
    Stmp = [P.sb(f"gs_St{d}", [64, 8, 64], F32) for d in range(2)]
    qtb = [Rot(P, f"gs_qt{d}_", 2, [64, 8, 512], BF16) for d in range(2)]
    ktb = [Rot(P, f"gs_kt{d}_", 2, [64, 8, 512], BF16) for d in range(2)]
    keb = [Rot(P, f"gs_ke{d}_", 2, [64, 8, 512], BF16) for d in range(2)]
    vbl = [Rot(P, f"gs_v{d}_", 2, [64, 8, 512], BF16) for d in range(2)]
    ost = [Rot(P, f"gs_o{d}_", 1, [64, 8, 512], F32) for d in range(2)]
    attm = [Rot(P, f"gs_att{d}_", 2, [64, 8, 64], BF16) for d in range(2)]
    ps_att = [P.ps(f"gs_psatt{d}", [128, 512], F32) for d in range(2)]
    ps_o = [P.ps(f"gs_pso{d}", [128, 512], F32) for d in range(2)]
    ps_s = [P.ps(f"gs_pss{d}", [128, 512], F32) for d in range(2)]

    for sqi, (t0, tl) in enumerate(SEQS):
        nch = tl // 64
        nblk = tl // 512 if tl >= 512 else 1
        bl = min(tl, 512)
        cpb = bl // 64
        for d in range(2):
            if sqi == 0:
                for h in range(8):
                    P.dma(Sst[d][:, h, :], I["sb"][j, d, h, :, :], w=[f"gs_S{d}"])
            else:
                P.op("pool", "memset", w=[f"gs_S{d}"], ap=Sst[d][:, :, :], constant=0.0)
            P.op("act", "activation", r=[f"gs_S{d}"], w=[f"gs_Sbf{d}"], out=Sbf[d][:, :, :], in_=Sst[d][:, :, :], func=AF.Identity)
        cur = [None, None]

        def gchunk(step, d):
                c = step if d == 0 else nch - 1 - step
                blk = c // cpb
                cc = c % cpb
                first_in_blk = (cc == 0) if d == 0 else (cc == cpb - 1)
                last_in_blk = (cc == cpb - 1) if d == 0 else (cc == 0)
                tb = t0 + blk * bl
                if first_in_blk:
                    qt, qtk = qtb[d].next(); kt, ktk = ktb[d].next(); ke, kek = keb[d].next(); vv, vk = vbl[d].next(); oo, ook = ost[d].next()
                    for h in range(8):
                        P.dma(qt[:, h, 0:bl], S["qtB"][d, h * 64:(h + 1) * 64, tb:tb + bl], r=["qtB"], w=[qtk])
                        P.dma(kt[:, h, 0:bl], S["ktB"][d, h * 64:(h + 1) * 64, tb:tb + bl], r=["ktB"], w=[ktk])
                    P.dma(ke[:, 0:cpb, :], S["kendB"][d, tb:tb + bl, :].rearrange("(c p) f -> p c f", p=64), r=["kendB"], w=[kek])
                    P.dma(vv[:, 0:cpb, :], S["vB"][tb:tb + bl, :].rearrange("(c p) f -> p c f", p=64), r=["vB"], w=[vk])
                    cur[d] = (qt, qtk, kt, ktk, ke, kek, vv, vk, oo, ook)
                qt, qtk, kt, ktk, ke, kek, vv, vk, oo, ook = cur[d]
                cs = slice(cc * 64, (cc + 1) * 64)
                gch = t0 // 64 + c
                pa = ps_att[d]; pak = f"gs_psatt{d}"
                for h in range(8):
                    P.op("pe", "matmul", r=[ktk, qtk], w=[pak], out=pa[0:64, h * 64:(h + 1) * 64], lhsT=kt[:, h, cs], rhs=qt[:, h, cs], start=True, stop=True)
                am, amk = attm[d].next()
                P.op("dve", "tensor_tensor", r=[pak, "gs_tri"], w=[amk], out=am[:, :, :], in0=pa[0:64, :].rearrange("p (h c) -> p h c", c=64),
                     in1=_bc(tri[:, d:d + 1, :], [64, 8, 64]), op=ALU.mult)
                po = ps_o[d]; pok = f"gs_pso{d}"
                for h in range(8):
                    P.op("pe", "matmul", r=[amk, vk], w=[pok], out=po[0:64, h * 64:(h + 1) * 64], lhsT=am[:, h, :], rhs=vv[:, cc, h * 64:(h + 1) * 64], start=True, stop=False)
                    P.op("pe", "matmul", r=[qtk, f"gs_Sbf{d}"], w=[pok], out=po[0:64, h * 64:(h + 1) * 64], lhsT=qt[:, h, cs], rhs=Sbf[d][:, h, :], start=False, stop=True)
                P.op("act", "activation", r=[pok], w=[ook], out=oo[:, cc, :], in_=po[0:64, :], func=AF.Identity)
                pS = ps_s[d]; pSk = f"gs_pss{d}"
                for h in range(8):
                    P.op("pe", "matmul", r=[kek, vk], w=[pSk], out=pS[0:64, h * 64:(h + 1) * 64], lhsT=ke[:, cc, h * 64:(h + 1) * 64], rhs=vv[:, cc, h * 64:(h + 1) * 64],
                         start=True, stop=True)
                P.op("dve", "tensor_tensor", r=[f"gs_S{d}", "gs_dec"], w=[f"gs_St{d}"], out=Stmp[d][:, :, :], in0=Sst[d][:, :, :],
                     in1=_bc(dec[:, d, :, gch:gch + 1], [64, 8, 64]), op=ALU.mult)
                P.op("dve", "tensor_tensor", r=[f"gs_St{d}", pSk], w=[f"gs_S{d}"], out=Sst[d][:, :, :], in0=Stmp[d][:, :, :],
                     in1=pS[0:64, :].rearrange("p (h v) -> p h v", v=64), op=ALU.add)
                P.op("act", "activation", r=[f"gs_S{d}"], w=[f"gs_Sbf{d}"], out=Sbf[d][:, :, :], in_=Sst[d][:, :, :], func=AF.Identity)
                if last_in_blk:
                    P.dma(S["ofb"][d, tb:tb + bl, :].rearrange("(c p) f -> p c f", p=64), oo[:, 0:cpb, :], r=[ook], w=[f"ofb{d}"])

        for step in range(nch):
            lists = []
            for d in range(2):
                ops, _ = capture(P, lambda d=d: gchunk(step, d))
                lists.append(ops)
            P.ops.extend(zipper(lists))
        if sqi > 0:
            for d in range(2):
                for h in range(8):
                    P.dma(O["nsb"][sqi - 1, j, d, h, :, :], Sst[d][:, h, :], r=[f"gs_S{d}"], w=["o_nsb"], isout=True)
    P.end()


def proj_residual(C, l, which, wsrc, nk, act_src, act_key, xsrc, xkey):
    P, nc, I, S, O = C.P, C.nc, C.I, C.S, C.O
    P.begin()
    W = P.sb("pr_w", [128, nk, 1024], BF16)
    load_w_bf16(C, W, wsrc, 1024, "prw")
    gbc = P.sb("pr_g", [128, 3, 1024], F32)
    for s_ in range(3):
        P.dma(gbc[:, s_, :], S["gates"][l, s_:s_ + 1, which, :].partition_broadcast(128), r=[f"gates{l}"], w=["pr_g"])
    ablk = Rot(P, "pr_a", 2, [128, nk, 512], BF16)
    xt = Rot(P, "pr_x", 3, [128, 1024], F32)
    tmp = Rot(P, "pr_t", 2, [128, 1024], F32)
    ps = Rot(P, "pr_ps", 4, [128, 512], F32, psum=True)
    for blk in range(T // 512):
        a, ak = ablk.next()
        for k0 in range(0, nk, 8):
            kn = min(8, nk - k0)
            P.dma(a[:, k0:k0 + kn, :], act_src[k0 * 128:(k0 + kn) * 128, blk * 512:(blk + 1) * 512].rearrange("(k p) t -> p k t", p=128), r=[act_key], w=[ak])
        for u in range(4):
            i = blk * 4 + u
            seq = 0 if i < 32 else (1 if i < 34 else 2)
            x, xk = xt.next()
            t_, tk = tmp.next()
            P.dma(x[:, :], xsrc[i * 128:(i + 1) * 128, :], r=[xkey], w=[xk])
            for half in range(2):
                p_, pk = ps.next()
                for k in range(nk):
                    P.op("pe", "matmul", r=[ak, f"prw_{k}"], w=[pk], out=p_[:, :], lhsT=a[:, k, u * 128:(u + 1) * 128], rhs=W[:, k, half * 512:(half + 1) * 512],
                         start=(k == 0), stop=(k == nk - 1))
                P.op("dve", "tensor_tensor", r=[pk, "pr_g"], w=[tk], out=t_[:, half * 512:(half + 1) * 512], in0=p_[:, :], in1=gbc[:, seq, half * 512:(half + 1) * 512], op=ALU.mult)
            P.op("pool", "tensor_tensor", r=[xk, tk], w=[xk], out=x[:, :], in0=x[:, :], in1=t_[:, :], op=ALU.add)
            P.dma(S["xres"][i * 128:(i + 1) * 128, :], x[:, :], r=[xk], w=["xres_w"])
    P.end()


def ffn_up(C, l):
    P, nc, I, S, O = C.P, C.nc, C.I, C.S, C.O
    P.begin()
    NC_ = T + 4
    rows = P.sb("fu_rows", [128, 128], F32)
    ps_t = P.ps("fu_pst", [128, 512], F32)
    wc = P.sb("fu_wc", [128, 3, 44], F32)
    load_fm(C, wc[:, 0:2, :].rearrange("p i c -> p (i c)"), I["ffn_conv"][l, 0:2, :].rearrange("i (c p) -> (i c) p", p=128), 88, ps_t, rows, "fuwc01")
    load_fm(C, wc[:, 2, :], I["ffn_conv"][l, 2, :].rearrange("(c p) -> c p", p=128), 44, ps_t, rows, "fuwc2")
    wkeys_c = ["fuwc01", "fuwc2"]
    U = [P.sb(f"fu_U{g}", [128, NC_], F32) for g in range(2)]
    Cv = [P.sb(f"fu_C{g}", [128, NC_], F32) for g in range(2)]
    for g in range(2):
        P.op("pool", "memset", w=[f"fu_U{g}"], ap=U[g][:, :], constant=0.0)
    wt = Rot(P, "fu_w", 4, [128, 8, 128], BF16)
    wst = Rot(P, "fu_wst", 4, [128, 8, 128], F32)
    act = Rot(P, "fu_act", 2, [128, NC_], BF16)
    ps = Rot(P, "fu_ps", 6, [128, 512], F32, psum=True)
    colof = lambda t: t + 1 if t < TS else (t + 2 if t < TS + TP else t + 3)
    ev = 0
    skip = os.environ.get("FU_SKIP", "").split(",")
    nm_ = int(os.environ.get("FU_M", DFF // 128))

    def load_pair(m):
        ws = []
        for g in range(2):
            w_, wk = wt.next()
            c0 = g * DFF + m * 128
            wf, wfk = wst.next()
            P.dma(wf[:, :, :], I["ffn_up"][l, :, c0:c0 + 128].rearrange("(k p) n -> p k n", p=128), w=[wfk])
            P.op("pool", "tensor_copy", r=[wfk], w=[wk], out=w_[:, :, :], in_=wf[:, :, :])
            ws.append((w_, wk))
        return ws
    nxt = load_pair(0)
    for m in range(nm_):
        ws = nxt
        if m + 1 < nm_:
            nxt = load_pair(m + 1)
        for g in range(2):
            w_, wk = ws[g]
            for tt in range(T // 512):
                p_, pk = ps.next()
                for k in range(8):
                    P.op("pe", "matmul", r=[wk] + [f"hT{4 * tt + u}" for u in range(4)], w=[pk], out=p_[:, :], lhsT=w_[:, k, :], rhs=C.hT[:, k, tt * 512:(tt + 1) * 512],
                         start=(k == 0), stop=(k == 7))
                pieces = [(0, 512)] if tt < 8 else [(0, 256), (256, 512)]
                for (a0, a1) in pieces:
                    c_ = colof(tt * 512 + a0)
                    P.op("act", "activation", r=[pk], w=[f"fu_U{g}"], out=U[g][:, c_:c_ + (a1 - a0)], in_=p_[:, a0:a1], func=AF.Identity)
            ci = g * 22 + m
            n = NC_ - 2
            if "conv" in skip:
                continue
            P.op("dve", "tensor_scalar", r=[f"fu_U{g}"] + wkeys_c, w=[f"fu_C{g}"], out=Cv[g][:, 1:1 + n], in0=U[g][:, 0:n], scalar1=wc[:, 0, ci:ci + 1], scalar2=None, op0=ALU.mult)
            P.op("dve", "scalar_tensor_tensor", r=[f"fu_U{g}", f"fu_C{g}"] + wkeys_c, w=[f"fu_C{g}"], out=Cv[g][:, 1:1 + n], in0=U[g][:, 1:1 + n], scalar=wc[:, 1, ci:ci + 1],
                 in1=Cv[g][:, 1:1 + n], op0=ALU.mult, op1=ALU.add)
            P.op("dve", "scalar_tensor_tensor", r=[f"fu_U{g}", f"fu_C{g}"] + wkeys_c, w=[f"fu_C{g}"], out=Cv[g][:, 1:1 + n], in0=U[g][:, 2:2 + n], scalar=wc[:, 2, ci:ci + 1],
                 in1=Cv[g][:, 1:1 + n], op0=ALU.mult, op1=ALU.add)
        if "silu" not in skip:
            P.op("act", "activation", r=["fu_C1"], w=["fu_C1"], out=Cv[1][:, 1:NC_ - 1], in_=Cv[1][:, 1:NC_ - 1], func=AF.Silu)
        if "mul" in skip:
            continue
        a_, ak = act.next()
        for (t0, tl) in SEQS:
            c_ = colof(t0)
            P.op("dve", "tensor_tensor", r=["fu_C0", "fu_C1"], w=[ak], out=a_[:, t0:t0 + tl], in0=Cv[0][:, c_:c_ + tl], in1=Cv[1][:, c_:c_ + tl], op=ALU.mult)
        P.dma(S["actT"][m * 128:(m + 1) * 128, :], a_[:, 0:T], r=[ak], w=["actT"])
    P.end()


def final_norm(C):
    P, nc, I, S, O = C.P, C.nc, C.I, C.S, C.O
    P.begin()
    gb = P.sb("fn_g", [128, 1024], F32)
    P.dma(gb[:, :], I["final_g"].rearrange("(o n) -> o n", o=1).partition_broadcast(128), w=["fn_g"])
    xt = Rot(P, "fn_x", 3, [128, 1024], F32)
    junk = P.sb("fn_junk", [128, 1024], BF16)
    ss = Rot(P, "fn_ss", 3, [128, 4], F32)
    yt = Rot(P, "fn_y", 3, [128, 1024], F32)
    for i in range(NT):
        x, xk = xt.next(); s_, sk = ss.next(); y, yk = yt.next()
        P.dma(x[:, :], S["xres"][i * 128:(i + 1) * 128, :], r=["xres_w"], w=[xk])
        P.op("act", "activation", r=[xk], w=["fn_junk", sk], out=junk[:, :], in_=x[:, :], func=AF.Square, accum_out=s_[:, 0:1])
        P.op("act", "activation", r=[sk, "epsb"], w=[sk], out=s_[:, 1:2], in_=s_[:, 0:1], func=AF.Ln, scale=1.0 / D, bias=C.epsb[:, 0:1])
        P.op("act", "activation", r=[sk], w=[sk], out=s_[:, 2:3], in_=s_[:, 1:2], func=AF.Exp, scale=-0.5)
        P.op("dve", "scalar_tensor_tensor", r=[xk, sk, "fn_g"], w=[yk], out=y[:, :], in0=x[:, :], scalar=s_[:, 2:3], in1=gb[:, :], op0=ALU.mult, op1=ALU.mult)
        P.dma(O["y"][i * 128:(i + 1) * 128, :], y[:, :], r=[yk], w=["o_y"], isout=True)
    P.end()


def l2_odd(C, j):
    P, nc, I, S, O = C.P, C.nc, C.I, C.S, C.O
    P.begin()
    W = P.sb("lo_w", [128, 8, OD_IN], BF16)
    load_w_bf16(C, W, I["od_w_in"][j], OD_IN, "low")
    wkeys = [f"low_{k}" for k in range(8)]
    rows = P.sb("lo_rows", [64, 128], F32)
    ps_t = P.ps("lo_pst", [128, 512], F32)
    wc = P.sb("lo_wc", [128, 3, 12], F32)
    load_fm(C, wc[:, :, :].rearrange("p i c -> p (i c)"), I["d_conv"][j].rearrange("i (c p) -> (i c) p", p=128), 36, ps_t, rows, "lowc")
    perm = P.sb("lo_perm", [128, 128], F32)
    P.dma(perm[:, :], I["c_perm"][:, :], w=["lo_perm"])
    bd = P.sb("lo_bd", [128, 128], F32)
    P.dma(bd[:, :], I["c_bd"][:, :], w=["lo_bd"])
    maskf = P.sb("lo_maskf", [8, 512], F32)
    P.dma(maskf[:, :], I["c_maskf"][0:8, :], w=["lo_maskf"])
    diag8 = P.sb("lo_diag8", [8, 8], F32)
    P.dma(diag8[:, :], I["c_ident"][0:8, 0:8], w=["lo_diag8"])
    par = P.sb("lo_par", [8, 4], F32)
    P.dma(par[:, 0:2], I["d_dt_bias"][j].rearrange("d h -> h d"), w=["lo_par"], allow_slow_non_contiguous=True)
    P.dma(par[:, 2:4], I["d_a_log"][j].rearrange("d h -> h d"), w=["lo_par"], allow_slow_non_contiguous=True)
    P.op("act", "activation", r=["lo_par"], w=["lo_par"], out=par[:, 2:4], in_=par[:, 2:4], func=AF.Exp)
    P.op("dve", "tensor_scalar", r=["lo_par"], w=["lo_par"], out=par[:, 2:4], in0=par[:, 2:4], scalar1=-1.0, scalar2=None, op0=ALU.mult)

    psg = Rot(P, "lo_psg", 4, [128, 512], F32, psum=True)
    psT = Rot(P, "lo_psT", 2, [128, 512], F32, psum=True)
    sbf = Rot(P, "lo_sbf", 3, [128, 512], BF16)
    sf32 = Rot(P, "lo_sf", 5, [128, 512], F32)
    vst = Rot(P, "lo_vst", 2, [128, 2, 65], BF16)
    for t_, k_ in zip(vst.tiles, vst.keys):
        P.op("pool", "memset", w=[k_], ap=t_[:, :, 64:65], constant=1.0)
    cs_t = Rot(P, "lo_cs", 2, [128, 2, 512], F32)
    ev = [0]

    def evac(dst, dkey, src, skey, extra_r=()):
        ev[0] += 1
        if ev[0] % 2:
            P.op("act", "activation", r=[skey] + list(extra_r), w=[dkey], out=dst, in_=src, func=AF.Identity)
        else:
            P.op("dve", "tensor_copy", r=[skey] + list(extra_r), w=[dkey], out=dst, in_=src)

    def proj_fm(ps, pkey, c0, M, tt):
        for k in range(8):
            P.op("pe", "matmul", r=[wkeys[k]] + [f"hT{4 * tt + u}" for u in range(4)], w=[pkey],
                 out=ps[0:M, :], lhsT=W[:, k, c0:c0 + M], rhs=C.hT[:, k, tt * 512:(tt + 1) * 512], start=(k == 0), stop=(k == 7))

    def proj_tm(ps, pkey, c0, N, i):
        for k in range(8):
            P.op("pe", "matmul", r=[wkeys[k], f"hT{i}"], w=[pkey],
                 out=ps[:, 0:N], lhsT=C.hT[:, k, i * 128:(i + 1) * 128], rhs=W[:, k, c0:c0 + N], start=(k == 0), stop=(k == 7))

    for tt in range(T // 512):
        tok = slice(tt * 512, (tt + 1) * 512)
        if tt < 8:
            cs, csk = cs_t.next()
            P.dma(cs[:, 0, :], I["c_cos"][:, tok], w=[csk])
            P.dma(cs[:, 1, :], I["c_sin"][:, tok], w=[csk])
        for m in range(5):
            ps, pk = psg.next()
            proj_fm(ps, pk, m * 128, 128, tt)
            st, sk = sbf.next()
            if tt == 8:
                evac(st[:, :], sk, ps[:, :], pk)
            else:
                xf, xfk = sf32.next()
                evac(xf[:, :], xfk, ps[:, :], pk)
                pr, prk = psg.next()
                P.op("pe", "matmul", r=[xfk, "lo_perm"], w=[prk], out=pr[:, :], lhsT=perm[:, :], rhs=xf[:, :], start=True, stop=True)
                t2, t2k = sf32.next()
                P.op("dve", "tensor_tensor", r=[prk, csk], w=[t2k], out=t2[:, :], in0=pr[:, :], in1=cs[:, 1, :], op=ALU.mult)
                P.op("dve", "tensor_tensor", r=[xfk, csk], w=[xfk], out=xf[:, :], in0=xf[:, :], in1=cs[:, 0, :], op=ALU.mult)
                P.op("pool", "tensor_tensor", r=[xfk, t2k], w=[sk], out=st[:, :], in0=xf[:, :], in1=t2[:, :], op=ALU.add)
            P.dma(S["qkC"][m * 128:(m + 1) * 128, tok], st[:, :], r=[sk], w=["qkC"])
        for u in range(4):
            i = 4 * tt + u
            rowsl = slice(i * 128, (i + 1) * 128)
            ps, pk = psg.next()
            proj_tm(ps, pk, 512, 256, i)
            st, sk = vst.next()
            if i < 32:
                evac(st[:, :, 0:64], sk, ps[:, 128:256].rearrange("p (g d) -> p g d", d=64), pk)
            else:
                seq = (i - 32) // 2; t0 = ((i - 32) % 2) * 128
                sf, sfk = sf32.next()
                evac(sf[:, 0:256], sfk, ps[:, 0:256], pk)
                P.op("pool", "tensor_copy", r=[sfk], w=[sk], out=st[:, :, 0:64], in_=sf[:, 128:256].rearrange("p (g d) -> p g d", d=64))
                for g in range(2):
                    P.dma(O["nck"][seq, j, g, t0:t0 + 128, :], sf[:, g * 64:(g + 1) * 64], r=[sfk], w=["o_nck"], isout=True)
                    P.dma(O["ncv"][seq, j, g, t0:t0 + 128, :], sf[:, 128 + g * 64:128 + (g + 1) * 64], r=[sfk], w=["o_ncv"], isout=True)
            P.dma(S["vC"][rowsl, :], st[:, :, :].rearrange("p g d -> p (g d)"), r=[sk], w=["vC"])
            ps, pk = psg.next()
            proj_tm(ps, pk, 2336, 512, i)
            sf, sfk = sf32.next()
            evac(sf[:, :], sfk, ps[:, :], pk)
            P.dma(S["gate"][rowsl, :], sf[:, :], r=[sfk], w=["gate"])
    P.sub_begin()
    sc8 = Rot(P, "lo_sc8", 8, [8, 512], F32)
    gex = Rot(P, "lo_gex", 2, [8, 8, 512], F32)
    sct = Rot(P, "lo_sct", 2, [128, 4, 40], F32)
    for tt in range(T // 512):
        tok = slice(tt * 512, (tt + 1) * 512)
        for d in range(2):
            pa, pak = psg.next()
            proj_fm(pa, pak, 2304 + 8 * d, 8, tt)
            pb, pbk = psg.next()
            proj_fm(pb, pbk, 2320 + 8 * d, 8, tt)
            t1, k1 = sc8.next(); Gt, Gk = sc8.next(); Lt, Lk = sc8.next(); Ht, Hk = sc8.next(); Bt, Bk = sc8.next(); BEt, BEk = sc8.next(); ELt, ELk = sc8.next()
            P.op("act", "activation", r=[pak, "lo_par"], w=[k1], out=t1[:, :], in_=pa[0:8, :], func=AF.Exp, bias=par[:, d:d + 1])
            P.op("act", "activation", r=[k1, "epsb"], w=[k1], out=t1[:, :], in_=t1[:, :], func=AF.Ln, bias=C.epsb[0:8, 1:2])
            if d == 0:
                P.op("dve", "tensor_tensor_scan", r=["lo_maskf", k1], w=[Gk], out=Gt[:, :], data0=maskf[:, :], data1=t1[:, :], initial=0.0, op0=ALU.mult, op1=ALU.add)
                lastv = Gt[:, 63::64]
            else:
                P.op("dve", "tensor_tensor_scan", r=["lo_maskf", k1], w=[Gk], out=Gt[:, ::-1], data0=maskf[:, :], data1=t1[:, ::-1], initial=0.0, op0=ALU.mult, op1=ALU.add)
                lastv = Gt[:, 0::64]
            P.op("dve", "tensor_scalar", r=[Gk, "lo_par"], w=[Gk], out=Gt[:, :], in0=Gt[:, :], scalar1=par[:, 2 + d:3 + d], scalar2=None, op0=ALU.mult)
            P.op("act", "activation", r=[pbk], w=[Lk], out=Lt[:, :], in_=pb[0:8, :], func=AF.Exp, scale=-1.0)
            P.op("act", "activation", r=[Lk, "epsb"], w=[Lk], out=Lt[:, :], in_=Lt[:, :], func=AF.Ln, bias=C.epsb[0:8, 1:2])
            P.op("dve", "tensor_tensor", r=[Gk, Lk], w=[Hk], out=Ht[:, :], in0=Gt[:, :], in1=Lt[:, :], op=ALU.subtract)
            P.op("act", "activation", r=[Lk], w=[Bk], out=Bt[:, :], in_=Lt[:, :], func=AF.Exp, scale=-1.0)
            P.op("act", "activation", r=[Hk], w=[BEk], out=BEt[:, :], in_=Ht[:, :], func=AF.Exp)
            P.op("dve", "tensor_tensor", r=[Gk], w=[ELk], out=ELt[:, :].rearrange("p (c s) -> p c s", s=64), in0=_bc(lastv.unsqueeze(2), [8, 8, 64]),
                 in1=Gt[:, :].rearrange("p (c s) -> p c s", s=64), op=ALU.subtract)
            P.op("act", "activation", r=[ELk], w=[ELk], out=ELt[:, :], in_=ELt[:, :], func=AF.Exp)
            for (src, srck, name) in ((Gt, Gk, "Gexp"), (Ht, Hk, "Hexp")):
                gx, gxk = gex.next()
                P.op("pool", "tensor_tensor", r=[srck, "lo_diag8"], w=[gxk], out=gx[:, :, :], in0=_bc(src[:, :].unsqueeze(1), [8, 8, 512]),
                     in1=_bc(diag8[:, :].unsqueeze(2), [8, 8, 512]), op=ALU.mult)
                P.dma(S[name][d, :, :, tok], gx[:, :, :], r=[gxk], w=[name])
            pT, pTk = psT.next()
            for u in range(4):
                for qi, (src, srck) in enumerate(((Gt, Gk), (Ht, Hk), (Bt, Bk), (BEt, BEk), (ELt, ELk))):
                    P.op("pe", "transpose", r=[srck, "ident"], w=[pTk], out=pT[:, u * 40 + qi * 8:u * 40 + qi * 8 + 8], in_=src[:, u * 128:(u + 1) * 128],
                         identity=C.ident[0:8, 0:8])
            stt_, sttk = sct.next()
            evac(stt_[:, :, :], sttk, pT[:, 0:160].rearrange("p (u f) -> p u f", f=40), pTk)
            P.dma(S["dsc"][d, tok, :].rearrange("(u p) f -> p u f", p=128), stt_[:, :, :], r=[sttk], w=["dsc"])
    P.sub_end()
    NC_ = T + 4
    U = P.sb("lo_U", [128, NC_], F32)
    Cv = P.sb("lo_C", [128, NC_], F32)
    P.op("pool", "memset", w=["lo_U"], ap=U[:, :], constant=0.0)
    colof = lambda t: t + 1 if t < TS else (t + 2 if t < TS + TP else t + 3)
    rs_t = Rot(P, "lo_rs", 2, [128, 512], F32)
    for m in range(12):
        kind = m // 4
        mm = m % 4
        for tt in range(T // 512):
            ps, pk = psg.next()
            proj_fm(ps, pk, 768 + m * 128, 128, tt)
            pieces = [(0, 512)] if tt < 8 else [(0, 256), (256, 512)]
            for (a0, a1) in pieces:
                c_ = colof(tt * 512 + a0)
                evac(U[:, c_:c_ + (a1 - a0)], "lo_U", ps[:, a0:a1], pk)
        n = NC_ - 2
        P.op("act", "activation", r=["lo_U", "lowc"], w=["lo_C"], out=Cv[:, 1:1 + n], in_=U[:, 0:n], func=AF.Identity, scale=wc[:, 0, m:m + 1])
        P.op("dve", "scalar_tensor_tensor", r=["lo_U", "lo_C", "lowc"], w=["lo_C"], out=Cv[:, 1:1 + n], in0=U[:, 1:1 + n], scalar=wc[:, 1, m:m + 1], in1=Cv[:, 1:1 + n],
             op0=ALU.mult, op1=ALU.add)
        P.op("dve", "scalar_tensor_tensor", r=["lo_U", "lo_C", "lowc"], w=["lo_C"], out=Cv[:, 1:1 + n], in0=U[:, 2:2 + n], scalar=wc[:, 2, m:m + 1], in1=Cv[:, 1:1 + n],
             op0=ALU.mult, op1=ALU.add)
        P.op("act", "activation", r=["lo_C"], w=["lo_C"], out=Cv[:, 1:1 + n], in_=Cv[:, 1:1 + n], func=AF.Silu)
        for tt in range(T // 512):
            pieces = [(0, 512)] if tt < 8 else [(0, 256), (256, 512)]
            tok = slice(tt * 512, (tt + 1) * 512)
            xn, xnk = sf32.next()
            for (a0, a1) in pieces:
                c_ = colof(tt * 512 + a0); w_ = a1 - a0
                if kind == 2:
                    P.op("act", "activation", r=["lo_C"], w=[xnk], out=xn[:, a0:a1], in_=Cv[:, c_:c_ + w_], func=AF.Identity)
                else:
                    sq, sqk = sf32.next()
                    P.op("act", "activation", r=["lo_C"], w=[sqk], out=sq[:, 0:w_], in_=Cv[:, c_:c_ + w_], func=AF.Square)
                    pn, pnk = psg.next()
                    P.op("pe", "matmul", r=[sqk, "lo_bd"], w=[pnk], out=pn[:, 0:w_], lhsT=bd[:, :], rhs=sq[:, 0:w_], start=True, stop=True)
                    rs, rsk = rs_t.next()
                    P.op("act", "activation", r=[pnk, "epsb"], w=[rsk], out=rs[:, 0:w_], in_=pn[:, 0:w_], func=AF.Ln, bias=C.epsb[:, 0:1])
                    P.op("act", "activation", r=[rsk], w=[rsk], out=rs[:, 0:w_], in_=rs[:, 0:w_], func=AF.Exp, scale=-0.5)
                    P.op("dve", "scalar_tensor_tensor", r=["lo_C", rsk], w=[xnk], out=xn[:, a0:a1], in0=Cv[:, c_:c_ + w_], scalar=(0.125 if kind == 0 else 1.0), in1=rs[:, 0:w_],
                         op0=ALU.mult, op1=ALU.mult)
            if kind < 2:
                st, sk = sbf.next()
                P.op("dve", "tensor_copy", r=[xnk], w=[sk], out=st[:, :], in_=xn[:, :])
                P.dma(S["qnT" if kind == 0 else "knT"][mm * 128:(mm + 1) * 128, tok], st[:, :], r=[sk], w=["qknT"])
            if kind >= 1:
                pT, pTk = psT.next()
                for u in range(4):
                    P.op("pe", "transpose", r=[xnk, "ident"], w=[pTk], out=pT[:, u * 128:(u + 1) * 128], in_=xn[:, u * 128:(u + 1) * 128], identity=C.ident[:, :])
                tk_, tkk = sf32.next()
                evac(tk_[:, :], tkk, pT[:, :], pTk)
                dstn = "kn_tok" if kind == 1 else "v_tok"
                P.dma(S[dstn][tok, mm * 128:(mm + 1) * 128].rearrange("(u p) f -> p u f", p=128), tk_[:, :].rearrange("p (u f) -> p u f", f=128), r=[tkk], w=[dstn])
    P.end()


def attn_finish(C, R, pso_pair, pkeys, nq, oT_t, oTk, col0, esink=None):
    P = C.P
    rd, rdk = R.rden.next()
    ot, otk = R.o.next()
    for b in range(2):
        v = pso_pair[b][0:nq, 0:260].rearrange("p (h e) -> p h e", e=65)
        if esink is not None:
            P.op("dve", "tensor_tensor", r=[pkeys[b], "esink"], w=[rdk], out=rd[0:nq, 4 * b:4 * b + 4], in0=v[:, :, 64], in1=esink[0:nq, 4 * b:4 * b + 4], op=ALU.add)
            P.op("dve", "reciprocal", r=[rdk], w=[rdk], out=rd[0:nq, 4 * b:4 * b + 4], in_=rd[0:nq, 4 * b:4 * b + 4])
        else:
            P.op("dve", "reciprocal", r=[pkeys[b]], w=[rdk], out=rd[0:nq, 4 * b:4 * b + 4], in_=v[:, :, 64])
        P.op("dve", "tensor_tensor", r=[pkeys[b], rdk], w=[otk], out=ot[0:nq, 256 * b:256 * (b + 1)].rearrange("p (h d) -> p h d", d=64),
             in0=v[:, :, 0:64], in1=_bc(rd[0:nq, 4 * b:4 * b + 4].unsqueeze(2), [nq, 4, 64]), op=ALU.mult)
    pT, pTk = R.ps_T.next()
    for m in range(4):
        P.op("pe", "transpose", r=[otk, "identb"], w=[pTk], out=pT[:, m, 0:nq], in_=ot[0:nq, m * 128:(m + 1) * 128], identity=C.identb[0:nq, 0:nq])
    P.op("act", "activation", r=[pTk], w=[oTk], out=oT_t[:, :, col0:col0 + nq], in_=pT[:, :, 0:nq], func=AF.Identity)


def attn_odd(C, j):
    P, nc, I, S, O = C.P, C.nc, C.I, C.S, C.O
    P.begin()
    R = attn_setup(C, 128)
    kT = P.sb("ao_kT", [64, 2, T], BF16)
    V = P.sb("ao_V", [128, NT, 2 * 65], BF16)
    for g in range(2):
        P.dma(kT[:, g, :], S["qkC"][512 + g * 64:512 + (g + 1) * 64, :], r=["qkC"], w=["ao_kT"])
    for n0 in range(0, NT, 4):
        P.dma(V[:, n0:n0 + 4, :], S["vC"][n0 * 128:(n0 + 4) * 128, :].rearrange("(n p) f -> p n f", p=128), r=["vC"], w=["ao_V"])
    ckT = P.sb("ao_ckT", [64, 2, 256], BF16)
    cV = P.sb("ao_cV", [128, 2, 2 * 65], BF16)
    ctmp = P.sb("ao_ctmp", [128, 2, 2, 64], F32)
    ctmp2 = P.sb("ao_ctmp2", [128, 2, 2, 64], F32)
    ps_m = P.ps("ao_psm", [128, 512], F32)
    for half in range(2):
        for g in range(2):
            P.dma(ctmp[:, half, g, :], I["cck"][j, g, half * 128:(half + 1) * 128, :], w=["ao_ctmp"])
            P.dma(ctmp2[:, half, g, :], I["ccv"][j, g, half * 128:(half + 1) * 128, :], w=["ao_ctmp2"])
    for half in range(2):
        for g in range(2):
            P.op("pe", "transpose", r=["ao_ctmp", "ident"], w=["ao_psm"], out=ps_m[0:64, (half * 2 + g) * 128:(half * 2 + g + 1) * 128],
                 in_=ctmp[:, half, g, :], identity=C.ident[:, :])
    for half in range(2):
        P.op("dve", "tensor_copy", r=["ao_psm"], w=["ao_ckT"], out=ckT[:, :, half * 128:(half + 1) * 128],
             in_=ps_m[0:64, half * 256:(half + 1) * 256].rearrange("p (g t) -> p g t", t=128))
    P.op("pool", "memset", w=["ao_cV"], ap=cV[:, :, :], constant=1.0)
    P.op("dve", "tensor_copy", r=["ao_ctmp2"], w=["ao_cV"], out=cV[:, :, :].rearrange("p a (g e) -> p a g e", e=65)[:, :, :, 0:64], in_=ctmp2[:, :, :, :])
    esink = P.sb("ao_esink", [128, 8], F32)
    P.dma(esink[:, :], I["c_sink"][j:j + 1, :].partition_broadcast(128), w=["esink"])
    P.op("act", "activation", r=["esink"], w=["esink"], out=esink[:, :], in_=esink[:, :], func=AF.Exp)
    tri = P.sb("ao_tri", [128, 2, 128], F32)
    P.dma(tri[:, 0, :], I["c_tri"][0], w=["ao_tri"])
    P.dma(tri[:, 1, :], I["c_tri"][1], w=["ao_tri"])

    qblk = Rot(P, "ao_q", 2, [64, 8, 512], BF16)
    Pt = Rot(P, "ao_Pt", 10, [128, 4, 128], BF16)
    pso_i = [0]

    def unit(q, qk, qcol, chunks, oT_t, oTk, ocol):
        pso_pair = (R.pso[2 * (pso_i[0] % 2)], R.pso[2 * (pso_i[0] % 2) + 1])
        pkeys = (f"at_pso{2 * (pso_i[0] % 2)}", f"at_pso{2 * (pso_i[0] % 2) + 1}")
        pso_i[0] += 1
        allpts = []
        for g in range(2):
            pts = []
            for (kfn, kkey, vfn, vkey, mask) in chunks:
                ps, psk = R.ps_s.next()
                P.op("pe", "matmul", r=[kkey, qk], w=[psk], out=ps[:, :], lhsT=kfn(g), rhs=q[:, 4 * g:4 * g + 4, qcol:qcol + 128], start=True, stop=True)
                pt, ptk = Pt.next()
                if mask is None:
                    P.op("act", "activation", r=[psk], w=[ptk], out=pt[:, :, :], in_=ps[:, :].rearrange("p (h q) -> p h q", q=128), func=AF.Exp, scale=0.125)
                else:
                    E, Ek = R.E.next()
                    P.op("act", "activation", r=[psk], w=[Ek], out=E[:, :], in_=ps[:, :], func=AF.Exp, scale=0.125)
                    P.op("dve", "tensor_tensor", r=[Ek, "ao_tri"], w=[ptk], out=pt[:, :, :], in0=E[:, :].rearrange("p (h q) -> p h q", q=128),
                         in1=_bc(tri[:, mask:mask + 1, :], [128, 4, 128]), op=ALU.mult)
                pts.append((pt, ptk, vfn, vkey))
            allpts.append(pts)
        for g in range(2):
            pts = allpts[g]
            po = pso_pair[g]; pok = pkeys[g]
            for hh in range(4):
                for x, (pt, ptk, vfn, vkey) in enumerate(pts):
                    P.op("pe", "matmul", r=[ptk, vkey], w=[pok], out=po[:, hh * 65:(hh + 1) * 65], lhsT=pt[:, hh, :], rhs=vfn(g),
                         start=(x == 0), stop=(x == len(pts) - 1))
        attn_finish(C, R, pso_pair, pkeys, 128, oT_t, oTk, ocol, esink=esink)

    def kchunk(tok0):
        return (lambda g, tok0=tok0: kT[:, g, tok0:tok0 + 128])

    def vchunk(n):
        return (lambda g, n=n: V[:, n, g * 65:(g + 1) * 65])

    ctx_chunks = [((lambda g, x=x: ckT[:, g, x * 128:(x + 1) * 128]), "ao_ckT", (lambda g, x=x: cV[:, x, g * 65:(g + 1) * 65]), "ao_cV", None) for x in range(2)]
    for blk in range(8):
        q, qk = qblk.next()
        for h in range(8):
            P.dma(q[:, h, :], S["qkC"][h * 64:(h + 1) * 64, blk * 512:(blk + 1) * 512], r=["qkC"], w=[qk])
        oT_t, oTk = R.oT.next()
        for u in range(4):
            n = blk * 4 + u
            chunks = []
            if n > 0:
                chunks.append((kchunk((n - 1) * 128), "ao_kT", vchunk(n - 1), "ao_V", 1))
            chunks.append((kchunk(n * 128), "ao_kT", vchunk(n), "ao_V", None))
            if n < 31:
                chunks.append((kchunk((n + 1) * 128), "ao_kT", vchunk(n + 1), "ao_V", 0))
            chunks += ctx_chunks
            unit(q, qk, u * 128, chunks, oT_t, oTk, u * 128)
        P.dma(S["oT"][0:512, blk * 512:(blk + 1) * 512].rearrange("(m p) t -> p m t", p=128), oT_t[:, :, :], r=[oTk], w=["oT_a"])
    q, qk = qblk.next()
    for h in range(8):
        P.dma(q[:, h, :], S["qkC"][h * 64:(h + 1) * 64, TS:TS + 512], r=["qkC"], w=[qk])
    oT_t, oTk = R.oT.next()
    for sq in range(2):
        for qb in range(2):
            col = sq * 256 + qb * 128
            chunks = [(kchunk(TS + sq * 256 + x * 128), "ao_kT", vchunk((TS + sq * 256 + x * 128) // 128), "ao_V", None) for x in range(2)]
            unit(q, qk, col, chunks, oT_t, oTk, col)
    P.dma(S["oT"][0:512, TS:TS + 512].rearrange("(m p) t -> p m t", p=128), oT_t[:, :, :], r=[oTk], w=["oT_a"])
    P.end()


def zipper(lists):
    lists = [l for l in lists if l]
    pos = [0] * len(lists)
    out = []
    total = sum(len(l) for l in lists)
    while len(out) < total:
        best = None
        for i, l in enumerate(lists):
            if pos[i] < len(l):
                f = pos[i] / len(l)
                if best is None or f < best[0]:
                    best = (f, i)
        i = best[1]
        out.append(lists[i][pos[i]])
        pos[i] += 1
    return out


def capture(P, fn):
    saved = P.ops
    P.ops = []
    ret = fn()
    out = P.ops
    P.ops = saved
    return out, ret


F32R = mybir.dt.float32r


def delta_scan(C, j):
    P, nc, I, S, O = C.P, C.nc, C.I, C.S, C.O
    P.begin()
    BL = 128
    CPB = BL // 64
    dm = P.sb("ds_dm", [64, 4, 64], F32)
    for q_ in range(4):
        P.dma(dm[:, q_, :], I["c_dmask"][q_], w=["ds_dm"])
    ones8 = P.sb("ds_ones8", [8, 64], F32)
    P.op("pool", "memset", w=["ds_ones8"], ap=ones8[:, :], constant=1.0)
    id64 = P.sb("ds_id64", [64, 64], F32)
    P.dma(id64[:, :], I["c_ident"][0:64, 0:64], w=["ds_id64"])
    PSP = [Rot(P, f"ds_psp{d}_", 3, [128, 512], F32, psum=True) for d in range(2)]
    PSR = [Rot(P, f"ds_psr{d}_", 1, [128, 512], F32, psum=True) for d in range(2)]
    DP = F32

    def mk(name, n, shape, dt):
        return [Rot(P, f"ds_{name}{d}_", n, shape, dt) for d in range(2)]
    knTb = mk("knT", 1, [64, 8, BL], BF16); qnTb = mk("qnT", 1, [64, 8, BL], BF16)
    kntok = mk("kntok", 1, [64, CPB, 512], F32); vtok = mk("vtok", 1, [64, CPB, 512], F32)
    dscb = mk("dsc", 1, [64, CPB, 40], F32)
    gexb = mk("gex", 1, [8, 8, BL], F32); hexb = mk("hex", 1, [8, 8, BL], F32)
    ost = mk("o", 2, [64, CPB, 512], F32)
    D1 = mk("D1", 1, [64, 8, 64], F32); D2 = mk("D2", 1, [64, 8, 64], F32); D3 = mk("D3", 1, [64, 8, 64], F32)
    aqk = mk("aqk", 2, [64, 8, 64], BF16)
    Npw = mk("N", 2, [64, 8, 64], DP); Mpw = mk("M", 2, [64, 8, 64], DP)
    QR = F32R if os.environ.get("MK_QR", "0") == "1" else F32
    Qf = mk("Qf", 1, [64, 8, 64], F32); Qb = mk("Qb", 2, [64, 8, 64], QR)
    Mr = mk("Mr", 2, [64, 8, 64], QR)
    vbt = mk("vb", 1, [64, 8, 64], QR); kbg = mk("kbg", 1, [64, 8, 64], QR); kend = mk("kend", 2, [64, 8, 64], BF16)
    egb = mk("eg", 1, [64, 8, 64], F32); qg = mk("qg", 2, [64, 8, 64], BF16)
    wval = mk("wval", 2, [64, 8, 64], F32); kcT = mk("kcT", 2, [64, 8, 64], F32); vnew = mk("vnew", 1, [64, 8, 64], BF16)
    dl = mk("dl", 2, [64, 8], F32)
    Sst = [P.sb(f"ds_S{d}", [64, 8, 64], F32) for d in range(2)]
    Sbf = [P.sb(f"ds_Sbf{d}", [64, 8, 64], BF16) for d in range(2)]
    Sr = [P.sb(f"ds_Sr{d}", [64, 8, 64], F32) for d in range(2)]
    Stmp = [P.sb(f"ds_St{d}", [64, 8, 64], F32) for d in range(2)]
    evc = [0]

    def evac(dst, dkey, src, skey):
        evc[0] += 1
        if evc[0] % 2:
            P.op("act", "activation", r=[skey], w=[dkey], out=dst, in_=src, func=AF.Identity)
        else:
            P.op("dve", "tensor_copy", r=[skey], w=[dkey], out=dst, in_=src)

    def v3(ps):
        return ps[0:64, :].rearrange("p (h c) -> p h c", c=64)

    def mm8(ps, psk, lhs_fn, lkeys, rhs_fn, rkeys):
        for h in range(8):
            P.op("pe", "matmul", r=list(lkeys) + list(rkeys), w=[psk], out=ps[0:64, h * 64:(h + 1) * 64], lhsT=lhs_fn(h), rhs=rhs_fn(h), start=True, stop=True)

    cur = [None, None]
    curo = [None, None]

    def prep(d, t0, nch, c):
        blk = c // CPB; cc = c % CPB
        first_in_blk = (cc == 0) if d == 0 else (cc == CPB - 1)
        tb = t0 + blk * BL
        if first_in_blk:
            kn, knk = knTb[d].next(); qn, qnk = qnTb[d].next(); kt, ktk = kntok[d].next(); vt, vtk = vtok[d].next()
            sc, sck = dscb[d].next(); gx, gxk = gexb[d].next(); hx, hxk = hexb[d].next()
            for h in range(8):
                P.dma(kn[:, h, :], S["knT"][h * 64:(h + 1) * 64, tb:tb + BL], r=["knT"], w=[knk])
                P.dma(qn[:, h, :], S["qnT"][h * 64:(h + 1) * 64, tb:tb + BL], r=["qnT"], w=[qnk])
            P.dma(kt[:, :, :], S["kn_tok"][tb:tb + BL, :].rearrange("(c p) f -> p c f", p=64), r=["kn_tok"], w=[ktk])
            P.dma(vt[:, :, :], S["v_tok"][tb:tb + BL, :].rearrange("(c p) f -> p c f", p=64), r=["v_tok"], w=[vtk])
            P.dma(sc[:, :, :], S["dsc"][d, tb:tb + BL, :].rearrange("(c p) f -> p c f", p=64), r=["dsc"], w=[sck])
            for k8 in range(8):
                P.dma(gx[:, k8, :], S["Gexp"][d, :, k8, tb:tb + BL], r=["Gexp"], w=[gxk])
                P.dma(hx[:, k8, :], S["Hexp"][d, :, k8, tb:tb + BL], r=["Hexp"], w=[hxk])
            cur[d] = (kn, knk, qn, qnk, kt, ktk, vt, vtk, sc, sck, gx, gxk, hx, hxk)
        kn, knk, qn, qnk, kt, ktk, vt, vtk, sc, sck, gx, gxk, hx, hxk = cur[d]
        cs = slice(cc * 64, (cc + 1) * 64)
        Gs = sc[:, cc, 0:8]; Hs = sc[:, cc, 8:16]; Bs = sc[:, cc, 16:24]; BEs = sc[:, cc, 24:32]; ELs = sc[:, cc, 32:40]
        m_incl, m_strict, m_strictT = (0, 1, 3) if d == 0 else (2, 3, 1)
        lastc = 63 if d == 0 else 0
        pG, pGk = PSP[d].next()
        P.op("pe", "matmul", r=["ds_ones8", gxk], w=[pGk], out=pG[0:64, :], lhsT=ones8[:, :], rhs=gx[:, :, cs], start=True, stop=True)
        pH, pHk = PSP[d].next()
        P.op("pe", "matmul", r=["ds_ones8", hxk], w=[pHk], out=pH[0:64, :], lhsT=ones8[:, :], rhs=hx[:, :, cs], start=True, stop=True)
        pKK, pKKk = PSP[d].next()
        mm8(pKK, pKKk, lambda h: kn[:, h, cs], [knk], lambda h: kn[:, h, cs], [])
        d1, d1k = D1[d].next(); d2, d2k = D2[d].next(); d3, d3k = D3[d].next()
        Gs_bc = _bc(Gs.unsqueeze(2), [64, 8, 64]); Hs_bc = _bc(Hs.unsqueeze(2), [64, 8, 64])
        eg, egk = egb[d].next(); qgt, qgk = qg[d].next(); dlt, dlk = dl[d].next()
        P.op("dve", "tensor_tensor", r=[pGk, sck], w=[d1k], out=d1[:, :, :], in0=v3(pG), in1=Gs_bc, op=ALU.subtract)
        P.op("dve", "scalar_tensor_tensor", r=[pGk, sck], w=[d3k], out=d3[:, :, :], in0=v3(pG), scalar=-1.0, in1=Hs_bc, op0=ALU.mult, op1=ALU.add)
        P.op("act", "activation", r=[pGk], w=[egk], out=eg[:, :, :], in_=v3(pG), func=AF.Exp)
        P.op("dve", "tensor_tensor", r=[pHk, sck], w=[d2k], out=d2[:, :, :], in0=v3(pH), in1=Gs_bc, op=ALU.subtract)
        pQK, pQKk = PSP[d].next()
        mm8(pQK, pQKk, lambda h: kn[:, h, cs], [knk], lambda h: qn[:, h, cs], [qnk])
        P.op("dve", "tensor_tensor", r=[qnk, egk], w=[qgk], out=qgt[:, :, :], in0=qn[:, :, cs], in1=eg[:, :, :], op=ALU.mult)
        P.op("act", "activation", r=[egk], w=[dlk], out=dlt[:, :], in_=eg[:, :, lastc], func=AF.Identity)
        P.op("dve", "tensor_tensor", r=[d1k, "ds_dm"], w=[d1k], out=d1[:, :, :], in0=d1[:, :, :], in1=_bc(dm[:, m_incl:m_incl + 1, :], [64, 8, 64]), op=ALU.add)
        P.op("act", "activation", r=[d1k], w=[d1k], out=d1[:, :, :], in_=d1[:, :, :], func=AF.Exp)
        aq, aqk_ = aqk[d].next()
        P.op("dve", "tensor_tensor", r=[pQKk, d1k], w=[aqk_], out=aq[:, :, :], in0=v3(pQK), in1=d1[:, :, :], op=ALU.mult)
        P.op("dve", "tensor_tensor", r=[d2k, "ds_dm"], w=[d2k], out=d2[:, :, :], in0=d2[:, :, :], in1=_bc(dm[:, m_strict:m_strict + 1, :], [64, 8, 64]), op=ALU.add)
        P.op("act", "activation", r=[d2k], w=[d2k], out=d2[:, :, :], in_=d2[:, :, :], func=AF.Exp)
        N1, N1k = Npw[d].next()
        P.op("dve", "scalar_tensor_tensor", r=[pKKk, d2k], w=[N1k], out=N1[:, :, :], in0=v3(pKK), scalar=-1.0, in1=d2[:, :, :], op0=ALU.mult, op1=ALU.mult)
        P.op("dve", "tensor_tensor", r=[d3k, "ds_dm"], w=[d3k], out=d3[:, :, :], in0=d3[:, :, :], in1=_bc(dm[:, m_strictT:m_strictT + 1, :], [64, 8, 64]), op=ALU.add)
        P.op("act", "activation", r=[d3k], w=[d3k], out=d3[:, :, :], in_=d3[:, :, :], func=AF.Exp)
        M1, M1k = Mpw[d].next()
        P.op("dve", "scalar_tensor_tensor", r=[pKKk, d3k], w=[M1k], out=M1[:, :, :], in0=v3(pKK), scalar=-1.0, in1=d3[:, :, :], op0=ALU.mult, op1=ALU.mult)
        vb_, vbk = vbt[d].next(); kb_, kbk = kbg[d].next(); ke_, kek = kend[d].next()
        P.op("dve", "tensor_tensor", r=[vtk, sck], w=[vbk], out=vb_[:, :, :], in0=vt[:, cc, :].rearrange("p (h v) -> p h v", v=64), in1=_bc(Bs.unsqueeze(2), [64, 8, 64]), op=ALU.mult)
        P.op("dve", "tensor_tensor", r=[ktk, sck], w=[kbk], out=kb_[:, :, :], in0=kt[:, cc, :].rearrange("p (h v) -> p h v", v=64), in1=_bc(BEs.unsqueeze(2), [64, 8, 64]), op=ALU.mult)
        P.op("dve", "tensor_tensor", r=[ktk, sck], w=[kek], out=ke_[:, :, :], in0=kt[:, cc, :].rearrange("p (h v) -> p h v", v=64), in1=_bc(ELs.unsqueeze(2), [64, 8, 64]), op=ALU.mult)
        qf, qfk = Qf[d].next(); qb, qbk = Qb[d].next()
        P.op("dve", "tensor_tensor", r=[N1k, "ds_id64"], w=[qfk], out=qf[:, :, :], in0=N1[:, :, :], in1=_bc(id64[:, :].unsqueeze(1), [64, 8, 64]), op=ALU.add)
        P.op("act", "activation", r=[qfk], w=[qbk], out=qb[:, :, :], in_=qf[:, :, :], func=AF.Identity)
        Nc, Nck, Mc, Mck = N1, N1k, M1, M1k
        for lev in range(5):
            Mn, Mnk = Mpw[d].next()
            pM, pMk = PSP[d].next()
            mm8(pM, pMk, lambda h: Nc[:, h, :], [Nck], lambda h: Mc[:, h, :], [Mck])
            if lev < 4:
                Nn, Nnk = Npw[d].next()
                pN, pNk = PSP[d].next()
                mm8(pN, pNk, lambda h: Mc[:, h, :], [Mck], lambda h: Nc[:, h, :], [Nck])
                evac(Nn[:, :, :], Nnk, v3(pN), pNk)
            evac(Mn[:, :, :], Mnk, v3(pM), pMk)
            if QR == F32R:
                mr, mrk = Mr[d].next()
                evac(mr[:, :, :], mrk, v3(pM), pMk)
            else:
                mr, mrk = Mn, Mnk
            pQ, pQk = PSP[d].next()
            mm8(pQ, pQk, lambda h: mr[:, h, :], [mrk], lambda h: qb[:, h, :], [qbk])
            P.op("dve", "tensor_tensor", r=[pQk, qfk], w=[qfk], out=qf[:, :, :], in0=qf[:, :, :], in1=v3(pQ), op=ALU.add)
            qb, qbk = Qb[d].next()
            P.op("act", "activation", r=[qfk], w=[qbk], out=qb[:, :, :], in_=qf[:, :, :], func=AF.Identity)
            Mc, Mck = Mn, Mnk
            if lev < 4:
                Nc, Nck = Nn, Nnk
        pW, pWk = PSP[d].next()
        mm8(pW, pWk, lambda h: qb[:, h, :], [qbk], lambda h: vb_[:, h, :], [vbk])
        wv, wvk = wval[d].next()
        evac(wv[:, :, :], wvk, v3(pW), pWk)
        pK, pKk = PSP[d].next()
        mm8(pK, pKk, lambda h: kb_[:, h, :], [kbk], lambda h: qb[:, h, :], [qbk])
        kc, kck = kcT[d].next()
        evac(kc[:, :, :], kck, v3(pK), pKk)
        return dict(wv=wv, wvk=wvk, kc=kc, kck=kck, qgt=qgt, qgk=qgk, aq=aq, aqk=aqk_, ke=ke_, kek=kek, dlt=dlt, dlk=dlk)

    def rec(d, t0, nch, c, H):
        blk = c // CPB; cc = c % CPB
        first_in_blk = (cc == 0) if d == 0 else (cc == CPB - 1)
        last_in_blk = (cc == CPB - 1) if d == 0 else (cc == 0)
        tb = t0 + blk * BL
        if first_in_blk:
            curo[d] = ost[d].next()
        oo, ook = curo[d]
        Sk = f"ds_S{d}"; Sbk = f"ds_Sbf{d}"; Srk = f"ds_Sr{d}"
        pV, pVk = PSR[d].next()
        mm8(pV, pVk, lambda h: H["kc"][:, h, :], [H["kck"]], lambda h: Sr[d][:, h, :], [Srk])
        vn, vnk = vnew[d].next()
        P.op("dve", "tensor_tensor", r=[H["wvk"], pVk], w=[vnk], out=vn[:, :, :], in0=H["wv"][:, :, :], in1=v3(pV), op=ALU.subtract)
        pO, pOk = PSR[d].next()
        for h in range(8):
            P.op("pe", "matmul", r=[H["qgk"], Sbk], w=[pOk], out=pO[0:64, h * 64:(h + 1) * 64], lhsT=H["qgt"][:, h, :], rhs=Sbf[d][:, h, :], start=True, stop=False)
            P.op("pe", "matmul", r=[H["aqk"], vnk], w=[pOk], out=pO[0:64, h * 64:(h + 1) * 64], lhsT=H["aq"][:, h, :], rhs=vn[:, h, :], start=False, stop=True)
        P.op("act", "activation", r=[pOk], w=[ook], out=oo[:, cc, :], in_=pO[0:64, :], func=AF.Identity)
        pS, pSk = PSR[d].next()
        mm8(pS, pSk, lambda h: H["ke"][:, h, :], [H["kek"]], lambda h: vn[:, h, :], [vnk])
        P.op("dve", "tensor_tensor", r=[Sk, H["dlk"]], w=[f"ds_St{d}"], out=Stmp[d][:, :, :], in0=Sst[d][:, :, :], in1=_bc(H["dlt"][:, :].unsqueeze(2), [64, 8, 64]), op=ALU.mult)
        P.op("dve", "tensor_tensor", r=[f"ds_St{d}", pSk], w=[Sk], out=Sst[d][:, :, :], in0=Stmp[d][:, :, :], in1=v3(pS), op=ALU.add)
        P.op("act", "activation", r=[Sk], w=[Sbk], out=Sbf[d][:, :, :], in_=Sst[d][:, :, :], func=AF.Identity)
        P.op("act", "activation", r=[Sk], w=[Srk], out=Sr[d][:, :, :], in_=Sst[d][:, :, :], func=AF.Identity)
        if last_in_blk:
            P.dma(S["ofb"][d, tb:tb + BL, :].rearrange("(c p) f -> p c f", p=64), oo[:, :, :], r=[ook], w=[f"ofb{d}"])

    for sqi, (t0, tl) in enumerate(SEQS):
        nch = tl // 64
        for d in range(2):
            if sqi == 0:
                for h in range(8):
                    P.dma(Sst[d][:, h, :], I["sd"][j, d, h, :, :], w=[f"ds_S{d}"])
            else:
                P.op("pool", "memset", w=[f"ds_S{d}"], ap=Sst[d][:, :, :], constant=0.0)
            P.op("act", "activation", r=[f"ds_S{d}"], w=[f"ds_Sbf{d}"], out=Sbf[d][:, :, :], in_=Sst[d][:, :, :], func=AF.Identity)
            P.op("pool", "tensor_copy", r=[f"ds_S{d}"], w=[f"ds_Sr{d}"], out=Sr[d][:, :, :], in_=Sst[d][:, :, :])
        Hprev = [None, None]
        chunk_of = lambda step, d: step if d == 0 else nch - 1 - step
        for step in range(nch + 1):
            lists = []
            Hnew = [None, None]
            for d in range(2):
                if step < nch:
                    ops, Hnew[d] = capture(P, lambda d=d: prep(d, t0, nch, chunk_of(step, d)))
                    lists.append(ops)
                if step > 0:
                    ops, _ = capture(P, lambda d=d: rec(d, t0, nch, chunk_of(step - 1, d), Hprev[d]))
                    lists.append(ops)
            P.ops.extend(zipper(lists))
            Hprev = Hnew
        if sqi > 0:
            for d in range(2):
                for h in range(8):
                    P.dma(O["nsd"][sqi - 1, j, d, h, :, :], Sst[d][:, h, :], r=[f"ds_S{d}"], w=["o_nsd"], isout=True)
    P.end()
```

```python
import contextlib
import os
import numpy as np
import ml_dtypes
import concourse.bass as bass
import concourse.mybir as mybir
from concourse.bass_utils import run_bass_kernel_spmd

F32 = mybir.dt.float32
BF16 = mybir.dt.bfloat16
AF = mybir.ActivationFunctionType
ALU = mybir.AluOpType
AX = mybir.AxisListType

NCORES = 8
D = 1024
TS = 4096
TP = 256
T = TS + 2 * TP
NT = T // 128
DEPTH = 4
DFF = 2816
EV_IN = 3616
OD_IN = 2848
EPS = 1e-6
SEQS = [(0, TS), (TS, TP), (TS + TP, TP)]

ENGS = ("pe", "act", "dve", "pool", "sp")
NDMA = 24


class Op:
    __slots__ = ("eng", "fn", "r", "w", "dma", "id", "waits", "sig", "dslot", "dval", "prev_dma")


class Prog:
    def __init__(self, nc):
        self.nc = nc
        self.es = contextlib.ExitStack()
        self.ops = []
        self.nid = 0
        self.engobj = {"pe": nc.tensor, "act": nc.scalar, "dve": nc.vector, "pool": nc.gpsimd, "sp": nc.sync}
        self.sem = {e: self.es.enter_context(nc.semaphore("sem_" + e)) for e in ENGS}
        self.dsem = {e: [self.es.enter_context(nc.semaphore(f"dsem_{e}_{i}")) for i in range(NDMA)]
                     for e in ("sp", "act", "pool")}
        self.sigcnt = {e: 0 for e in ENGS}
        self.dcnt = {e: 0 for e in ("sp", "act", "pool")}
        self.dhist = {e: [] for e in ("sp", "act", "pool")}
        self.last_w = {}
        self.readers = {}
        self.seen = {e: {} for e in ENGS}
        self.out_dmas = []
        self.phase_allocs = None
        self.psum_keys = set(["ps_t", "ps_fm", "ps_row", "mh_ps0", "mh_ps1", "lfm_pst", "l2psT0", "l2psT1", "ae_psm", "ao_psm",
                              "at_pso0", "at_pso1", "at_pso2", "at_pso3", "gs_psatt0", "gs_psatt1", "gs_pso0", "gs_pso1", "gs_pss0", "gs_pss1"])

    def begin(self, name=None):
        import inspect
        self.phase_name = name or inspect.stack()[1].function
        self.phase_allocs = contextlib.ExitStack()

    def uname(self, name):
        self.uid = getattr(self, "uid", 0) + 1
        return f"{name}_u{self.uid}"

    def sb(self, name, shape, dt):
        return self.phase_allocs.enter_context(self.nc.sbuf_tensor(self.uname(name), list(shape), dt))

    def ps(self, name, shape, dt=F32):
        return self.phase_allocs.enter_context(self.nc.psum_tensor(self.uname(name), list(shape), dt))

    def sub_begin(self):
        self._outer = self.phase_allocs
        self.phase_allocs = contextlib.ExitStack()

    def sub_end(self):
        self.flush(fence=True)
        self.phase_allocs.close()
        self.phase_allocs = self._outer

    def end(self):
        self.flush(fence=True)
        self.phase_allocs.close()
        self.phase_allocs = None

    def op(self, eng, method, r=(), w=(), dma=False, isout=False, **kw):
        o = Op()
        o.eng = eng; o.fn = (method, kw); o.r = tuple(r); o.w = tuple(w); o.dma = dma
        o.id = self.nid; self.nid += 1
        o.waits = None; o.sig = 0; o.dslot = None; o.dval = 0; o.prev_dma = None
        self.ops.append(o)
        if isout:
            self.out_dmas.append(o)
        return o

    def dma(self, out, in_, r=(), w=(), q="sp", isout=False, **kw):
        return self._dma(q, out, in_, r, w, isout, kw)

    def _dma(self, q, out, in_, r, w, isout, kw):
        o = Op()
        kk = dict(kw); kk["out"] = out; kk["in_"] = in_
        o.eng = q; o.fn = ("dma_start", kk); o.r = tuple(r); o.w = tuple(w); o.dma = True
        o.id = self.nid; self.nid += 1
        o.waits = None; o.sig = 0; o.dslot = None; o.dval = 0; o.prev_dma = None
        self.ops.append(o)
        if isout:
            self.out_dmas.append(o)
        return o

    def flush(self, fence=False, final=False):
        ops = self.ops
        self.ops = []
        for o in ops:
            deps = {}
            def add(p):
                if p is None or p is o:
                    return
                deps[p.id] = p
            for k in o.r:
                add(self.last_w.get(k))
                if k in self.psum_keys:
                    rd = self.readers.get(k)
                    if rd:
                        for kk, p in rd.items():
                            if kk != "dma" and kk != o.eng:
                                add(p)
            for k in o.w:
                add(self.last_w.get(k))
                rd = self.readers.get(k)
                if rd:
                    for kk, p in rd.items():
                        if kk == "dma":
                            for pp in p:
                                add(pp)
                        else:
                            add(p)
            if o.dma:
                q = o.eng
                n = self.dcnt[q]
                self.dcnt[q] += 1
                o.dslot = n % NDMA
                o.dval = 16 * (n // NDMA + 1)
                if n >= NDMA:
                    add(self.dhist[q][n - NDMA])
                self.dhist[q].append(o)
            o.waits = list(deps.values())
            for p in o.waits:
                if not p.dma:
                    if not (p.eng == "pe" and o.eng == "pe" and not o.dma):
                        p.sig = -1
            for k in o.w:
                self.last_w[k] = o
                self.readers[k] = {}
            for k in o.r:
                rd = self.readers.setdefault(k, {})
                if o.dma:
                    rd.setdefault("dma", []).append(o)
                else:
                    rd[o.eng] = o
        lastop = {}
        for o in ops:
            lastop[o.eng] = o
        if fence or final:
            for e, o in lastop.items():
                if not o.dma:
                    o.sig = -1
        for o in ops:
            if o.dma:
                continue
            if o.sig == -1:
                self.sigcnt[o.eng] += 1
                o.sig = self.sigcnt[o.eng]
        per_eng = {e: [] for e in ENGS}
        for o in ops:
            per_eng[o.eng].append(o)
        nc = self.nc
        fence_dmas = [(q, i) for q in self.dcnt for i in range(min(NDMA, self.dcnt[q]))]

        def emit_engine(ename):
            def body(eng):
                seen = self.seen[ename]
                for o in per_eng[ename]:
                    need = {}
                    for p in o.waits:
                        if p.dma:
                            key = ("d", p.eng, p.dslot)
                            if seen.get(key, 0) < p.dval:
                                seen[key] = p.dval
                                eng.wait_ge(self.dsem[p.eng][p.dslot], p.dval)
                        else:
                            if p.eng == "pe" and ename == "pe" and not o.dma:
                                continue
                            if p.sig > need.get(p.eng, 0):
                                need[p.eng] = p.sig
                    for pe_, v in need.items():
                        if seen.get(pe_, 0) < v:
                            seen[pe_] = v
                            eng.wait_ge(self.sem[pe_], v)
                    ins = getattr(eng, o.fn[0])(**o.fn[1])
                    if o.dma:
                        ins.then_inc(self.dsem[o.eng][o.dslot], 16)
                    elif o.sig > 0:
                        ins.then_inc(self.sem[o.eng], 1)
                if fence or final:
                    for e2 in ENGS:
                        v = self.sigcnt[e2]
                        if v > 0 and seen.get(e2, 0) < v:
                            seen[e2] = v
                            eng.wait_ge(self.sem[e2], v)
                    for q in self.dcnt:
                        n = self.dcnt[q]
                        for i in range(min(NDMA, n)):
                            last = 16 * ((n - 1 - i) // NDMA + 1)
                            key = ("d", q, i)
                            if seen.get(key, 0) < last:
                                seen[key] = last
                                eng.wait_ge(self.dsem[q][i], last)
            return body

        self.scope_i = getattr(self, "scope_i", 0) + 1
        with nc.named_scope(f"{getattr(self, 'phase_name', 'x')}_{self.scope_i}"), nc.Block() as block:
            block.tensor(emit_engine("pe"))
            block.scalar(emit_engine("act"))
            block.vector(emit_engine("dve"))
            block.gpsimd(emit_engine("pool"))
            block.sync(emit_engine("sp"))
        if fence or final:
            self.last_w = {}
            self.readers = {}


def _bc(ap, shape):
    return ap.broadcast_to(list(shape))


class Ctx:
    pass


def build_program(stage=99, dbg=None):
    nc = bass.Bass("TRN2", target_bir_lowering=False)
    P = Prog(nc)
    C = Ctx()
    C.nc = nc; C.P = P; C.stage = stage

    def din(name, shape, dt=F32):
        return nc.dram_tensor(name, list(shape), dt, kind="ExternalInput").ap()

    def dout(name, shape, dt=F32):
        return nc.dram_tensor(name, list(shape), dt, kind="ExternalOutput").ap()

    def dscr(name, shape, dt=F32):
        return nc.dram_tensor(name, list(shape), dt).ap()

    I = {}
    I["x"] = din("x", [T, D])
    I["cond"] = din("cond", [3, D])
    I["cak"] = din("cak", [2, 8, 256, 64]); I["cav"] = din("cav", [2, 8, 256, 64])
    I["sb"] = din("sb", [2, 2, 8, 64, 64])
    I["cck"] = din("cck", [2, 2, 256, 64]); I["ccv"] = din("ccv", [2, 2, 256, 64])
    I["sd"] = din("sd", [2, 2, 8, 64, 64])
    I["ada_w"] = din("ada_w", [DEPTH, D, 6 * D]); I["ada_b"] = din("ada_b", [DEPTH, 6 * D])
    I["norm1_g"] = din("norm1_g", [DEPTH, D]); I["norm2_g"] = din("norm2_g", [DEPTH, D])
    I["ffn_up"] = din("ffn_up", [DEPTH, D, 2 * DFF]); I["ffn_conv"] = din("ffn_conv", [DEPTH, 3, 2 * DFF])
    I["ffn_down"] = din("ffn_down", [DEPTH, DFF, D])
    I["ev_w_in"] = din("ev_w_in", [2, D, EV_IN]); I["ev_w_out"] = din("ev_w_out", [2, D, D])
    I["a_rpb"] = din("a_rpb", [2, 8, 15, 31]); I["b_w_g2"] = din("b_w_g2", [2, 2, 16, 512])
    I["b_b_g"] = din("b_b_g", [2, 2, 512]); I["b_norm_g"] = din("b_norm_g", [2, 512])
    I["od_w_in"] = din("od_w_in", [2, D, OD_IN]); I["od_w_out"] = din("od_w_out", [2, D, D])
    I["c_sink"] = din("c_sink", [2, 8]); I["d_conv"] = din("d_conv", [2, 3, 1536])
    I["d_a_log"] = din("d_a_log", [2, 2, 8]); I["d_dt_bias"] = din("d_dt_bias", [2, 2, 8])
    I["d_norm_g"] = din("d_norm_g", [2, 64]); I["final_g"] = din("final_g", [D])
    I["c_ident"] = din("c_ident", [128, 128])
    I["c_maskf"] = din("c_maskf", [128, 512])
    I["c_eoh"] = din("c_eoh", [32, 64, 64])
    I["c_tri"] = din("c_tri", [2, 128, 128])
    I["c_perm"] = din("c_perm", [128, 128])
    I["c_bd"] = din("c_bd", [128, 128])
    I["c_cos"] = din("c_cos", [128, TS])
    I["c_sin"] = din("c_sin", [128, TS])
    I["c_dmask"] = din("c_dmask", [4, 64, 64])
    C.I = I
    O = {}
    O["y"] = dout("y", [T, D])
    O["nak"] = dout("nak", [2, 2, 8, 256, 64]); O["nav"] = dout("nav", [2, 2, 8, 256, 64])
    O["nsb"] = dout("nsb", [2, 2, 2, 8, 64, 64])
    O["nck"] = dout("nck", [2, 2, 2, 256, 64]); O["ncv"] = dout("ncv", [2, 2, 2, 256, 64])
    O["nsd"] = dout("nsd", [2, 2, 2, 8, 64, 64])
    C.O = O
    C.taps = {}
    if dbg:
        for nm, shp, dt in dbg:
            O[nm] = dout(nm, shp, dt)
            C.taps[nm] = O[nm]

    def tap(name, src, keys, dst_view=None):
        if name in C.taps:
            dst = C.taps[name] if dst_view is None else dst_view(C.taps[name])
            P.dma(dst, src, r=keys, w=["o_" + name], isout=True)
    C.tap = tap

    def dump(dst, src, key):
        names = " ".join("abcdef"[:len(src.shape)])
        fs = src.rearrange(f"{names} -> ({names})")
        fd = dst.rearrange(f"{names} -> ({names})")
        n = fs.shape[0]
        per = 128 * 8192
        o = 0
        while o < n:
            m = min(per, n - o)
            assert m % 128 == 0
            P.dma(fd[o:o + m].rearrange("(p f) -> p f", p=128), fs[o:o + m].rearrange("(p f) -> p f", p=128), r=[key], w=["o_dbg_" + key], isout=True)
            o += m
    C.dump = dump
    S = {}
    S["gates"] = dscr("s_gates", [DEPTH, 3, 2, D])
    S["xres"] = dscr("s_xres", [T, D])
    S["qkA"] = dscr("s_qkA", [1024, T], BF16)
    S["vA"] = dscr("s_vA", [T, 8 * 65], BF16)
    S["vB"] = dscr("s_vB", [T, 512], BF16)
    S["gate"] = dscr("s_gate", [T, 512])
    S["qtB"] = dscr("s_qtB", [2, 512, T], BF16)
    S["ktB"] = dscr("s_ktB", [2, 512, T], BF16)
    S["kendB"] = dscr("s_kendB", [2, T, 512], BF16)
    S["decB"] = dscr("s_decB", [2, 512, 72])
    S["oT"] = dscr("s_oT", [1024, T], BF16)
    S["ofb"] = dscr("s_ofb", [2, T, 512])
    S["actT"] = dscr("s_actT", [DFF, T], BF16)
    S["qkC"] = dscr("s_qkC", [640, T], BF16)
    S["vC"] = dscr("s_vC", [T, 2 * 65], BF16)
    S["qnT"] = dscr("s_qnT", [512, T], BF16)
    S["knT"] = dscr("s_knT", [512, T], BF16)
    S["kn_tok"] = dscr("s_kn_tok", [T, 512])
    S["v_tok"] = dscr("s_v_tok", [T, 512])
    S["Gexp"] = dscr("s_Gexp", [2, 8, 8, T])
    S["Hexp"] = dscr("s_Hexp", [2, 8, 8, T])
    S["dsc"] = dscr("s_dsc", [2, T, 40])
    S["dng"] = dscr("s_dng", [2, 512])
    C.S = S

    def gsb(name, shape, dt):
        return P.es.enter_context(nc.sbuf_tensor(name, list(shape), dt))
    C.ident = gsb("ident", [128, 128], F32)
    C.identb = gsb("identb", [128, 128], BF16)
    C.modfm = gsb("modfm", [128, DEPTH, 3, 4, 8], F32)
    C.gsb = gsb

    def scoped(name, shape, dt):
        st = contextlib.ExitStack()
        t = st.enter_context(nc.sbuf_tensor(P.uname(name), list(shape), dt))
        return t, st
    C.scoped = scoped
    C.epsb = gsb("epsb", [128, 4], F32)

    phase0(C)
    nlayers = int(os.environ.get("MK_LAYERS", DEPTH))
    xsrc, xkey = I["x"], "x_in"
    for l in range(nlayers):
        j = l // 2
        C.hT, hst = scoped("hT", [128, 8, T], BF16)
        make_hT(C, l, 0, xsrc, xkey)
        if l % 2 == 0:
            l2_even(C, j)
        else:
            l2_odd(C, j)
        hst.close()
        if l % 2 == 0:
            C.Mp, mst = scoped("Mp", [128, 8, 19, 64], F32)
            build_mp(C, j)
            attn_even(C, j)
            mst.close()
            gla_scan(C, j)
            scan_finalize(C, I["b_norm_g"][j:j + 1, :], "gla")
            wout = I["ev_w_out"][j]
        else:
            attn_odd(C, j)
            delta_scan(C, j)
            scan_finalize(C, S["dng"][j:j + 1, :], "dl")
            wout = I["od_w_out"][j]
        if dbg and stage == 10 + l:
            P.begin()
            C.dump(C.taps["dbg_oT"], S["oT"], "oT")
            P.end()
        proj_residual(C, l, 0, wout, 8, S["oT"], "oT", xsrc, xkey)
        xsrc, xkey = S["xres"], "xres_w"
        if dbg and stage == 20 + l:
            P.begin()
            C.dump(C.taps["dbg_x"], S["xres"], "xres_w")
            P.end()
        stop = os.environ.get("MK_STOP", "")
        if stop == "l4":
            break
        C.hT, hst = scoped("hT", [128, 8, T], BF16)
        make_hT(C, l, 1, xsrc, xkey)
        if stop != "hT2":
            ffn_up(C, l)
        hst.close()
        if stop in ("hT2", "ffn_up"):
            break
        proj_residual(C, l, 1, I["ffn_down"][l], 22, S["actT"], "actT", xsrc, xkey)
        if dbg and stage == 30 + l:
            P.begin()
            C.dump(C.taps["dbg_x"], S["xres"], "xres_w")
            P.end()
    if not os.environ.get("MK_STOP"):
        final_norm(C)
    P.flush(final=True)
    P.es.close()
    return nc


def phase0(C):
    P, nc, I, S = C.P, C.nc, C.I, C.S
    P.begin()
    rows = P.sb("p0_rows", [64, 128], F32)
    rows2 = P.sb("p0_rows2", [24, 128], F32)
    vecfm = P.sb("p0_vecfm", [128, 64], F32)
    scT = P.sb("p0_scT", [128, 3, 8], F32)
    wts = [P.sb(f"p0_wt{i}", [128, 8, 1024], F32) for i in range(2)]
    brow = P.sb("p0_brow", [3, 1024], F32)
    grow = P.sb("p0_grow", [3, 1024], F32)
    tmp = P.sb("p0_tmp", [128, 8], F32)
    ps_t = P.ps("p0_pst", [128, 128], F32)
    ps_fm = P.ps("p0_psfm", [128, 4, 8, 3], F32)
    ps_row = P.ps("p0_psrow", [3, 1024], F32)

    P.dma(C.ident[:, :], I["c_ident"][:, :], w=["ident"])
    P.op("dve", "memset", w=["epsb"], ap=C.epsb[:, 0:1], constant=EPS)
    P.op("dve", "memset", w=["epsb"], ap=C.epsb[:, 1:2], constant=1.0)
    P.op("dve", "memset", w=["epsb"], ap=C.epsb[:, 2:4], constant=0.0)
    P.op("dve", "tensor_copy", r=["ident"], w=["identb"], out=C.identb[:, :], in_=C.ident[:, :])
    P.dma(rows2[:, :], I["cond"].rearrange("s (k p) -> (s k) p", p=128), w=["rows2"])
    P.op("pe", "transpose", r=["rows2", "ident"], w=["ps_t"], out=ps_t[:, 0:24], in_=rows2[:, :], identity=C.ident[0:24, 0:24])
    P.op("act", "activation", r=["ps_t"], w=["scT"], out=scT[:, :, :].rearrange("p s k -> p (s k)"), in_=ps_t[:, 0:24], func=AF.Silu)
    for j_ in range(2):
        for h_ in range(8):
            P.dma(S["dng"][j_:j_ + 1, h_ * 64:(h_ + 1) * 64], I["d_norm_g"][j_:j_ + 1, :], w=["dng"])
    wcnt = 0
    for l in range(DEPTH):
        P.dma(rows[0:48, :], I["ada_b"][l].rearrange("(r p) -> r p", p=128), w=["rows"])
        P.dma(rows[48:56, :], I["norm1_g"][l].rearrange("(r p) -> r p", p=128), w=["rows"])
        P.dma(rows[56:64, :], I["norm2_g"][l].rearrange("(r p) -> r p", p=128), w=["rows"])
        P.op("pe", "transpose", r=["rows", "ident"], w=["ps_t"], out=ps_t[:, 0:64], in_=rows[:, :], identity=C.ident[0:64, 0:64])
        P.op("dve", "tensor_copy", r=["ps_t"], w=["vecfm"], out=vecfm[:, :], in_=ps_t[:, 0:64])
        for j in range(6):
            wt = wts[wcnt % 2]; wkey = f"p0wt{wcnt % 2}"; wcnt += 1
            for k in range(8):
                P.dma(wt[:, k, :], I["ada_w"][l, k * 128:(k + 1) * 128, j * 1024:(j + 1) * 1024], w=[wkey + f"_{k}"])
            if j in (0, 1, 3, 4):
                jq = (0, 1, None, 2, 3)[j]
                for c in range(8):
                    for k in range(8):
                        P.op("pe", "matmul", r=[wkey + f"_{k}", "scT"], w=["ps_fm"],
                             out=ps_fm[:, jq, c, :], lhsT=wt[:, k, c * 128:(c + 1) * 128], rhs=scT[:, :, k],
                             start=(k == 0), stop=(k == 7))
            else:
                jg = 0 if j == 2 else 1
                P.dma(brow[:, :], I["ada_b"][l:l + 1, j * 1024:(j + 1) * 1024].partition_broadcast(3), w=["brow"])
                for half in range(2):
                    for k in range(8):
                        P.op("pe", "matmul", r=[wkey + f"_{k}", "scT"], w=["ps_row"],
                             out=ps_row[:, half * 512:(half + 1) * 512], lhsT=scT[:, :, k], rhs=wt[:, k, half * 512:(half + 1) * 512],
                             start=(k == 0), stop=(k == 7))
                P.op("dve", "tensor_tensor", r=["ps_row", "brow"], w=["grow"], out=grow[:, :], in0=ps_row[:, :], in1=brow[:, :], op=ALU.add)
                P.dma(S["gates"][l, :, jg, :], grow[:, :], r=["grow"], w=[f"gates{l}"])
        for s in range(3):
            for half in range(2):
                q_sh, q_sc = (0, 1) if half == 0 else (2, 3)
                j_sh, j_sc = (0, 1) if half == 0 else (3, 4)
                gcol = 48 if half == 0 else 56
                P.op("dve", "tensor_tensor", r=["ps_fm", "vecfm"], w=["modfm"],
                     out=C.modfm[:, l, s, 2 * half + 1, :], in0=ps_fm[:, q_sh, :, s], in1=vecfm[:, j_sh * 8:(j_sh + 1) * 8], op=ALU.add)
                P.op("dve", "scalar_tensor_tensor", r=["ps_fm", "vecfm"], w=["p0tmp"],
                     out=tmp[:, :], in0=ps_fm[:, q_sc, :, s], scalar=1.0, in1=vecfm[:, j_sc * 8:(j_sc + 1) * 8], op0=ALU.add, op1=ALU.add)
                P.op("dve", "tensor_tensor", r=["p0tmp", "vecfm"], w=["modfm"],
                     out=C.modfm[:, l, s, 2 * half, :], in0=tmp[:, :], in1=vecfm[:, gcol:gcol + 8], op=ALU.mult)
    P.end()


def make_hT(C, l, which, xsrc, xkey="xres"):
    P, nc = C.P, C.nc
    P.begin()
    NB = 3
    xts = [P.sb(f"mh_x{i}", [128, D], F32) for i in range(NB)]
    junk = P.sb("mh_junk", [128, D], BF16)
    xns = [P.sb(f"mh_xn{i}", [128, D], BF16) for i in range(NB)]
    sss = [P.sb(f"mh_ss{i}", [128, 4], F32) for i in range(NB)]
    pss = [[P.ps(f"mh_ps{i}_{e}", [128, 4, 128], BF16) for e in range(2)] for i in range(2)]
    P.psum_keys.update([f"mh_ps{i}_{e}" for i in range(2) for e in range(2)])
    for i in range(NT):
        b = i % NB
        seq = 0 if i < 32 else (1 if i < 34 else 2)
        xt, xn, ss = xts[b], xns[b], sss[b]
        ps = pss[i % 2]; pk = [f"mh_ps{i % 2}_0", f"mh_ps{i % 2}_1"]
        P.dma(xt[:, :], xsrc[i * 128:(i + 1) * 128, :], r=[xkey], w=[f"mh_x{b}"])
        P.op("act", "activation", r=[f"mh_x{b}"], w=["mh_junk", f"mh_ss{b}"],
             out=junk[:, :], in_=xt[:, :], func=AF.Square, accum_out=ss[:, 0:1])
        P.op("act", "activation", r=[f"mh_ss{b}"], w=[f"mh_ss{b}"], out=ss[:, 1:2], in_=ss[:, 0:1], func=AF.Ln, scale=1.0 / D, bias=C.epsb[:, 0:1])
        P.op("act", "activation", r=[f"mh_ss{b}"], w=[f"mh_ss{b}"], out=ss[:, 2:3], in_=ss[:, 1:2], func=AF.Exp, scale=-0.5)
        P.op("dve", "tensor_scalar", r=[f"mh_x{b}", f"mh_ss{b}"], w=[f"mh_xn{b}"],
             out=xn[:, :], in0=xt[:, :], scalar1=ss[:, 2:3], scalar2=None, op0=ALU.mult)
        for k in range(8):
            P.op("pe", "transpose", r=[f"mh_xn{b}", "identb"], w=[pk[k % 2]], out=ps[k % 2][:, k // 2, :], in_=xn[:, k * 128:(k + 1) * 128], identity=C.identb[:, :])
        for k in range(8):
            A = C.modfm[:, l, seq, 2 * which, k:k + 1]
            B = C.modfm[:, l, seq, 2 * which + 1, k:k + 1]
            dst = C.hT[:, k, i * 128:(i + 1) * 128]
            if k % 2 == 0:
                P.op("dve", "tensor_scalar", r=[pk[0], "modfm"], w=[f"hT{i}_d"], out=dst, in0=ps[0][:, k // 2, :], scalar1=A, scalar2=B, op0=ALU.mult, op1=ALU.add)
            else:
                P.op("act", "activation", r=[pk[1], "modfm"], w=[f"hT{i}_a"], out=dst, in_=ps[1][:, k // 2, :], func=AF.Identity, scale=A, bias=B)
    P.end()


def host_consts():
    c = {}
    c["c_ident"] = np.eye(128, dtype=np.float32)
    mf = np.ones((128, 512), np.float32); mf[:, 0::64] = 0.0
    c["c_maskf"] = mf
    eoh = np.zeros((32, 64, 64), np.float32)
    for w in range(64):
        cs = min(max(w - 8, 0), 48)
        for cc in range(64):
            if cs <= cc < cs + 16:
                eoh[cc - w + 15, w, cc] = 1.0
            else:
                eoh[31, w, cc] = -30000.0
    c["c_eoh"] = eoh
    perm = np.zeros((128, 128), np.float32)
    for m_ in range(128):
        src = m_ + 16 if (m_ % 32) < 16 else m_ - 16
        perm[src, m_] = 1.0
    c["c_perm"] = perm
    bd = np.zeros((128, 128), np.float32); bd[:64, :64] = 1.0; bd[64:, 64:] = 1.0
    c["c_bd"] = bd
    tpos = np.arange(TS)
    inv = (1.0 / (10000.0 ** (np.arange(16, dtype=np.float32) / 16.0))).astype(np.float32)
    cos_t = np.zeros((128, TS), np.float32); sin_t = np.zeros((128, TS), np.float32)
    for p_ in range(128):
        i_ = p_ % 64
        pos = (tpos // 64) if i_ < 32 else (tpos % 64)
        ang = pos.astype(np.float32) * inv[i_ % 16]
        cos_t[p_] = np.cos(ang)
        sin_t[p_] = np.sin(ang) * (-1.0 if (i_ % 32) < 16 else 1.0)
    c["c_cos"] = cos_t; c["c_sin"] = sin_t
    i64 = np.arange(64)
    NEG = -30000.0
    dm = np.stack([np.where(i64[:, None] <= i64[None, :], 0.0, NEG), np.where(i64[:, None] < i64[None, :], 0.0, NEG),
                   np.where(i64[:, None] >= i64[None, :], 0.0, NEG), np.where(i64[:, None] > i64[None, :], 0.0, NEG)]).astype(np.float32)
    c["c_dmask"] = dm
    ii = np.arange(128)
    c["c_tri"] = np.stack([(ii[:, None] <= ii[None, :]), (ii[:, None] >= ii[None, :])]).astype(np.float32)
    return c


def make_in_maps(inp, cores):
    g = lambda k: np.asarray(inp[k])
    consts = host_consts()
    maps = []
    for c in cores:
        m = {}
        m["x"] = np.ascontiguousarray(np.concatenate([g("x_sample")[c], g("x_prompt")[2 * c], g("x_prompt")[2 * c + 1]], axis=0))
        m["cond"] = np.ascontiguousarray(np.stack([g("c")[c], g("c_ctx"), g("c_ctx")], axis=0))
        m["cak"] = np.ascontiguousarray(g("cache_a_k")[c]); m["cav"] = np.ascontiguousarray(g("cache_a_v")[c])
        m["sb"] = np.ascontiguousarray(g("state_b")[c])
        m["cck"] = np.ascontiguousarray(g("cache_c_k")[c]); m["ccv"] = np.ascontiguousarray(g("cache_c_v")[c])
        m["sd"] = np.ascontiguousarray(g("state_d")[c])
        for k in ("ada_w", "ada_b", "norm1_g", "norm2_g", "ffn_up", "ffn_conv", "ffn_down", "ev_w_in", "ev_w_out",
                  "a_rpb", "b_w_g2", "b_b_g", "b_norm_g", "od_w_in", "od_w_out", "c_sink", "d_conv", "d_a_log",
                  "d_dt_bias", "d_norm_g", "final_g"):
            m[k] = g(k)
        m.update(consts)
        maps.append(m)
    return maps


_NC_CACHE = {}


def kernel(**inputs):
    if "nc" not in _NC_CACHE:
        _NC_CACHE["nc"] = build_program()
    nc = _NC_CACHE["nc"]
    maps = make_in_maps(inputs, list(range(NCORES)))
    res = run_bass_kernel_spmd(nc, maps, core_ids=list(range(NCORES)))
    R = res.results
    y_sample = np.stack([R[c]["y"][:TS] for c in range(NCORES)], 0)
    y_prompt = np.concatenate([R[c]["y"][TS:].reshape(2, TP, D) for c in range(NCORES)], 0)
    cat = lambda k: np.concatenate([R[c][k] for c in range(NCORES)], 0)
    return (y_prompt, y_sample, cat("nak"), cat("nav"), cat("nsb"), cat("nck"), cat("ncv"), cat("nsd"))


def load_w_bf16(C, dst, src, ncols, key):
    P = C.P
    for k in range(src.shape[0] // 128):
        c0 = 0
        while c0 < ncols:
            n = min(2048, ncols - c0)
            P.dma(dst[:, k, c0:c0 + n], src[k * 128:(k + 1) * 128, c0:c0 + n], w=[f"{key}_{k}"], q="pool")
            c0 += n


def load_fm(C, dst, src_rows, n, ps_t, rows_tile, key):
    P = C.P
    P.dma(rows_tile[0:n, :], src_rows, w=["lfm_rows"])
    P.op("pe", "transpose", r=["lfm_rows", "ident"], w=["lfm_pst"], out=ps_t[:, 0:n], in_=rows_tile[0:n, :], identity=C.ident[0:n, 0:n])
    P.op("dve", "tensor_copy", r=["lfm_pst"], w=[key], out=dst, in_=ps_t[:, 0:n])


class Rot:
    def __init__(self, P, name, n, shape, dt, psum=False):
        self.tiles = [(P.ps if psum else P.sb)(f"{name}{i}", shape, dt) for i in range(n)]
        self.keys = [f"{name}{i}" for i in range(n)]
        if psum:
            P.psum_keys.update(self.keys)
        self.i = -1

    def next(self):
        self.i = (self.i + 1) % len(self.tiles)
        return self.tiles[self.i], self.keys[self.i]


def l2_even(C, j):
    P, nc, I, S, O = C.P, C.nc, C.I, C.S, C.O
    P.begin()
    W = P.sb("l2_w", [128, 8, EV_IN], BF16)
    load_w_bf16(C, W, I["ev_w_in"][j], EV_IN, "l2w")
    wkeys = [f"l2w_{k}" for k in range(8)]
    rows = P.sb("l2_rows", [16, 128], F32)
    ps_t = P.ps("l2_pst", [128, 128], F32)
    negb = P.sb("l2_negb", [128, 8], F32)
    load_fm(C, negb[:, :], I["b_b_g"][j].rearrange("d (m p) -> (d m) p", p=128), 8, ps_t, rows, "l2negb")
    P.op("dve", "tensor_scalar", r=["l2negb"], w=["l2negb"], out=negb[:, :], in0=negb[:, :], scalar1=-1.0, scalar2=None, op0=ALU.mult)
    wg2 = P.sb("l2_wg2", [16, 2, 512], F32)
    P.dma(wg2[:, :, :], I["b_w_g2"][j].rearrange("d r c -> r d c"), w=["l2wg2"])
    maskf = P.sb("l2_maskf", [128, 512], F32)
    P.dma(maskf[:, :], I["c_maskf"][:, :], w=["maskf"])

    psq = Rot(P, "l2_psq", 1, [128, 512], F32, psum=True)
    psk = Rot(P, "l2_psk", 1, [128, 512], F32, psum=True)
    psz = Rot(P, "l2_psz", 2, [128, 512], F32, psum=True)
    psg = Rot(P, "l2_psg", 2, [128, 512], F32, psum=True)
    psT = P.ps("l2_psT", [128, 2, 4, 128], BF16)
    tA = Rot(P, "l2_tA", 2, [128, 512], F32)
    tB = Rot(P, "l2_tB", 2, [128, 512], F32)
    tC = Rot(P, "l2_tC", 2, [128, 512], F32)
    tD = Rot(P, "l2_tD", 2, [128, 512], F32)
    tE = Rot(P, "l2_tE", 2, [128, 512], F32)
    sbf = Rot(P, "l2_sbf", 4, [128, 512], BF16)
    sf32 = Rot(P, "l2_sf", 3, [128, 512], F32)
    ketok = Rot(P, "l2_ketok", 2, [128, 4, 128], BF16)
    vst = Rot(P, "l2_vst", 2, [128, 8, 65], BF16)
    for t_, k_ in zip(vst.tiles, vst.keys):
        P.op("pool", "memset", w=[k_], ap=t_[:, :, 64:65], constant=1.0)
    dect = Rot(P, "l2_dec", 2, [128, 8], F32)
    glr_sb = [Rot(P, f"l2_glr{d}_", 2, [16, 512], F32) for d in range(2)]
    ev = [0]

    def evac_engine():
        ev[0] += 1
        return "act" if ev[0] % 2 else "dve"

    def evac(dst, dkey, src, skey):
        e = evac_engine()
        if e == "act":
            P.op("act", "activation", r=[skey], w=[dkey], out=dst, in_=src, func=AF.Identity)
        else:
            P.op("dve", "tensor_copy", r=[skey], w=[dkey], out=dst, in_=src)

    def proj_fm(ps, pkey, c0, M, tt):
        for k in range(8):
            P.op("pe", "matmul", r=[wkeys[k]] + [f"hT{4 * tt + u}" for u in range(4)], w=[pkey],
                 out=ps[0:M, :], lhsT=W[:, k, c0:c0 + M], rhs=C.hT[:, k, tt * 512:(tt + 1) * 512], start=(k == 0), stop=(k == 7))

    def proj_tm(ps, pkey, c0, N, i):
        for k in range(8):
            P.op("pe", "matmul", r=[wkeys[k], f"hT{i}"], w=[pkey],
                 out=ps[:, 0:N], lhsT=C.hT[:, k, i * 128:(i + 1) * 128], rhs=W[:, k, c0:c0 + N], start=(k == 0), stop=(k == 7))

    parts = os.environ.get("L2P", "qk,tm,gla").split(",")
    for tt in range(int(os.environ.get("L2TT", T // 512))):
        tok = slice(tt * 512, (tt + 1) * 512)
        for m in range(8 if "qk" in parts else 0):
            ps, pk = psg.next()
            proj_fm(ps, pk, m * 128, 128, tt)
            st, sk = sbf.next()
            evac(st[:, :], sk, ps[:, :], pk)
            P.dma(S["qkA"][m * 128:(m + 1) * 128, tok], st[:, :], r=[sk], w=["qkA"])
        for u in range(4 if "tm" in parts else 0):
            i = 4 * tt + u
            rowsl = slice(i * 128, (i + 1) * 128)
            ps, pk = psg.next()
            proj_tm(ps, pk, 1024, 512, i)
            st, sk = vst.next()
            if i < 32:
                evac(st[:, :, 0:64], sk, ps[:, :].rearrange("p (h d) -> p h d", d=64), pk)
            else:
                seq = (i - 32) // 2; t0 = ((i - 32) % 2) * 128
                sf, sfk = sf32.next()
                evac(sf[:, :], sfk, ps[:, :], pk)
                P.op("pool", "tensor_copy", r=[sfk], w=[sk], out=st[:, :, 0:64], in_=sf[:, :].rearrange("p (h d) -> p h d", d=64))
                for h in range(8):
                    P.dma(O["nav"][seq, j, h, t0:t0 + 128, :], sf[:, h * 64:(h + 1) * 64], r=[sfk], w=["o_nav"], isout=True)
                ps2, pk2 = psg.next()
                proj_tm(ps2, pk2, 512, 512, i)
                sf, sfk = sf32.next()
                evac(sf[:, :], sfk, ps2[:, :], pk2)
                for h in range(8):
                    P.dma(O["nak"][seq, j, h, t0:t0 + 128, :], sf[:, h * 64:(h + 1) * 64], r=[sfk], w=["o_nak"], isout=True)
            P.dma(S["vA"][rowsl, :], st[:, :, :].rearrange("p h d -> p (h d)"), r=[sk], w=["vA"])
            ps, pk = psg.next()
            proj_tm(ps, pk, 2560, 512, i)
            st, sk = sbf.next()
            evac(st[:, :], sk, ps[:, :], pk)
            P.dma(S["vB"][rowsl, :], st[:, :], r=[sk], w=["vB"])
            ps, pk = psg.next()
            proj_tm(ps, pk, 3104, 512, i)
            sf, sfk = sf32.next()
            evac(sf[:, :], sfk, ps[:, :], pk)
            P.dma(S["gate"][rowsl, :], sf[:, :], r=[sfk], w=["gate"])
        glr = []
        if "gla" not in parts:
            continue
        for d in range(2):
            ps, pk = psg.next()
            proj_fm(ps, pk, 3072 + 16 * d, 16, tt)
            g, gk = glr_sb[d].next()
            evac(g[:, :], gk, ps[0:16, :], pk)
            glr.append((g, gk))
        for m in range(4):
            pq, pqk = psq.next()
            proj_fm(pq, pqk, 1536 + m * 128, 128, tt)
            pkk, pkkk = psk.next()
            proj_fm(pkk, pkkk, 2048 + m * 128, 128, tt)
            for d in range(2):
                pz, pzk = psz.next()
                P.op("pe", "matmul", r=["l2wg2", glr[d][1]], w=[pzk], out=pz[:, :], lhsT=wg2[:, d, m * 128:(m + 1) * 128], rhs=glr[d][0][:, :],
                     start=True, stop=True)
                t1, k1 = tA.next(); t2, k2 = tB.next(); t3, k3 = tC.next(); t4, k4 = tD.next(); t5, k5 = tE.next()
                P.op("act", "activation", r=[pzk, "l2negb"], w=[k1], out=t1[:, :], in_=pz[:, :], func=AF.Exp, scale=-1.0, bias=negb[:, 4 * d + m:4 * d + m + 1])
                P.op("act", "activation", r=[k1, "epsb"], w=[k2], out=t2[:, :], in_=t1[:, :], func=AF.Ln, bias=C.epsb[:, 1:2])
                if d == 0:
                    P.op("dve", "tensor_tensor_scan", r=["maskf", k2], w=[k3], out=t3[:, :], data0=maskf[:, :], data1=t2[:, :], initial=0.0,
                         op0=ALU.mult, op1=ALU.add)
                    last = t3[:, 63::64]
                else:
                    P.op("dve", "tensor_tensor_scan", r=["maskf", k2], w=[k3], out=t3[:, ::-1], data0=maskf[:, :], data1=t2[:, ::-1], initial=0.0,
                         op0=ALU.mult, op1=ALU.add)
                    last = t3[:, 0::64]
                P.op("act", "activation", r=[k3], w=[k4], out=t4[:, :], in_=t3[:, :], func=AF.Exp, scale=-1.0 / 16)
                P.op("act", "activation", r=[k3], w=[k5], out=t5[:, :], in_=t3[:, :], func=AF.Exp, scale=1.0 / 16)
                st, sk = sbf.next()
                P.op("dve", "scalar_tensor_tensor", r=[pqk, k4], w=[sk], out=st[:, :], in0=pq[:, :], scalar=0.125, in1=t4[:, :], op0=ALU.mult, op1=ALU.mult)
                P.dma(S["qtB"][d, m * 128:(m + 1) * 128, tok], st[:, :], r=[sk], w=["qtB"])
                st, sk = sbf.next()
                P.op("dve", "tensor_tensor", r=[pkkk, k5], w=[sk], out=st[:, :], in0=pkk[:, :], in1=t5[:, :], op=ALU.mult)
                P.dma(S["ktB"][d, m * 128:(m + 1) * 128, tok], st[:, :], r=[sk], w=["ktB"])
                P.op("dve", "tensor_tensor", r=[k3], w=[k1], out=t1[:, :].rearrange("p (c s) -> p c s", s=64),
                     in0=t3[:, :].rearrange("p (c s) -> p c s", s=64), in1=_bc(last.unsqueeze(2), [128, 8, 64]), op=ALU.subtract)
                P.op("act", "activation", r=[k1], w=[k1], out=t1[:, :], in_=t1[:, :], func=AF.Exp, scale=1.0 / 16)
                st, sk = sbf.next()
                P.op("dve", "tensor_tensor", r=[pkkk, k1], w=[sk], out=st[:, :], in0=pkk[:, :], in1=t1[:, :], op=ALU.mult)
                for c in range(4):
                    P.op("pe", "transpose", r=[sk, "identb"], w=[f"l2psT{d}"], out=psT[:, d, c, :], in_=st[:, c * 128:(c + 1) * 128], identity=C.identb[:, :])
                kt_, ktk = ketok.next()
                evac(kt_[:, :, :], ktk, psT[:, d, :, :], f"l2psT{d}")
                P.dma(S["kendB"][d, tok, m * 128:(m + 1) * 128].rearrange("(c p) f -> p c f", p=128), kt_[:, :, :], r=[ktk], w=["kendB"])
                dc, dck = dect.next()
                P.op("act", "activation", r=[k3], w=[dck], out=dc[:, :], in_=last, func=AF.Exp, scale=-1.0 / 16)
                P.dma(S["decB"][d, m * 128:(m + 1) * 128, tt * 8:(tt + 1) * 8], dc[:, :], r=[dck], w=["decB"])
    P.end()


class AttnRes:
    pass


def attn_setup(C, nq_max):
    P = C.P
    R = AttnRes()
    R.ps_s = Rot(P, "at_pss", 2, [128, 512], F32, psum=True)
    R.ps_o = Rot(P, "at_pso", 2, [128, 2, 4 * 65], F32, psum=False) if False else None
    R.pso = [P.ps(f"at_pso{i}", [128, 512], F32) for i in range(4)]
    R.ps_T = Rot(P, "at_psT", 1, [128, 4, 128], BF16, psum=True)
    R.E = Rot(P, "at_E", 2, [128, 512], F32)
    R.Pt = Rot(P, "at_Pt", 3, [128, 7, 512 // 4], BF16) if False else None
    R.rden = Rot(P, "at_rden", 2, [128, 8], F32)
    R.o = Rot(P, "at_o", 2, [128, 512], BF16)
    R.oT = Rot(P, "at_oT", 2, [128, 4, 512], BF16)
    return R


def build_mp(C, j):
    P, nc, I, S, O = C.P, C.nc, C.I, C.S, C.O
    P.begin()
    ps_m = P.ps("mp_psm", [128, 512], F32)
    Mp = C.Mp
    eoh = P.sb("ae_eoh", [32, 64, 128], F32)
    rrows = P.sb("ae_rrows", [120, 32], F32)
    rpbT = P.sb("ae_rpbT", [32, 120], F32)
    P.dma(eoh[:, :, 0:64], I["c_eoh"][:, :, :], w=["ae_eoh"])
    P.dma(eoh[:, :, 64:128], I["c_eoh"][:, :, :], w=["ae_eoh"])
    P.op("pool", "memset", w=["ae_rrows"], ap=rrows[:, :], constant=1.0)
    P.dma(rrows[:, 0:31], I["a_rpb"][j].rearrange("h r k -> (h r) k"), r=[], w=["ae_rrows"])
    P.op("pe", "transpose", r=["ae_rrows", "ident"], w=["ae_psm"], out=ps_m[0:32, 0:120], in_=rrows[:, :], identity=C.ident[0:120, 0:120])
    P.op("dve", "tensor_copy", r=["ae_psm"], w=["ae_rpbT"], out=rpbT[:, :], in_=ps_m[0:32, 0:120])
    P.op("pool", "memset", w=["ae_Mp"], ap=Mp[:, :, :, :], constant=0.0)
    for w0 in range(0, 64, 4):
        for u in range(4):
            P.op("pe", "matmul", r=["ae_eoh", "ae_rpbT"], w=["ae_psm"], out=ps_m[:, u * 120:(u + 1) * 120], lhsT=eoh[:, w0 + u, :], rhs=rpbT[:, :],
                 start=True, stop=True)
        src = ps_m[:, 0:480].rearrange("p (w h r) -> p h r w", w=4, h=8)
        wsl = slice(w0, w0 + 4)
        P.op("act", "activation", r=["ae_psm"], w=["ae_Mp"], out=Mp[0:64, :, 0:14, wsl], in_=src[0:64, :, 0:14, :], func=AF.Exp)
        P.op("act", "activation", r=["ae_psm"], w=["ae_Mp"], out=Mp[64:128, :, 0:14, wsl], in_=src[64:128, :, 1:15, :], func=AF.Exp)
        P.op("act", "activation", r=["ae_psm"], w=["ae_Mp"], out=Mp[0:64, :, 15:19, wsl], in_=src[0:64, :, 4:11:2, :], func=AF.Exp)
        P.op("act", "activation", r=["ae_psm"], w=["ae_Mp"], out=Mp[64:128, :, 14:18, wsl], in_=src[64:128, :, 3:10:2, :], func=AF.Exp)

    P.end()


def attn_even(C, j):
    P, nc, I, S, O = C.P, C.nc, C.I, C.S, C.O
    P.begin()
    R = attn_setup(C, 128)
    kT = P.sb("ae_kT", [64, 8, T], BF16)
    V = P.sb("ae_V", [128, NT, 8 * 65], BF16)
    for h in range(8):
        P.dma(kT[:, h, :], S["qkA"][512 + h * 64:512 + (h + 1) * 64, :], r=["qkA"], w=["ae_kT"])
    for n0 in range(0, NT, 4):
        P.dma(V[:, n0:n0 + 4, :], S["vA"][n0 * 128:(n0 + 4) * 128, :].rearrange("(n p) f -> p n f", p=128), r=["vA"], w=["ae_V"])
    ckT = P.sb("ae_ckT", [64, 8, 256], BF16)
    cV = P.sb("ae_cV", [128, 2, 8 * 65], BF16)
    ctmp = P.sb("ae_ctmp", [128, 2, 8, 64], F32)
    ps_m = P.ps("ae_psm", [128, 512], F32)
    for half in range(2):
        for h in range(8):
            P.dma(ctmp[:, half, h, :], I["cak"][j, h, half * 128:(half + 1) * 128, :], w=["ae_ctmp"])
    for half in range(2):
        for h in range(8):
            P.op("pe", "transpose", r=["ae_ctmp", "ident"], w=["ae_psm"], out=ps_m[0:64, h * 64:h * 64 + 128] if False else ps_m[0:64, (h % 4) * 128:(h % 4) * 128 + 128],
                 in_=ctmp[:, half, h, :], identity=C.ident[:, :])
            if h % 4 == 3:
                h0 = h - 3
                P.op("dve", "tensor_copy", r=["ae_psm"], w=["ae_ckT"], out=ckT[:, h0:h0 + 4, half * 128:(half + 1) * 128],
                     in_=ps_m[0:64, :].rearrange("p (h t) -> p h t", t=128))
    ctmp2 = P.sb("ae_ctmp2", [128, 2, 8, 64], F32)
    for half in range(2):
        for h in range(8):
            P.dma(ctmp2[:, half, h, :], I["cav"][j, h, half * 128:(half + 1) * 128, :], w=["ae_ctmp2"])
    P.op("pool", "memset", w=["ae_cV"], ap=cV[:, :, :], constant=1.0)
    P.op("dve", "tensor_copy", r=["ae_ctmp2"], w=["ae_cV"], out=cV[:, :, :].rearrange("p a (h e) -> p a h e", e=65)[:, :, :, 0:64], in_=ctmp2[:, :, :, :])
    Mp = C.Mp
    qblk = Rot(P, "ae_q", 2, [64, 8, 512], BF16)
    Pt = Rot(P, "ae_Pt", 3, [128, 7, 64], BF16)
    PtP = Rot(P, "ae_PtP", 3, [128, 2, 128], BF16)
    stage_cnt = [0]

    def finish_block(pso_pair, pkeys, nq, tok0, oT_t, oTk, col0):
        rd, rdk = R.rden.next()
        ot, otk = R.o.next()
        for b in range(2):
            v = pso_pair[b][0:nq, 0:260].rearrange("p (h e) -> p h e", e=65)
            P.op("dve", "reciprocal", r=[pkeys[b]], w=[rdk], out=rd[0:nq, 4 * b:4 * b + 4], in_=v[:, :, 64])
            P.op("dve", "tensor_tensor", r=[pkeys[b], rdk], w=[otk], out=ot[0:nq, 256 * b:256 * (b + 1)].rearrange("p (h d) -> p h d", d=64),
                 in0=v[:, :, 0:64], in1=_bc(rd[0:nq, 4 * b:4 * b + 4].unsqueeze(2), [nq, 4, 64]), op=ALU.mult)
        pT, pTk = R.ps_T.next()
        for m in range(4):
            P.op("pe", "transpose", r=[otk, "identb"], w=[pTk], out=pT[:, m, 0:nq], in_=ot[0:nq, m * 128:(m + 1) * 128], identity=C.identb[0:nq, 0:nq])
        P.op("act", "activation", r=[pTk], w=[oTk], out=oT_t[:, :, col0:col0 + nq], in_=pT[:, :, 0:nq], func=AF.Identity)

    pso_i = 0
    for blk in range(8):
        q, qk = qblk.next()
        for h in range(8):
            P.dma(q[:, h, :], S["qkA"][h * 64:(h + 1) * 64, blk * 512:(blk + 1) * 512], r=["qkA"], w=[qk])
        oT_t, oTk = R.oT.next()
        for ri in range(8):
            i = blk * 8 + ri
            r0 = min(max(i - 4, 0), 56)
            if 4 <= i <= 60 and i % 2 == 1:
                a0 = (i - 5) // 2; nl = 5; slots = slice(14, 19)
            else:
                a0 = r0 // 2; nl = 4; s0 = 2 * a0 - i + 7; slots = slice(s0, s0 + 7, 2)
            pso_pair = (R.pso[2 * (pso_i % 2)], R.pso[2 * (pso_i % 2) + 1])
            pkeys = (f"at_pso{2 * (pso_i % 2)}", f"at_pso{2 * (pso_i % 2) + 1}")
            pso_i += 1
            def scores(h):
                ps, psk = R.ps_s.next()
                qrhs = q[:, h, ri * 64:(ri + 1) * 64]
                for x in range(nl):
                    P.op("pe", "matmul", r=["ae_kT", qk], w=[psk], out=ps[:, x * 64:(x + 1) * 64], lhsT=kT[:, h, (a0 + x) * 128:(a0 + x + 1) * 128], rhs=qrhs,
                         start=True, stop=True)
                for x in range(2):
                    P.op("pe", "matmul", r=["ae_ckT", qk], w=[psk], out=ps[:, (nl + x) * 64:(nl + x + 1) * 64], lhsT=ckT[:, h, x * 128:(x + 1) * 128], rhs=qrhs,
                         start=True, stop=True)
                E, Ek = R.E.next()
                pt, ptk = Pt.next()
                P.op("act", "activation", r=[psk], w=[Ek], out=E[:, 0:nl * 64], in_=ps[:, 0:nl * 64], func=AF.Exp, scale=0.125)
                P.op("act", "activation", r=[psk], w=[ptk], out=pt[:, nl:nl + 2, :], in_=ps[:, nl * 64:(nl + 2) * 64].rearrange("p (x q) -> p x q", q=64),
                     func=AF.Exp, scale=0.125)
                P.op("dve", "tensor_tensor", r=[Ek, "ae_Mp"], w=[ptk], out=pt[:, 0:nl, :], in0=E[:, 0:nl * 64].rearrange("p (x q) -> p x q", q=64),
                     in1=Mp[:, h, slots, :], op=ALU.mult)
                return pt, ptk

            def pv(h, pt, ptk):
                po = pso_pair[h // 4]; pok = pkeys[h // 4]
                hs = (h % 4) * 65
                for x in range(nl):
                    P.op("pe", "matmul", r=[ptk, "ae_V"], w=[pok], out=po[0:64, hs:hs + 65], lhsT=pt[:, x, :], rhs=V[:, a0 + x, h * 65:(h + 1) * 65],
                         start=(x == 0), stop=False)
                for x in range(2):
                    P.op("pe", "matmul", r=[ptk, "ae_cV"], w=[pok], out=po[0:64, hs:hs + 65], lhsT=pt[:, nl + x, :], rhs=cV[:, x, h * 65:(h + 1) * 65],
                         start=False, stop=(x == 1))
            pend = scores(0)
            for h in range(8):
                nxt = scores(h + 1) if h < 7 else None
                pv(h, *pend)
                pend = nxt
            finish_block(pso_pair, pkeys, 64, i * 64, oT_t, oTk, ri * 64)
        P.dma(S["oT"][0:512, blk * 512:(blk + 1) * 512].rearrange("(m p) t -> p m t", p=128), oT_t[:, :, :], r=[oTk], w=["oT_a"])
    q, qk = qblk.next()
    for h in range(8):
        P.dma(q[:, h, :], S["qkA"][h * 64:(h + 1) * 64, TS:TS + 512], r=["qkA"], w=[qk])
    oT_t, oTk = R.oT.next()
    for sq in range(2):
        for qb in range(2):
            pso_pair = (R.pso[2 * (pso_i % 2)], R.pso[2 * (pso_i % 2) + 1])
            pkeys = (f"at_pso{2 * (pso_i % 2)}", f"at_pso{2 * (pso_i % 2) + 1}")
            pso_i += 1
            col = sq * 256 + qb * 128
            for h in range(8):
                ps, psk = R.ps_s.next()
                qrhs = q[:, h, col:col + 128]
                for x in range(2):
                    kc = TS + sq * 256 + x * 128
                    P.op("pe", "matmul", r=["ae_kT", qk], w=[psk], out=ps[:, x * 128:(x + 1) * 128], lhsT=kT[:, h, kc:kc + 128], rhs=qrhs, start=True, stop=True)
                pt, ptk = PtP.next()
                P.op("act", "activation", r=[psk], w=[ptk], out=pt[:, :, :], in_=ps[:, 0:256].rearrange("p (x q) -> p x q", q=128), func=AF.Exp, scale=0.125)
                po = pso_pair[h // 4]; pok = pkeys[h // 4]
                hs = (h % 4) * 65
                for x in range(2):
                    vn = (TS + sq * 256 + x * 128) // 128
                    P.op("pe", "matmul", r=[ptk, "ae_V"], w=[pok], out=po[:, hs:hs + 65], lhsT=pt[:, x, :], rhs=V[:, vn, h * 65:(h + 1) * 65],
                         start=(x == 0), stop=(x == 1))
            finish_block(pso_pair, pkeys, 128, 0, oT_t, oTk, col)
    P.dma(S["oT"][0:512, TS:TS + 512].rearrange("(m p) t -> p m t", p=128), oT_t[:, :, :], r=[oTk], w=["oT_a"])
    P.end()


def scan_finalize(C, ng_row, key_prefix):
    P, nc, I, S, O = C.P, C.nc, C.I, C.S, C.O
    P.begin()
    ngb = P.sb("fz_ng", [128, 512], F32)
    P.dma(ngb[:, :], ng_row.partition_broadcast(128), w=["fz_ng"])
    of = Rot(P, "fz_of", 2, [128, 512], F32)
    ob = Rot(P, "fz_ob", 2, [128, 512], F32)
    gt = Rot(P, "fz_gt", 2, [128, 512], F32)
    sq = Rot(P, "fz_sq", 2, [128, 512], F32)
    ss = Rot(P, "fz_ss", 2, [128, 16], F32)
    ob16 = Rot(P, "fz_o16", 2, [128, 512], BF16)
    psT = Rot(P, "fz_psT", 2, [128, 4, 128], BF16, psum=True)
    oTs = Rot(P, "fz_oT", 2, [128, 4, 512], BF16)
    oT_t = None
    for i in range(NT):
        rows = slice(i * 128, (i + 1) * 128)
        a, ak = of.next(); b, bk = ob.next(); g, gk = gt.next(); q, qk = sq.next(); s_, sk = ss.next(); o16, o16k = ob16.next()
        P.dma(a[:, :], S["ofb"][0, rows, :], r=["ofb0"], w=[ak])
        P.dma(b[:, :], S["ofb"][1, rows, :], r=["ofb1"], w=[bk])
        P.dma(g[:, :], S["gate"][rows, :], r=["gate"], w=[gk])
        P.op("dve", "tensor_tensor", r=[ak, bk], w=[ak], out=a[:, :], in0=a[:, :], in1=b[:, :], op=ALU.add)
        P.op("dve", "tensor_tensor", r=[ak], w=[qk], out=q[:, :], in0=a[:, :], in1=a[:, :], op=ALU.mult)
        P.op("dve", "tensor_reduce", r=[qk], w=[sk], out=s_[:, 0:8], in_=q[:, :].rearrange("p (h d) -> p h d", d=64), axis=AX.X, op=ALU.add)
        P.op("act", "activation", r=[sk, "epsb"], w=[sk], out=s_[:, 8:16], in_=s_[:, 0:8], func=AF.Ln, scale=1.0 / 64, bias=C.epsb[:, 0:1])
        P.op("act", "activation", r=[sk], w=[sk], out=s_[:, 0:8], in_=s_[:, 8:16], func=AF.Exp, scale=-0.5)
        P.op("dve", "tensor_tensor", r=[ak, sk], w=[ak], out=a[:, :].rearrange("p (h d) -> p h d", d=64), in0=a[:, :].rearrange("p (h d) -> p h d", d=64),
             in1=_bc(s_[:, 0:8].unsqueeze(2), [128, 8, 64]), op=ALU.mult)
        P.op("dve", "tensor_tensor", r=[ak, "fz_ng"], w=[ak], out=a[:, :], in0=a[:, :], in1=ngb[:, :], op=ALU.mult)
        P.op("act", "activation", r=[gk], w=[bk], out=b[:, :], in_=g[:, :], func=AF.Exp, scale=-1.0)
        P.op("dve", "tensor_scalar", r=[bk], w=[bk], out=b[:, :], in0=b[:, :], scalar1=1.0, scalar2=None, op0=ALU.add)
        P.op("dve", "reciprocal", r=[bk], w=[bk], out=b[:, :], in_=b[:, :])
        P.op("dve", "tensor_tensor", r=[gk, bk], w=[gk], out=g[:, :], in0=g[:, :], in1=b[:, :], op=ALU.mult)
        P.op("dve", "tensor_tensor", r=[ak, gk], w=[o16k], out=o16[:, :], in0=a[:, :], in1=g[:, :], op=ALU.mult)
        pT, pTk = psT.next()
        for m in range(4):
            P.op("pe", "transpose", r=[o16k, "identb"], w=[pTk], out=pT[:, m, :], in_=o16[:, m * 128:(m + 1) * 128], identity=C.identb[:, :])
        if i % 4 == 0:
            oT_t, oTk = oTs.next()
        P.op("act", "activation", r=[pTk], w=[oTk], out=oT_t[:, :, (i % 4) * 128:(i % 4 + 1) * 128], in_=pT[:, :, :], func=AF.Identity)
        if i % 4 == 3:
            blk = i // 4
            P.dma(S["oT"][512:1024, blk * 512:(blk + 1) * 512].rearrange("(m p) t -> p m t", p=128), oT_t[:, :, :], r=[oTk], w=["oT_b"])
    P.end()


def gla_scan(C, j):
    P, nc, I, S, O = C.P, C.nc, C.I, C.S, C.O
    P.begin()
    tri = P.sb("gs_tri", [64, 2, 64], F32)
    P.dma(tri[:, 0, :], I["c_tri"][0, 0:64, 0:64], w=["gs_tri"])
    P.dma(tri[:, 1, :], I["c_tri"][1, 0:64, 0:64], w=["gs_tri"])
    dec = P.sb("gs_dec", [64, 2, 8, 72], F32)
    for d in range(2):
        for h in range(8):
            P.dma(dec[:, d, h, :], S["decB"][d, h * 64:(h + 1) * 64, :], r=["decB"], w=["gs_dec"])
    Sst = [P.sb(f"gs_S{d}", [64, 8, 64], F32) for d in range(2)]
    Sbf = [P.sb(f"gs_Sbf{d}", [64, 8, 64], BF16) for d in range(2)]
    Stmp = [P.sb(f"gs_St{d}", [64, 8, 64], F32) for d in range(2)]
    qtb = [Rot(P, f"gs_qt{d}_", 2, [64, 8, 512], BF16) for d in range(2)]
    ktb = [Rot(P, f"gs_kt{d}_", 2, [64, 8, 512], BF16) for d in range(2)]
    keb = [Rot(P, f"gs_ke{d}_", 2, [64, 8, 512], BF16) for d in range(2)]
    vbl = [Rot(P, f"gs_v{d}_", 2, [64, 8, 512], BF16) for d in range(2)]
    ost = [Rot(P, f"gs_o{d}_", 1, [64, 8, 512], F32) for d in range(2)]
    attm = [Rot(P, f"gs_att{d}_", 2, [64, 8, 64], BF16) for d in range(2)]
    ps_att = [P.ps(f"gs_psatt{d}", [128, 512], F32) for d in range(2)]
    ps_o = [P.ps(f"gs_pso{d}", [128, 512], F32) for d in range(2)]
    ps_s = [P.ps(f"gs_pss{d}", [128, 512], F32) for d in range(2)]

    for sqi, (t0, tl) in enumerate(SEQS):
        nch = tl // 64
        nblk = tl // 512 if tl >= 512 else 1
        bl = min(tl, 512)
        cpb = bl // 64
        for d in range(2):
            if sqi == 0:
                for h in range(8):
                    P.dma(Sst[d][:, h, :], I["sb"][j, d, h, :, :], w=[f"gs_S{d}"])
            else:
                P.op("pool", "memset", w=[f"gs_S{d}"], ap=Sst[d][:, :, :], constant=0.0)
            P.op("act", "activation", r=[f"gs_S{d}"], w=[f"gs_Sbf{d}"], out=Sbf[d][:, :, :], in_=Sst[d][:, :, :], func=AF.Identity)
        cur = [None, None]

        def gchunk(step, d):
                c = step if d == 0 else nch - 1 - step
                blk = c // cpb
                cc = c % cpb
                first_in_blk = (cc == 0) if d == 0 else (cc == cpb - 1)
                last_in_blk = (cc == cpb - 1) if d == 0 else (cc == 0)
                tb = t0 + blk * bl
                if first_in_blk:
                    qt, qtk = qtb[d].next(); kt, ktk = ktb[d].next(); ke, kek = keb[d].next(); vv, vk = vbl[d].next(); oo, ook = ost[d].next()
                    for h in range(8):
                        P.dma(qt[:, h, 0:bl], S["qtB"][d, h * 64:(h + 1) * 64, tb:tb + bl], r=["qtB"], w=[qtk])
                        P.dma(kt[:, h, 0:bl], S["ktB"][d, h * 64:(h + 1) * 64, tb:tb + bl], r=["ktB"], w=[ktk])
                    P.dma(ke[:, 0:cpb, :], S["kendB"][d, tb:tb + bl, :].rearrange("(c p) f -> p c f", p=64), r=["kendB"], w=[kek])
                    P.dma(vv[:, 0:cpb, :], S["vB"][tb:tb + bl, :].rearrange("(c p) f -> p c f", p=64), r=["vB"], w=[vk])
                    cur[d] = (qt, qtk, kt, ktk, ke, kek, vv, vk, oo, ook)
                qt, qtk, kt, ktk, ke, kek, vv, vk, oo, ook = cur[d]
                cs = slice(cc * 64, (cc + 1) * 64)
                gch = t0 // 64 + c
                pa = ps_att[d]; pak = f"gs_psatt{d}"
                for h in range(8):
                    P.op("pe", "matmul", r=[ktk, qtk], w=[pak], out=pa[0:64, h * 64:(h + 1) * 64], lhsT=kt[:, h, cs], rhs=qt[:, h, cs], start=True, stop=True)
                am, amk = attm[d].next()
                P.op("dve", "tensor_tensor", r=[pak, "gs_tri"], w=[amk], out=am[:, :, :], in0=pa[0:64, :].rearrange("p (h c) -> p h c", c=64),
                     in1=_bc(tri[:, d:d + 1, :], [64, 8, 64]), op=ALU.mult)
                po = ps_o[d]; pok = f"gs_pso{d}"
                for h in range(8):
                    P.op("pe", "matmul", r=[amk, vk], w=[pok], out=po[0:64, h * 64:(h + 1) * 64], lhsT=am[:, h, :], rhs=vv[:, cc, h * 64:(h + 1) * 64], start=True, stop=False)
                    P.op("pe", "matmul", r=[qtk, f"gs_Sbf{d}"], w=[pok], out=po[0:64, h * 64:(h + 1) * 64], lhsT=qt[:, h, cs], rhs=Sbf[d][:, h, :], start=False, stop=True)
                P.op("act", "activation", r=[pok], w=[ook], out=oo[:, cc, :], in_=po[0:64, :], func=AF.Identity)
                pS = ps_s[d]; pSk = f"gs_pss{d}"
                for h in range(8):
                    P.op("pe", "matmul", r=[kek, vk], w=[pSk], out=pS[0:64, h * 64:(h + 1) * 64], lhsT=ke[:, cc, h * 64:(h + 1) * 64], rhs=vv[:, cc, h * 64:(h + 1) * 64],
                         start=True, stop=True)
                P.op("dve", "tensor_tensor", r=[f"gs_S{d}", "gs_dec"], w=[f"gs_St{d}"], out=Stmp[d][:, :, :], in0=Sst[d][:, :, :],
                     in1=_bc(dec[:, d, :, gch:gch + 1], [64, 8, 64]), op=ALU.mult)
                P.op("dve", "tensor_tensor", r=[f"gs_St{d}", pSk], w=[f"gs_S{d}"], out=Sst[d][:, :, :], in0=Stmp[d][:, :, :],
                     in1=pS[0:64, :].rearrange("p (h v) -> p h v", v=64), op=ALU.add)
                P.op("act", "activation", r=[f"gs_S{d}"], w=[f"gs_Sbf{d}"], out=Sbf[d][:, :, :], in_=Sst[d][:, :, :], func=AF.Identity)
                if last_in_blk:
                    P.dma(S["ofb"][d, tb:tb + bl, :].rearrange("(c p) f -> p c f", p=64), oo[:, 0:cpb, :], r=[ook], w=[f"ofb{d}"])

        for step in range(nch):
            lists = []
            for d in range(2):
                ops, _ = capture(P, lambda d=d: gchunk(step, d))
                lists.append(ops)
            P.ops.extend(zipper(lists))
        if sqi > 0:
            for d in range(2):
                for h in range(8):
                    P.dma(O["nsb"][sqi - 1, j, d, h, :, :], Sst[d][:, h, :], r=[f"gs_S{d}"], w=["o_nsb"], isout=True)
    P.end()


def proj_residual(C, l, which, wsrc, nk, act_src, act_key, xsrc, xkey):
    P, nc, I, S, O = C.P, C.nc, C.I, C.S, C.O
    P.begin()
    W = P.sb("pr_w", [128, nk, 1024], BF16)
    load_w_bf16(C, W, wsrc, 1024, "prw")
    gbc = P.sb("pr_g", [128, 3, 1024], F32)
    for s_ in range(3):
        P.dma(gbc[:, s_, :], S["gates"][l, s_:s_ + 1, which, :].partition_broadcast(128), r=[f"gates{l}"], w=["pr_g"])
    ablk = Rot(P, "pr_a", 2, [128, nk, 512], BF16)
    xt = Rot(P, "pr_x", 3, [128, 1024], F32)
    tmp = Rot(P, "pr_t", 2, [128, 1024], F32)
    ps = Rot(P, "pr_ps", 4, [128, 512], F32, psum=True)
    def load_a(blk):
        a, ak = ablk.next()
        for k0 in range(0, nk, 8):
            kn = min(8, nk - k0)
            P.dma(a[:, k0:k0 + kn, :], act_src[k0 * 128:(k0 + kn) * 128, blk * 512:(blk + 1) * 512].rearrange("(k p) t -> p k t", p=128), r=[act_key], w=[ak])
        return a, ak

    def load_x(i):
        x, xk = xt.next()
        P.dma(x[:, :], xsrc[i * 128:(i + 1) * 128, :], r=[xkey], w=[xk])
        return x, xk
    nxt_a = load_a(0)
    nxt_x = load_x(0)
    for blk in range(T // 512):
        a, ak = nxt_a
        for u in range(4):
            i = blk * 4 + u
            seq = 0 if i < 32 else (1 if i < 34 else 2)
            x, xk = nxt_x
            if u == 0 and blk + 1 < T // 512:
                nxt_a = load_a(blk + 1)
            if i + 1 < NT:
                nxt_x = load_x(i + 1)
            t_, tk = tmp.next()
            for half in range(2):
                p_, pk = ps.next()
                for k in range(nk):
                    P.op("pe", "matmul", r=[ak, f"prw_{k}"], w=[pk], out=p_[:, :], lhsT=a[:, k, u * 128:(u + 1) * 128], rhs=W[:, k, half * 512:(half + 1) * 512],
                         start=(k == 0), stop=(k == nk - 1))
                P.op("dve", "tensor_tensor", r=[pk, "pr_g"], w=[tk], out=t_[:, half * 512:(half + 1) * 512], in0=p_[:, :], in1=gbc[:, seq, half * 512:(half + 1) * 512], op=ALU.mult)
            P.op("dve", "tensor_tensor", r=[xk, tk], w=[xk], out=x[:, :], in0=x[:, :], in1=t_[:, :], op=ALU.add)
            P.dma(S["xres"][i * 128:(i + 1) * 128, :], x[:, :], r=[xk], w=["xres_w"])
    P.end()


def ffn_up(C, l):
    P, nc, I, S, O = C.P, C.nc, C.I, C.S, C.O
    P.begin()
    NC_ = T + 4
    rows = P.sb("fu_rows", [128, 128], F32)
    ps_t = P.ps("fu_pst", [128, 512], F32)
    wc = P.sb("fu_wc", [128, 3, 44], F32)
    load_fm(C, wc[:, 0:2, :].rearrange("p i c -> p (i c)"), I["ffn_conv"][l, 0:2, :].rearrange("i (c p) -> (i c) p", p=128), 88, ps_t, rows, "fuwc01")
    load_fm(C, wc[:, 2, :], I["ffn_conv"][l, 2, :].rearrange("(c p) -> c p", p=128), 44, ps_t, rows, "fuwc2")
    wkeys_c = ["fuwc01", "fuwc2"]
    U = [P.sb(f"fu_U{g}", [128, NC_], F32) for g in range(2)]
    Cv = [P.sb(f"fu_C{g}", [128, NC_], F32) for g in range(2)]
    for g in range(2):
        P.op("pool", "memset", w=[f"fu_U{g}"], ap=U[g][:, :], constant=0.0)
    wt = Rot(P, "fu_w", 4, [128, 8, 128], BF16)
    wst = Rot(P, "fu_wst", 4, [128, 8, 128], F32)
    act = Rot(P, "fu_act", 2, [128, NC_], BF16)
    ps = Rot(P, "fu_ps", 6, [128, 512], F32, psum=True)
    colof = lambda t: t + 1 if t < TS else (t + 2 if t < TS + TP else t + 3)
    ev = 0
    skip = os.environ.get("FU_SKIP", "").split(",")
    nm_ = int(os.environ.get("FU_M", DFF // 128))

    def load_pair(m):
        ws = []
        for g in range(2):
            w_, wk = wt.next()
            c0 = g * DFF + m * 128
            wf, wfk = wst.next()
            P.dma(wf[:, :, :], I["ffn_up"][l, :, c0:c0 + 128].rearrange("(k p) n -> p k n", p=128), w=[wfk])
            P.op("pool", "tensor_copy", r=[wfk], w=[wk], out=w_[:, :, :], in_=wf[:, :, :])
            ws.append((w_, wk))
        return ws
    nxt = load_pair(0)
    for m in range(nm_):
        ws = nxt
        if m + 1 < nm_:
            nxt = load_pair(m + 1)
        for g in range(2):
            w_, wk = ws[g]
            for tt in range(T // 512):
                p_, pk = ps.next()
                for k in range(8):
                    P.op("pe", "matmul", r=[wk] + [f"hT{4 * tt + u}" for u in range(4)], w=[pk], out=p_[:, :], lhsT=w_[:, k, :], rhs=C.hT[:, k, tt * 512:(tt + 1) * 512],
                         start=(k == 0), stop=(k == 7))
                pieces = [(0, 512)] if tt < 8 else [(0, 256), (256, 512)]
                for (a0, a1) in pieces:
                    c_ = colof(tt * 512 + a0)
                    P.op("act", "activation", r=[pk], w=[f"fu_U{g}"], out=U[g][:, c_:c_ + (a1 - a0)], in_=p_[:, a0:a1], func=AF.Identity)
            ci = g * 22 + m
            n = NC_ - 2
            if "conv" in skip:
                continue
            P.op("dve", "tensor_scalar", r=[f"fu_U{g}"] + wkeys_c, w=[f"fu_C{g}"], out=Cv[g][:, 1:1 + n], in0=U[g][:, 0:n], scalar1=wc[:, 0, ci:ci + 1], scalar2=None, op0=ALU.mult)
            P.op("dve", "scalar_tensor_tensor", r=[f"fu_U{g}", f"fu_C{g}"] + wkeys_c, w=[f"fu_C{g}"], out=Cv[g][:, 1:1 + n], in0=U[g][:, 1:1 + n], scalar=wc[:, 1, ci:ci + 1],
                 in1=Cv[g][:, 1:1 + n], op0=ALU.mult, op1=ALU.add)
            P.op("dve", "scalar_tensor_tensor", r=[f"fu_U{g}", f"fu_C{g}"] + wkeys_c, w=[f"fu_C{g}"], out=Cv[g][:, 1:1 + n], in0=U[g][:, 2:2 + n], scalar=wc[:, 2, ci:ci + 1],
                 in1=Cv[g][:, 1:1 + n], op0=ALU.mult, op1=ALU.add)
        if "silu" not in skip:
            P.op("act", "activation", r=["fu_C1"], w=["fu_C1"], out=Cv[1][:, 1:NC_ - 1], in_=Cv[1][:, 1:NC_ - 1], func=AF.Silu)
        if "mul" in skip:
            continue
        a_, ak = act.next()
        for (t0, tl) in SEQS:
            c_ = colof(t0)
            P.op("dve", "tensor_tensor", r=["fu_C0", "fu_C1"], w=[ak], out=a_[:, t0:t0 + tl], in0=Cv[0][:, c_:c_ + tl], in1=Cv[1][:, c_:c_ + tl], op=ALU.mult)
        P.dma(S["actT"][m * 128:(m + 1) * 128, :], a_[:, 0:T], r=[ak], w=["actT"])
    P.end()


def final_norm(C):
    P, nc, I, S, O = C.P, C.nc, C.I, C.S, C.O
    P.begin()
    gb = P.sb("fn_g", [128, 1024], F32)
    P.dma(gb[:, :], I["final_g"].rearrange("(o n) -> o n", o=1).partition_broadcast(128), w=["fn_g"])
    xt = Rot(P, "fn_x", 3, [128, 1024], F32)
    junk = P.sb("fn_junk", [128, 1024], BF16)
    ss = Rot(P, "fn_ss", 3, [128, 4], F32)
    yt = Rot(P, "fn_y", 3, [128, 1024], F32)
    for i in range(NT):
        x, xk = xt.next(); s_, sk = ss.next(); y, yk = yt.next()
        P.dma(x[:, :], S["xres"][i * 128:(i + 1) * 128, :], r=["xres_w"], w=[xk])
        P.op("act", "activation", r=[xk], w=["fn_junk", sk], out=junk[:, :], in_=x[:, :], func=AF.Square, accum_out=s_[:, 0:1])
        P.op("act", "activation", r=[sk, "epsb"], w=[sk], out=s_[:, 1:2], in_=s_[:, 0:1], func=AF.Ln, scale=1.0 / D, bias=C.epsb[:, 0:1])
        P.op("act", "activation", r=[sk], w=[sk], out=s_[:, 2:3], in_=s_[:, 1:2], func=AF.Exp, scale=-0.5)
        P.op("dve", "scalar_tensor_tensor", r=[xk, sk, "fn_g"], w=[yk], out=y[:, :], in0=x[:, :], scalar=s_[:, 2:3], in1=gb[:, :], op0=ALU.mult, op1=ALU.mult)
        P.dma(O["y"][i * 128:(i + 1) * 128, :], y[:, :], r=[yk], w=["o_y"], isout=True)
    P.end()


def l2_odd(C, j):
    P, nc, I, S, O = C.P, C.nc, C.I, C.S, C.O
    P.begin()
    W = P.sb("lo_w", [128, 8, OD_IN], BF16)
    load_w_bf16(C, W, I["od_w_in"][j], OD_IN, "low")
    wkeys = [f"low_{k}" for k in range(8)]
    rows = P.sb("lo_rows", [64, 128], F32)
    ps_t = P.ps("lo_pst", [128, 512], F32)
    wc = P.sb("lo_wc", [128, 3, 12], F32)
    load_fm(C, wc[:, :, :].rearrange("p i c -> p (i c)"), I["d_conv"][j].rearrange("i (c p) -> (i c) p", p=128), 36, ps_t, rows, "lowc")
    perm = P.sb("lo_perm", [128, 128], F32)
    P.dma(perm[:, :], I["c_perm"][:, :], w=["lo_perm"])
    bd = P.sb("lo_bd", [128, 128], F32)
    P.dma(bd[:, :], I["c_bd"][:, :], w=["lo_bd"])
    maskf = P.sb("lo_maskf", [8, 512], F32)
    P.dma(maskf[:, :], I["c_maskf"][0:8, :], w=["lo_maskf"])
    diag8 = P.sb("lo_diag8", [8, 8], F32)
    P.dma(diag8[:, :], I["c_ident"][0:8, 0:8], w=["lo_diag8"])
    par = P.sb("lo_par", [8, 4], F32)
    P.dma(par[:, 0:2], I["d_dt_bias"][j].rearrange("d h -> h d"), w=["lo_par"], allow_slow_non_contiguous=True)
    P.dma(par[:, 2:4], I["d_a_log"][j].rearrange("d h -> h d"), w=["lo_par"], allow_slow_non_contiguous=True)
    P.op("act", "activation", r=["lo_par"], w=["lo_par"], out=par[:, 2:4], in_=par[:, 2:4], func=AF.Exp)
    P.op("dve", "tensor_scalar", r=["lo_par"], w=["lo_par"], out=par[:, 2:4], in0=par[:, 2:4], scalar1=-1.0, scalar2=None, op0=ALU.mult)

    psg = Rot(P, "lo_psg", 4, [128, 512], F32, psum=True)
    psT = Rot(P, "lo_psT", 2, [128, 512], F32, psum=True)
    sbf = Rot(P, "lo_sbf", 3, [128, 512], BF16)
    sf32 = Rot(P, "lo_sf", 5, [128, 512], F32)
    vst = Rot(P, "lo_vst", 2, [128, 2, 65], BF16)
    for t_, k_ in zip(vst.tiles, vst.keys):
        P.op("pool", "memset", w=[k_], ap=t_[:, :, 64:65], constant=1.0)
    cs_t = Rot(P, "lo_cs", 2, [128, 2, 512], F32)
    ev = [0]

    def evac(dst, dkey, src, skey, extra_r=()):
        ev[0] += 1
        if ev[0] % 2:
            P.op("act", "activation", r=[skey] + list(extra_r), w=[dkey], out=dst, in_=src, func=AF.Identity)
        else:
            P.op("dve", "tensor_copy", r=[skey] + list(extra_r), w=[dkey], out=dst, in_=src)

    def proj_fm(ps, pkey, c0, M, tt):
        for k in range(8):
            P.op("pe", "matmul", r=[wkeys[k]] + [f"hT{4 * tt + u}" for u in range(4)], w=[pkey],
                 out=ps[0:M, :], lhsT=W[:, k, c0:c0 + M], rhs=C.hT[:, k, tt * 512:(tt + 1) * 512], start=(k == 0), stop=(k == 7))

    def proj_tm(ps, pkey, c0, N, i):
        for k in range(8):
            P.op("pe", "matmul", r=[wkeys[k], f"hT{i}"], w=[pkey],
                 out=ps[:, 0:N], lhsT=C.hT[:, k, i * 128:(i + 1) * 128], rhs=W[:, k, c0:c0 + N], start=(k == 0), stop=(k == 7))

    for tt in range(T // 512):
        tok = slice(tt * 512, (tt + 1) * 512)
        if tt < 8:
            cs, csk = cs_t.next()
            P.dma(cs[:, 0, :], I["c_cos"][:, tok], w=[csk])
            P.dma(cs[:, 1, :], I["c_sin"][:, tok], w=[csk])
        for m in range(5):
            ps, pk = psg.next()
            proj_fm(ps, pk, m * 128, 128, tt)
            st, sk = sbf.next()
            if tt == 8:
                evac(st[:, :], sk, ps[:, :], pk)
            else:
                xf, xfk = sf32.next()
                evac(xf[:, :], xfk, ps[:, :], pk)
                pr, prk = psg.next()
                P.op("pe", "matmul", r=[xfk, "lo_perm"], w=[prk], out=pr[:, :], lhsT=perm[:, :], rhs=xf[:, :], start=True, stop=True)
                t2, t2k = sf32.next()
                P.op("dve", "tensor_tensor", r=[prk, csk], w=[t2k], out=t2[:, :], in0=pr[:, :], in1=cs[:, 1, :], op=ALU.mult)
                P.op("dve", "tensor_tensor", r=[xfk, csk], w=[xfk], out=xf[:, :], in0=xf[:, :], in1=cs[:, 0, :], op=ALU.mult)
                P.op("pool", "tensor_tensor", r=[xfk, t2k], w=[sk], out=st[:, :], in0=xf[:, :], in1=t2[:, :], op=ALU.add)
            P.dma(S["qkC"][m * 128:(m + 1) * 128, tok], st[:, :], r=[sk], w=["qkC"])
        for u in range(4):
            i = 4 * tt + u
            rowsl = slice(i * 128, (i + 1) * 128)
            ps, pk = psg.next()
            proj_tm(ps, pk, 512, 256, i)
            st, sk = vst.next()
            if i < 32:
                evac(st[:, :, 0:64], sk, ps[:, 128:256].rearrange("p (g d) -> p g d", d=64), pk)
            else:
                seq = (i - 32) // 2; t0 = ((i - 32) % 2) * 128
                sf, sfk = sf32.next()
                evac(sf[:, 0:256], sfk, ps[:, 0:256], pk)
                P.op("pool", "tensor_copy", r=[sfk], w=[sk], out=st[:, :, 0:64], in_=sf[:, 128:256].rearrange("p (g d) -> p g d", d=64))
                for g in range(2):
                    P.dma(O["nck"][seq, j, g, t0:t0 + 128, :], sf[:, g * 64:(g + 1) * 64], r=[sfk], w=["o_nck"], isout=True)
                    P.dma(O["ncv"][seq, j, g, t0:t0 + 128, :], sf[:, 128 + g * 64:128 + (g + 1) * 64], r=[sfk], w=["o_ncv"], isout=True)
            P.dma(S["vC"][rowsl, :], st[:, :, :].rearrange("p g d -> p (g d)"), r=[sk], w=["vC"])
            ps, pk = psg.next()
            proj_tm(ps, pk, 2336, 512, i)
            sf, sfk = sf32.next()
            evac(sf[:, :], sfk, ps[:, :], pk)
            P.dma(S["gate"][rowsl, :], sf[:, :], r=[sfk], w=["gate"])
    P.sub_begin()
    sc8 = Rot(P, "lo_sc8", 8, [8, 512], F32)
    gex = Rot(P, "lo_gex", 2, [8, 8, 512], F32)
    sct = Rot(P, "lo_sct", 2, [128, 4, 40], F32)
    for tt in range(T // 512):
        tok = slice(tt * 512, (tt + 1) * 512)
        for d in range(2):
            pa, pak = psg.next()
            proj_fm(pa, pak, 2304 + 8 * d, 8, tt)
            pb, pbk = psg.next()
            proj_fm(pb, pbk, 2320 + 8 * d, 8, tt)
            t1, k1 = sc8.next(); Gt, Gk = sc8.next(); Lt, Lk = sc8.next(); Ht, Hk = sc8.next(); Bt, Bk = sc8.next(); BEt, BEk = sc8.next(); ELt, ELk = sc8.next()
            P.op("act", "activation", r=[pak, "lo_par"], w=[k1], out=t1[:, :], in_=pa[0:8, :], func=AF.Exp, bias=par[:, d:d + 1])
            P.op("act", "activation", r=[k1, "epsb"], w=[k1], out=t1[:, :], in_=t1[:, :], func=AF.Ln, bias=C.epsb[0:8, 1:2])
            if d == 0:
                P.op("dve", "tensor_tensor_scan", r=["lo_maskf", k1], w=[Gk], out=Gt[:, :], data0=maskf[:, :], data1=t1[:, :], initial=0.0, op0=ALU.mult, op1=ALU.add)
                lastv = Gt[:, 63::64]
            else:
                P.op("dve", "tensor_tensor_scan", r=["lo_maskf", k1], w=[Gk], out=Gt[:, ::-1], data0=maskf[:, :], data1=t1[:, ::-1], initial=0.0, op0=ALU.mult, op1=ALU.add)
                lastv = Gt[:, 0::64]
            P.op("dve", "tensor_scalar", r=[Gk, "lo_par"], w=[Gk], out=Gt[:, :], in0=Gt[:, :], scalar1=par[:, 2 + d:3 + d], scalar2=None, op0=ALU.mult)
            P.op("act", "activation", r=[pbk], w=[Lk], out=Lt[:, :], in_=pb[0:8, :], func=AF.Exp, scale=-1.0)
            P.op("act", "activation", r=[Lk, "epsb"], w=[Lk], out=Lt[:, :], in_=Lt[:, :], func=AF.Ln, bias=C.epsb[0:8, 1:2])
            P.op("dve", "tensor_tensor", r=[Gk, Lk], w=[Hk], out=Ht[:, :], in0=Gt[:, :], in1=Lt[:, :], op=ALU.subtract)
            P.op("act", "activation", r=[Lk], w=[Bk], out=Bt[:, :], in_=Lt[:, :], func=AF.Exp, scale=-1.0)
            P.op("act", "activation", r=[Hk], w=[BEk], out=BEt[:, :], in_=Ht[:, :], func=AF.Exp)
            P.op("dve", "tensor_tensor", r=[Gk], w=[ELk], out=ELt[:, :].rearrange("p (c s) -> p c s", s=64), in0=_bc(lastv.unsqueeze(2), [8, 8, 64]),
                 in1=Gt[:, :].rearrange("p (c s) -> p c s", s=64), op=ALU.subtract)
            P.op("act", "activation", r=[ELk], w=[ELk], out=ELt[:, :], in_=ELt[:, :], func=AF.Exp)
            for (src, srck, name) in ((Gt, Gk, "Gexp"), (Ht, Hk, "Hexp")):
                gx, gxk = gex.next()
                P.op("pool", "tensor_tensor", r=[srck, "lo_diag8"], w=[gxk], out=gx[:, :, :], in0=_bc(src[:, :].unsqueeze(1), [8, 8, 512]),
                     in1=_bc(diag8[:, :].unsqueeze(2), [8, 8, 512]), op=ALU.mult)
                P.dma(S[name][d, :, :, tok], gx[:, :, :], r=[gxk], w=[name])
            pT, pTk = psT.next()
            for u in range(4):
                for qi, (src, srck) in enumerate(((Gt, Gk), (Ht, Hk), (Bt, Bk), (BEt, BEk), (ELt, ELk))):
                    P.op("pe", "transpose", r=[srck, "ident"], w=[pTk], out=pT[:, u * 40 + qi * 8:u * 40 + qi * 8 + 8], in_=src[:, u * 128:(u + 1) * 128],
                         identity=C.ident[0:8, 0:8])
            stt_, sttk = sct.next()
            evac(stt_[:, :, :], sttk, pT[:, 0:160].rearrange("p (u f) -> p u f", f=40), pTk)
            P.dma(S["dsc"][d, tok, :].rearrange("(u p) f -> p u f", p=128), stt_[:, :, :], r=[sttk], w=["dsc"])
    P.sub_end()
    NC_ = T + 4
    U = P.sb("lo_U", [128, NC_], F32)
    Cv = P.sb("lo_C", [128, NC_], F32)
    P.op("pool", "memset", w=["lo_U"], ap=U[:, :], constant=0.0)
    colof = lambda t: t + 1 if t < TS else (t + 2 if t < TS + TP else t + 3)
    rs_t = Rot(P, "lo_rs", 2, [128, 512], F32)
    for m in range(12):
        kind = m // 4
        mm = m % 4
        for tt in range(T // 512):
            ps, pk = psg.next()
            proj_fm(ps, pk, 768 + m * 128, 128, tt)
            pieces = [(0, 512)] if tt < 8 else [(0, 256), (256, 512)]
            for (a0, a1) in pieces:
                c_ = colof(tt * 512 + a0)
                evac(U[:, c_:c_ + (a1 - a0)], "lo_U", ps[:, a0:a1], pk)
        n = NC_ - 2
        P.op("act", "activation", r=["lo_U", "lowc"], w=["lo_C"], out=Cv[:, 1:1 + n], in_=U[:, 0:n], func=AF.Identity, scale=wc[:, 0, m:m + 1])
        P.op("dve", "scalar_tensor_tensor", r=["lo_U", "lo_C", "lowc"], w=["lo_C"], out=Cv[:, 1:1 + n], in0=U[:, 1:1 + n], scalar=wc[:, 1, m:m + 1], in1=Cv[:, 1:1 + n],
             op0=ALU.mult, op1=ALU.add)
        P.op("dve", "scalar_tensor_tensor", r=["lo_U", "lo_C", "lowc"], w=["lo_C"], out=Cv[:, 1:1 + n], in0=U[:, 2:2 + n], scalar=wc[:, 2, m:m + 1], in1=Cv[:, 1:1 + n],
             op0=ALU.mult, op1=ALU.add)
        P.op("act", "activation", r=["lo_C"], w=["lo_C"], out=Cv[:, 1:1 + n], in_=Cv[:, 1:1 + n], func=AF.Silu)
        for tt in range(T // 512):
            pieces = [(0, 512)] if tt < 8 else [(0, 256), (256, 512)]
            tok = slice(tt * 512, (tt + 1) * 512)
            xn, xnk = sf32.next()
            for (a0, a1) in pieces:
                c_ = colof(tt * 512 + a0); w_ = a1 - a0
                if kind == 2:
                    P.op("act", "activation", r=["lo_C"], w=[xnk], out=xn[:, a0:a1], in_=Cv[:, c_:c_ + w_], func=AF.Identity)
                else:
                    sq, sqk = sf32.next()
                    P.op("act", "activation", r=["lo_C"], w=[sqk], out=sq[:, 0:w_], in_=Cv[:, c_:c_ + w_], func=AF.Square)
                    pn, pnk = psg.next()
                    P.op("pe", "matmul", r=[sqk, "lo_bd"], w=[pnk], out=pn[:, 0:w_], lhsT=bd[:, :], rhs=sq[:, 0:w_], start=True, stop=True)
                    rs, rsk = rs_t.next()
                    P.op("act", "activation", r=[pnk, "epsb"], w=[rsk], out=rs[:, 0:w_], in_=pn[:, 0:w_], func=AF.Ln, bias=C.epsb[:, 0:1])
                    P.op("act", "activation", r=[rsk], w=[rsk], out=rs[:, 0:w_], in_=rs[:, 0:w_], func=AF.Exp, scale=-0.5)
                    P.op("dve", "scalar_tensor_tensor", r=["lo_C", rsk], w=[xnk], out=xn[:, a0:a1], in0=Cv[:, c_:c_ + w_], scalar=(0.125 if kind == 0 else 1.0), in1=rs[:, 0:w_],
                         op0=ALU.mult, op1=ALU.mult)
            if kind < 2:
                st, sk = sbf.next()
                P.op("dve", "tensor_copy", r=[xnk], w=[sk], out=st[:, :], in_=xn[:, :])
                P.dma(S["qnT" if kind == 0 else "knT"][mm * 128:(mm + 1) * 128, tok], st[:, :], r=[sk], w=["qknT"])
            if kind >= 1:
                pT, pTk = psT.next()
                for u in range(4):
                    P.op("pe", "transpose", r=[xnk, "ident"], w=[pTk], out=pT[:, u * 128:(u + 1) * 128], in_=xn[:, u * 128:(u + 1) * 128], identity=C.ident[:, :])
                tk_, tkk = sf32.next()
                evac(tk_[:, :], tkk, pT[:, :], pTk)
                dstn = "kn_tok" if kind == 1 else "v_tok"
                P.dma(S[dstn][tok, mm * 128:(mm + 1) * 128].rearrange("(u p) f -> p u f", p=128), tk_[:, :].rearrange("p (u f) -> p u f", f=128), r=[tkk], w=[dstn])
    P.end()


def attn_finish(C, R, pso_pair, pkeys, nq, oT_t, oTk, col0, esink=None):
    P = C.P
    rd, rdk = R.rden.next()
    ot, otk = R.o.next()
    for b in range(2):
        v = pso_pair[b][0:nq, 0:260].rearrange("p (h e) -> p h e", e=65)
        if esink is not None:
            P.op("dve", "tensor_tensor", r=[pkeys[b], "esink"], w=[rdk], out=rd[0:nq, 4 * b:4 * b + 4], in0=v[:, :, 64], in1=esink[0:nq, 4 * b:4 * b + 4], op=ALU.add)
            P.op("dve", "reciprocal", r=[rdk], w=[rdk], out=rd[0:nq, 4 * b:4 * b + 4], in_=rd[0:nq, 4 * b:4 * b + 4])
        else:
            P.op("dve", "reciprocal", r=[pkeys[b]], w=[rdk], out=rd[0:nq, 4 * b:4 * b + 4], in_=v[:, :, 64])
        P.op("dve", "tensor_tensor", r=[pkeys[b], rdk], w=[otk], out=ot[0:nq, 256 * b:256 * (b + 1)].rearrange("p (h d) -> p h d", d=64),
             in0=v[:, :, 0:64], in1=_bc(rd[0:nq, 4 * b:4 * b + 4].unsqueeze(2), [nq, 4, 64]), op=ALU.mult)
    pT, pTk = R.ps_T.next()
    for m in range(4):
        P.op("pe", "transpose", r=[otk, "identb"], w=[pTk], out=pT[:, m, 0:nq], in_=ot[0:nq, m * 128:(m + 1) * 128], identity=C.identb[0:nq, 0:nq])
    P.op("act", "activation", r=[pTk], w=[oTk], out=oT_t[:, :, col0:col0 + nq], in_=pT[:, :, 0:nq], func=AF.Identity)


def attn_odd(C, j):
    P, nc, I, S, O = C.P, C.nc, C.I, C.S, C.O
    P.begin()
    R = attn_setup(C, 128)
    kT = P.sb("ao_kT", [64, 2, T], BF16)
    V = P.sb("ao_V", [128, NT, 2 * 65], BF16)
    for g in range(2):
        P.dma(kT[:, g, :], S["qkC"][512 + g * 64:512 + (g + 1) * 64, :], r=["qkC"], w=["ao_kT"])
    for n0 in range(0, NT, 4):
        P.dma(V[:, n0:n0 + 4, :], S["vC"][n0 * 128:(n0 + 4) * 128, :].rearrange("(n p) f -> p n f", p=128), r=["vC"], w=["ao_V"])
    ckT = P.sb("ao_ckT", [64, 2, 256], BF16)
    cV = P.sb("ao_cV", [128, 2, 2 * 65], BF16)
    ctmp = P.sb("ao_ctmp", [128, 2, 2, 64], F32)
    ctmp2 = P.sb("ao_ctmp2", [128, 2, 2, 64], F32)
    ps_m = P.ps("ao_psm", [128, 512], F32)
    for half in range(2):
        for g in range(2):
            P.dma(ctmp[:, half, g, :], I["cck"][j, g, half * 128:(half + 1) * 128, :], w=["ao_ctmp"])
            P.dma(ctmp2[:, half, g, :], I["ccv"][j, g, half * 128:(half + 1) * 128, :], w=["ao_ctmp2"])
    for half in range(2):
        for g in range(2):
            P.op("pe", "transpose", r=["ao_ctmp", "ident"], w=["ao_psm"], out=ps_m[0:64, (half * 2 + g) * 128:(half * 2 + g + 1) * 128],
                 in_=ctmp[:, half, g, :], identity=C.ident[:, :])
    for half in range(2):
        P.op("dve", "tensor_copy", r=["ao_psm"], w=["ao_ckT"], out=ckT[:, :, half * 128:(half + 1) * 128],
             in_=ps_m[0:64, half * 256:(half + 1) * 256].rearrange("p (g t) -> p g t", t=128))
    P.op("pool", "memset", w=["ao_cV"], ap=cV[:, :, :], constant=1.0)
    P.op("dve", "tensor_copy", r=["ao_ctmp2"], w=["ao_cV"], out=cV[:, :, :].rearrange("p a (g e) -> p a g e", e=65)[:, :, :, 0:64], in_=ctmp2[:, :, :, :])
    esink = P.sb("ao_esink", [128, 8], F32)
    P.dma(esink[:, :], I["c_sink"][j:j + 1, :].partition_broadcast(128), w=["esink"])
    P.op("act", "activation", r=["esink"], w=["esink"], out=esink[:, :], in_=esink[:, :], func=AF.Exp)
    tri = P.sb("ao_tri", [128, 2, 128], F32)
    P.dma(tri[:, 0, :], I["c_tri"][0], w=["ao_tri"])
    P.dma(tri[:, 1, :], I["c_tri"][1], w=["ao_tri"])

    qblk = Rot(P, "ao_q", 2, [64, 8, 512], BF16)
    Pt = Rot(P, "ao_Pt", 10, [128, 4, 128], BF16)
    pso_i = [0]

    def unit(q, qk, qcol, chunks, oT_t, oTk, ocol):
        pso_pair = (R.pso[2 * (pso_i[0] % 2)], R.pso[2 * (pso_i[0] % 2) + 1])
        pkeys = (f"at_pso{2 * (pso_i[0] % 2)}", f"at_pso{2 * (pso_i[0] % 2) + 1}")
        pso_i[0] += 1
        allpts = []
        for g in range(2):
            pts = []
            for (kfn, kkey, vfn, vkey, mask) in chunks:
                ps, psk = R.ps_s.next()
                P.op("pe", "matmul", r=[kkey, qk], w=[psk], out=ps[:, :], lhsT=kfn(g), rhs=q[:, 4 * g:4 * g + 4, qcol:qcol + 128], start=True, stop=True)
                pt, ptk = Pt.next()
                if mask is None:
                    P.op("act", "activation", r=[psk], w=[ptk], out=pt[:, :, :], in_=ps[:, :].rearrange("p (h q) -> p h q", q=128), func=AF.Exp, scale=0.125)
                else:
                    E, Ek = R.E.next()
                    P.op("act", "activation", r=[psk], w=[Ek], out=E[:, :], in_=ps[:, :], func=AF.Exp, scale=0.125)
                    P.op("dve", "tensor_tensor", r=[Ek, "ao_tri"], w=[ptk], out=pt[:, :, :], in0=E[:, :].rearrange("p (h q) -> p h q", q=128),
                         in1=_bc(tri[:, mask:mask + 1, :], [128, 4, 128]), op=ALU.mult)
                pts.append((pt, ptk, vfn, vkey))
            allpts.append(pts)
        for g in range(2):
            pts = allpts[g]
            po = pso_pair[g]; pok = pkeys[g]
            for hh in range(4):
                for x, (pt, ptk, vfn, vkey) in enumerate(pts):
                    P.op("pe", "matmul", r=[ptk, vkey], w=[pok], out=po[:, hh * 65:(hh + 1) * 65], lhsT=pt[:, hh, :], rhs=vfn(g),
                         start=(x == 0), stop=(x == len(pts) - 1))
        attn_finish(C, R, pso_pair, pkeys, 128, oT_t, oTk, ocol, esink=esink)

    def kchunk(tok0):
        return (lambda g, tok0=tok0: kT[:, g, tok0:tok0 + 128])

    def vchunk(n):
        return (lambda g, n=n: V[:, n, g * 65:(g + 1) * 65])

    ctx_chunks = [((lambda g, x=x: ckT[:, g, x * 128:(x + 1) * 128]), "ao_ckT", (lambda g, x=x: cV[:, x, g * 65:(g + 1) * 65]), "ao_cV", None) for x in range(2)]
    for blk in range(8):
        q, qk = qblk.next()
        for h in range(8):
            P.dma(q[:, h, :], S["qkC"][h * 64:(h + 1) * 64, blk * 512:(blk + 1) * 512], r=["qkC"], w=[qk])
        oT_t, oTk = R.oT.next()
        for u in range(4):
            n = blk * 4 + u
            chunks = []
            if n > 0:
                chunks.append((kchunk((n - 1) * 128), "ao_kT", vchunk(n - 1), "ao_V", 1))
            chunks.append((kchunk(n * 128), "ao_kT", vchunk(n), "ao_V", None))
            if n < 31:
                chunks.append((kchunk((n + 1) * 128), "ao_kT", vchunk(n + 1), "ao_V", 0))
            chunks += ctx_chunks
            unit(q, qk, u * 128, chunks, oT_t, oTk, u * 128)
        P.dma(S["oT"][0:512, blk * 512:(blk + 1) * 512].rearrange("(m p) t -> p m t", p=128), oT_t[:, :, :], r=[oTk], w=["oT_a"])
    q, qk = qblk.next()
    for h in range(8):
        P.dma(q[:, h, :], S["qkC"][h * 64:(h + 1) * 64, TS:TS + 512], r=["qkC"], w=[qk])
    oT_t, oTk = R.oT.next()
    for sq in range(2):
        for qb in range(2):
            col = sq * 256 + qb * 128
            chunks = [(kchunk(TS + sq * 256 + x * 128), "ao_kT", vchunk((TS + sq * 256 + x * 128) // 128), "ao_V", None) for x in range(2)]
            unit(q, qk, col, chunks, oT_t, oTk, col)
    P.dma(S["oT"][0:512, TS:TS + 512].rearrange("(m p) t -> p m t", p=128), oT_t[:, :, :], r=[oTk], w=["oT_a"])
    P.end()


def zipper(lists):
    lists = [l for l in lists if l]
    pos = [0] * len(lists)
    out = []
    total = sum(len(l) for l in lists)
    while len(out) < total:
        best = None
        for i, l in enumerate(lists):
            if pos[i] < len(l):
                f = pos[i] / len(l)
                if best is None or f < best[0]:
                    best = (f, i)
        i = best[1]
        out.append(lists[i][pos[i]])
        pos[i] += 1
    return out


def capture(P, fn):
    saved = P.ops
    P.ops = []
    ret = fn()
    out = P.ops
    P.ops = saved
    return out, ret


F32R = mybir.dt.float32r


def delta_scan(C, j):
    P, nc, I, S, O = C.P, C.nc, C.I, C.S, C.O
    P.begin()
    BL = 128
    CPB = BL // 64
    dm = P.sb("ds_dm", [64, 4, 64], F32)
    for q_ in range(4):
        P.dma(dm[:, q_, :], I["c_dmask"][q_], w=["ds_dm"])
    ones8 = P.sb("ds_ones8", [8, 64], F32)
    P.op("pool", "memset", w=["ds_ones8"], ap=ones8[:, :], constant=1.0)
    id64 = P.sb("ds_id64", [64, 64], F32)
    P.dma(id64[:, :], I["c_ident"][0:64, 0:64], w=["ds_id64"])
    PSP = [Rot(P, f"ds_psp{d}_", 3, [128, 512], F32, psum=True) for d in range(2)]
    PSR = [Rot(P, f"ds_psr{d}_", 1, [128, 512], F32, psum=True) for d in range(2)]
    DP = F32

    def mk(name, n, shape, dt):
        return [Rot(P, f"ds_{name}{d}_", n, shape, dt) for d in range(2)]
    knTb = mk("knT", 1, [64, 8, BL], BF16); qnTb = mk("qnT", 1, [64, 8, BL], BF16)
    kntok = mk("kntok", 1, [64, CPB, 512], F32); vtok = mk("vtok", 1, [64, CPB, 512], F32)
    dscb = mk("dsc", 1, [64, CPB, 40], F32)
    gexb = mk("gex", 1, [8, 8, BL], F32); hexb = mk("hex", 1, [8, 8, BL], F32)
    ost = mk("o", 2, [64, CPB, 512], F32)
    D1 = mk("D1", 1, [64, 8, 64], F32); D2 = mk("D2", 1, [64, 8, 64], F32); D3 = mk("D3", 1, [64, 8, 64], F32)
    aqk = mk("aqk", 2, [64, 8, 64], BF16)
    Npw = mk("N", 2, [64, 8, 64], DP); Mpw = mk("M", 2, [64, 8, 64], DP)
    QR = F32R if os.environ.get("MK_QR", "0") == "1" else F32
    Qf = mk("Qf", 1, [64, 8, 64], F32); Qb = mk("Qb", 2, [64, 8, 64], QR)
    Mr = mk("Mr", 2, [64, 8, 64], QR)
    vbt = mk("vb", 1, [64, 8, 64], QR); kbg = mk("kbg", 1, [64, 8, 64], QR); kend = mk("kend", 2, [64, 8, 64], BF16)
    egb = mk("eg", 1, [64, 8, 64], F32); qg = mk("qg", 2, [64, 8, 64], BF16)
    wval = mk("wval", 2, [64, 8, 64], F32); kcT = mk("kcT", 2, [64, 8, 64], F32); vnew = mk("vnew", 1, [64, 8, 64], BF16)
    dl = mk("dl", 2, [64, 8], F32)
    Sst = [P.sb(f"ds_S{d}", [64, 8, 64], F32) for d in range(2)]
    Sbf = [P.sb(f"ds_Sbf{d}", [64, 8, 64], BF16) for d in range(2)]
    Sr = [P.sb(f"ds_Sr{d}", [64, 8, 64], F32) for d in range(2)]
    Stmp = [P.sb(f"ds_St{d}", [64, 8, 64], F32) for d in range(2)]
    evc = [0]

    def evac(dst, dkey, src, skey):
        evc[0] += 1
        if evc[0] % 2:
            P.op("act", "activation", r=[skey], w=[dkey], out=dst, in_=src, func=AF.Identity)
        else:
            P.op("dve", "tensor_copy", r=[skey], w=[dkey], out=dst, in_=src)

    def v3(ps):
        return ps[0:64, :].rearrange("p (h c) -> p h c", c=64)

    def mm8(ps, psk, lhs_fn, lkeys, rhs_fn, rkeys):
        for h in range(8):
            P.op("pe", "matmul", r=list(lkeys) + list(rkeys), w=[psk], out=ps[0:64, h * 64:(h + 1) * 64], lhsT=lhs_fn(h), rhs=rhs_fn(h), start=True, stop=True)

    cur = [None, None]
    curo = [None, None]

    def prep(d, t0, nch, c):
        blk = c // CPB; cc = c % CPB
        first_in_blk = (cc == 0) if d == 0 else (cc == CPB - 1)
        tb = t0 + blk * BL
        if first_in_blk:
            kn, knk = knTb[d].next(); qn, qnk = qnTb[d].next(); kt, ktk = kntok[d].next(); vt, vtk = vtok[d].next()
            sc, sck = dscb[d].next(); gx, gxk = gexb[d].next(); hx, hxk = hexb[d].next()
            for h in range(8):
                P.dma(kn[:, h, :], S["knT"][h * 64:(h + 1) * 64, tb:tb + BL], r=["knT"], w=[knk])
                P.dma(qn[:, h, :], S["qnT"][h * 64:(h + 1) * 64, tb:tb + BL], r=["qnT"], w=[qnk])
            P.dma(kt[:, :, :], S["kn_tok"][tb:tb + BL, :].rearrange("(c p) f -> p c f", p=64), r=["kn_tok"], w=[ktk])
            P.dma(vt[:, :, :], S["v_tok"][tb:tb + BL, :].rearrange("(c p) f -> p c f", p=64), r=["v_tok"], w=[vtk])
            P.dma(sc[:, :, :], S["dsc"][d, tb:tb + BL, :].rearrange("(c p) f -> p c f", p=64), r=["dsc"], w=[sck])
            for k8 in range(8):
                P.dma(gx[:, k8, :], S["Gexp"][d, :, k8, tb:tb + BL], r=["Gexp"], w=[gxk])
                P.dma(hx[:, k8, :], S["Hexp"][d, :, k8, tb:tb + BL], r=["Hexp"], w=[hxk])
            cur[d] = (kn, knk, qn, qnk, kt, ktk, vt, vtk, sc, sck, gx, gxk, hx, hxk)
        kn, knk, qn, qnk, kt, ktk, vt, vtk, sc, sck, gx, gxk, hx, hxk = cur[d]
        cs = slice(cc * 64, (cc + 1) * 64)
        Gs = sc[:, cc, 0:8]; Hs = sc[:, cc, 8:16]; Bs = sc[:, cc, 16:24]; BEs = sc[:, cc, 24:32]; ELs = sc[:, cc, 32:40]
        m_incl, m_strict, m_strictT = (0, 1, 3) if d == 0 else (2, 3, 1)
        lastc = 63 if d == 0 else 0
        pG, pGk = PSP[d].next()
        P.op("pe", "matmul", r=["ds_ones8", gxk], w=[pGk], out=pG[0:64, :], lhsT=ones8[:, :], rhs=gx[:, :, cs], start=True, stop=True)
        pH, pHk = PSP[d].next()
        P.op("pe", "matmul", r=["ds_ones8", hxk], w=[pHk], out=pH[0:64, :], lhsT=ones8[:, :], rhs=hx[:, :, cs], start=True, stop=True)
        pKK, pKKk = PSP[d].next()
        mm8(pKK, pKKk, lambda h: kn[:, h, cs], [knk], lambda h: kn[:, h, cs], [])
        d1, d1k = D1[d].next(); d2, d2k = D2[d].next(); d3, d3k = D3[d].next()
        Gs_bc = _bc(Gs.unsqueeze(2), [64, 8, 64]); Hs_bc = _bc(Hs.unsqueeze(2), [64, 8, 64])
        eg, egk = egb[d].next(); qgt, qgk = qg[d].next(); dlt, dlk = dl[d].next()
        P.op("dve", "tensor_tensor", r=[pGk, sck], w=[d1k], out=d1[:, :, :], in0=v3(pG), in1=Gs_bc, op=ALU.subtract)
        P.op("dve", "scalar_tensor_tensor", r=[pGk, sck], w=[d3k], out=d3[:, :, :], in0=v3(pG), scalar=-1.0, in1=Hs_bc, op0=ALU.mult, op1=ALU.add)
        P.op("act", "activation", r=[pGk], w=[egk], out=eg[:, :, :], in_=v3(pG), func=AF.Exp)
        P.op("dve", "tensor_tensor", r=[pHk, sck], w=[d2k], out=d2[:, :, :], in0=v3(pH), in1=Gs_bc, op=ALU.subtract)
        pQK, pQKk = PSP[d].next()
        mm8(pQK, pQKk, lambda h: kn[:, h, cs], [knk], lambda h: qn[:, h, cs], [qnk])
        P.op("dve", "tensor_tensor", r=[qnk, egk], w=[qgk], out=qgt[:, :, :], in0=qn[:, :, cs], in1=eg[:, :, :], op=ALU.mult)
        P.op("act", "activation", r=[egk], w=[dlk], out=dlt[:, :], in_=eg[:, :, lastc], func=AF.Identity)
        P.op("dve", "tensor_tensor", r=[d1k, "ds_dm"], w=[d1k], out=d1[:, :, :], in0=d1[:, :, :], in1=_bc(dm[:, m_incl:m_incl + 1, :], [64, 8, 64]), op=ALU.add)
        P.op("act", "activation", r=[d1k], w=[d1k], out=d1[:, :, :], in_=d1[:, :, :], func=AF.Exp)
        aq, aqk_ = aqk[d].next()
        P.op("dve", "tensor_tensor", r=[pQKk, d1k], w=[aqk_], out=aq[:, :, :], in0=v3(pQK), in1=d1[:, :, :], op=ALU.mult)
        P.op("dve", "tensor_tensor", r=[d2k, "ds_dm"], w=[d2k], out=d2[:, :, :], in0=d2[:, :, :], in1=_bc(dm[:, m_strict:m_strict + 1, :], [64, 8, 64]), op=ALU.add)
        P.op("act", "activation", r=[d2k], w=[d2k], out=d2[:, :, :], in_=d2[:, :, :], func=AF.Exp)
        N1, N1k = Npw[d].next()
        P.op("dve", "scalar_tensor_tensor", r=[pKKk, d2k], w=[N1k], out=N1[:, :, :], in0=v3(pKK), scalar=-1.0, in1=d2[:, :, :], op0=ALU.mult, op1=ALU.mult)
        P.op("dve", "tensor_tensor", r=[d3k, "ds_dm"], w=[d3k], out=d3[:, :, :], in0=d3[:, :, :], in1=_bc(dm[:, m_strictT:m_strictT + 1, :], [64, 8, 64]), op=ALU.add)
        P.op("act", "activation", r=[d3k], w=[d3k], out=d3[:, :, :], in_=d3[:, :, :], func=AF.Exp)
        M1, M1k = Mpw[d].next()
        P.op("dve", "scalar_tensor_tensor", r=[pKKk, d3k], w=[M1k], out=M1[:, :, :], in0=v3(pKK), scalar=-1.0, in1=d3[:, :, :], op0=ALU.mult, op1=ALU.mult)
        vb_, vbk = vbt[d].next(); kb_, kbk = kbg[d].next(); ke_, kek = kend[d].next()
        P.op("dve", "tensor_tensor", r=[vtk, sck], w=[vbk], out=vb_[:, :, :], in0=vt[:, cc, :].rearrange("p (h v) -> p h v", v=64), in1=_bc(Bs.unsqueeze(2), [64, 8, 64]), op=ALU.mult)
        P.op("dve", "tensor_tensor", r=[ktk, sck], w=[kbk], out=kb_[:, :, :], in0=kt[:, cc, :].rearrange("p (h v) -> p h v", v=64), in1=_bc(BEs.unsqueeze(2), [64, 8, 64]), op=ALU.mult)
        P.op("dve", "tensor_tensor", r=[ktk, sck], w=[kek], out=ke_[:, :, :], in0=kt[:, cc, :].rearrange("p (h v) -> p h v", v=64), in1=_bc(ELs.unsqueeze(2), [64, 8, 64]), op=ALU.mult)
        qf, qfk = Qf[d].next(); qb, qbk = Qb[d].next()
        P.op("dve", "tensor_tensor", r=[N1k, "ds_id64"], w=[qfk], out=qf[:, :, :], in0=N1[:, :, :], in1=_bc(id64[:, :].unsqueeze(1), [64, 8, 64]), op=ALU.add)
        P.op("act", "activation", r=[qfk], w=[qbk], out=qb[:, :, :], in_=qf[:, :, :], func=AF.Identity)
        Nc, Nck, Mc, Mck = N1, N1k, M1, M1k
        for lev in range(5):
            Mn, Mnk = Mpw[d].next()
            pM, pMk = PSP[d].next()
            mm8(pM, pMk, lambda h: Nc[:, h, :], [Nck], lambda h: Mc[:, h, :], [Mck])
            if lev < 4:
                Nn, Nnk = Npw[d].next()
                pN, pNk = PSP[d].next()
                mm8(pN, pNk, lambda h: Mc[:, h, :], [Mck], lambda h: Nc[:, h, :], [Nck])
                evac(Nn[:, :, :], Nnk, v3(pN), pNk)
            evac(Mn[:, :, :], Mnk, v3(pM), pMk)
            if QR == F32R:
                mr, mrk = Mr[d].next()
                evac(mr[:, :, :], mrk, v3(pM), pMk)
            else:
                mr, mrk = Mn, Mnk
            pQ, pQk = PSP[d].next()
            mm8(pQ, pQk, lambda h: mr[:, h, :], [mrk], lambda h: qb[:, h, :], [qbk])
            P.op("dve", "tensor_tensor", r=[pQk, qfk], w=[qfk], out=qf[:, :, :], in0=qf[:, :, :], in1=v3(pQ), op=ALU.add)
            qb, qbk = Qb[d].next()
            P.op("act", "activation", r=[qfk], w=[qbk], out=qb[:, :, :], in_=qf[:, :, :], func=AF.Identity)
            Mc, Mck = Mn, Mnk
            if lev < 4:
                Nc, Nck = Nn, Nnk
        pW, pWk = PSP[d].next()
        mm8(pW, pWk, lambda h: qb[:, h, :], [qbk], lambda h: vb_[:, h, :], [vbk])
        wv, wvk = wval[d].next()
        evac(wv[:, :, :], wvk, v3(pW), pWk)
        pK, pKk = PSP[d].next()
        mm8(pK, pKk, lambda h: kb_[:, h, :], [kbk], lambda h: qb[:, h, :], [qbk])
        kc, kck = kcT[d].next()
        evac(kc[:, :, :], kck, v3(pK), pKk)
        return dict(wv=wv, wvk=wvk, kc=kc, kck=kck, qgt=qgt, qgk=qgk, aq=aq, aqk=aqk_, ke=ke_, kek=kek, dlt=dlt, dlk=dlk)

    def rec(d, t0, nch, c, H):
        blk = c // CPB; cc = c % CPB
        first_in_blk = (cc == 0) if d == 0 else (cc == CPB - 1)
        last_in_blk = (cc == CPB - 1) if d == 0 else (cc == 0)
        tb = t0 + blk * BL
        if first_in_blk:
            curo[d] = ost[d].next()
        oo, ook = curo[d]
        Sk = f"ds_S{d}"; Sbk = f"ds_Sbf{d}"; Srk = f"ds_Sr{d}"
        pV, pVk = PSR[d].next()
        mm8(pV, pVk, lambda h: H["kc"][:, h, :], [H["kck"]], lambda h: Sr[d][:, h, :], [Srk])
        vn, vnk = vnew[d].next()
        P.op("dve", "tensor_tensor", r=[H["wvk"], pVk], w=[vnk], out=vn[:, :, :], in0=H["wv"][:, :, :], in1=v3(pV), op=ALU.subtract)
        pO, pOk = PSR[d].next()
        for h in range(8):
            P.op("pe", "matmul", r=[H["qgk"], Sbk], w=[pOk], out=pO[0:64, h * 64:(h + 1) * 64], lhsT=H["qgt"][:, h, :], rhs=Sbf[d][:, h, :], start=True, stop=False)
            P.op("pe", "matmul", r=[H["aqk"], vnk], w=[pOk], out=pO[0:64, h * 64:(h + 1) * 64], lhsT=H["aq"][:, h, :], rhs=vn[:, h, :], start=False, stop=True)
        P.op("act", "activation", r=[pOk], w=[ook], out=oo[:, cc, :], in_=pO[0:64, :], func=AF.Identity)
        pS, pSk = PSR[d].next()
        mm8(pS, pSk, lambda h: H["ke"][:, h, :], [H["kek"]], lambda h: vn[:, h, :], [vnk])
        P.op("dve", "tensor_tensor", r=[Sk, H["dlk"]], w=[f"ds_St{d}"], out=Stmp[d][:, :, :], in0=Sst[d][:, :, :], in1=_bc(H["dlt"][:, :].unsqueeze(2), [64, 8, 64]), op=ALU.mult)
        P.op("dve", "tensor_tensor", r=[f"ds_St{d}", pSk], w=[Sk], out=Sst[d][:, :, :], in0=Stmp[d][:, :, :], in1=v3(pS), op=ALU.add)
        P.op("act", "activation", r=[Sk], w=[Sbk], out=Sbf[d][:, :, :], in_=Sst[d][:, :, :], func=AF.Identity)
        P.op("act", "activation", r=[Sk], w=[Srk], out=Sr[d][:, :, :], in_=Sst[d][:, :, :], func=AF.Identity)
        if last_in_blk:
            P.dma(S["ofb"][d, tb:tb + BL, :].rearrange("(c p) f -> p c f", p=64), oo[:, :, :], r=[ook], w=[f"ofb{d}"])

    for sqi, (t0, tl) in enumerate(SEQS):
        nch = tl // 64
        for d in range(2):
            if sqi == 0:
                for h in range(8):
                    P.dma(Sst[d][:, h, :], I["sd"][j, d, h, :, :], w=[f"ds_S{d}"])
            else:
                P.op("pool", "memset", w=[f"ds_S{d}"], ap=Sst[d][:, :, :], constant=0.0)
            P.op("act", "activation", r=[f"ds_S{d}"], w=[f"ds_Sbf{d}"], out=Sbf[d][:, :, :], in_=Sst[d][:, :, :], func=AF.Identity)
            P.op("pool", "tensor_copy", r=[f"ds_S{d}"], w=[f"ds_Sr{d}"], out=Sr[d][:, :, :], in_=Sst[d][:, :, :])
        Hprev = [None, None]
        chunk_of = lambda step, d: step if d == 0 else nch - 1 - step
        for step in range(nch + 1):
            lists = []
            Hnew = [None, None]
            for d in range(2):
                if step < nch:
                    ops, Hnew[d] = capture(P, lambda d=d: prep(d, t0, nch, chunk_of(step, d)))
                    lists.append(ops)
                if step > 0:
                    ops, _ = capture(P, lambda d=d: rec(d, t0, nch, chunk_of(step - 1, d), Hprev[d]))
                    lists.append(ops)
            P.ops.extend(zipper(lists))
            Hprev = Hnew
        if sqi > 0:
            for d in range(2):
                for h in range(8):
                    P.dma(O["nsd"][sqi - 1, j, d, h, :, :], Sst[d][:, h, :], r=[f"ds_S{d}"], w=["o_nsd"], isout=True)
    P.end()
```
